# Optimizing a Trainium2 kernel written in Bass

```python
import math
import jax, jax.numpy as jnp
from jax import lax
import numpy as np

D_MODEL = 1024
BATCH = 4
SEQ = 8192
DEPTH = 1

D_PLE = 256
GRID_W = 64
D_SSM = 512
SSM_GROUP = 16
N_SSM_GROUPS = D_SSM // SSM_GROUP
SSM_STATE = 64
DT_MIN = 1e-3
DT_MAX = 1e-1
D_NA = 512
NA_HEADS = 8
NA_HEAD_DIM = D_NA // NA_HEADS
NA_ROWS_MAX = 8
NA_COLS = 16
D_MIX = D_SSM + D_NA
D_IN_PROJ = 2 * D_SSM + 4 * D_NA
EPS = 1e-6

kernel_name = "hybrid_s5_natten_sandwich_ple_encoder"


def rms_norm(x, gain):
    xf = x.astype(jnp.float32)
    y = xf * lax.rsqrt(jnp.mean(xf * xf, axis=-1, keepdims=True) + EPS)
    return (y * gain.astype(jnp.float32)).astype(x.dtype)


def _complex_linear_combine(left, right):
    a1r, a1i, b1r, b1i = left
    a2r, a2i, b2r, b2i = right
    return (a2r * a1r - a2i * a1i,
            a2r * a1i + a2i * a1r,
            a2r * b1r - a2i * b1i + b2r,
            a2r * b1i + a2i * b1r + b2i)


def s5_bidirectional(u, a_re, a_im, log_dt, b_re, b_im, c_re, c_im, d):
    f32 = jnp.float32
    bsz, seqlen, _ = u.shape
    uf = u.astype(f32).reshape(bsz, seqlen, N_SSM_GROUPS, SSM_GROUP)
    y = uf * d.astype(f32)
    for direction in range(2):
        ar = a_re[direction].astype(f32)
        ai = a_im[direction].astype(f32)
        dt = jnp.exp(log_dt[direction].astype(f32))[:, None]
        mag = jnp.exp(dt * ar)
        abar_re = mag * jnp.cos(dt * ai)
        abar_im = mag * jnp.sin(dt * ai)
        num_re = abar_re - 1.0
        num_im = abar_im
        denom = ar * ar + ai * ai
        coef_re = (num_re * ar + num_im * ai) / denom
        coef_im = (num_im * ar - num_re * ai) / denom
        br = b_re[direction].astype(f32)
        bi = b_im[direction].astype(f32)
        bbar_re = coef_re[..., None] * br - coef_im[..., None] * bi
        bbar_im = coef_re[..., None] * bi + coef_im[..., None] * br
        bu_re = jnp.einsum('blgh,gph->blgp', uf, bbar_re)
        bu_im = jnp.einsum('blgh,gph->blgp', uf, bbar_im)
        shp = (1, seqlen, N_SSM_GROUPS, SSM_STATE)
        a_seq_re = jnp.broadcast_to(abar_re, shp)
        a_seq_im = jnp.broadcast_to(abar_im, shp)
        _, _, h_re, h_im = lax.associative_scan(
            _complex_linear_combine, (a_seq_re, a_seq_im, bu_re, bu_im),
            reverse=(direction == 1), axis=1)
        cr = c_re[direction].astype(f32)
        ci = c_im[direction].astype(f32)
        y = y + jnp.einsum('blgp,ghp->blgh', h_re, cr) - jnp.einsum('blgp,ghp->blgh', h_im, ci)
    return y.reshape(bsz, seqlen, D_SSM)


def neighborhood_attention_2d(q, k, v, rpb):
    f32 = jnp.float32
    bsz, seqlen, _ = q.shape
    rows = seqlen // GRID_W
    kh = min(NA_ROWS_MAX, rows)
    shp = (bsz, rows, GRID_W, NA_HEADS, NA_HEAD_DIM)
    q = q.reshape(shp)
    k = k.reshape(shp)
    v = v.reshape(shp)
    r = jnp.arange(rows)
    row_start = jnp.clip(r - kh // 2, 0, rows - kh)
    row_idx = row_start[:, None] + jnp.arange(kh)[None, :]
    k_blk = k[:, row_idx]
    v_blk = v[:, row_idx]
    c = jnp.arange(GRID_W)
    col_start = jnp.clip(c - NA_COLS // 2, 0, GRID_W - NA_COLS)
    col_in = (c[None, :] >= col_start[:, None]) & (c[None, :] < col_start[:, None] + NA_COLS)
    dr = row_idx - r[:, None] + (NA_ROWS_MAX - 1)
    dc = jnp.clip(c[None, :] - c[:, None] + (NA_COLS - 1), 0, 2 * NA_COLS - 2)
    bias = rpb.astype(f32)[:, dr[:, None, :, None], dc[None, :, None, :]]
    scale = NA_HEAD_DIM ** -0.5
    scores = jnp.einsum('brqhd,brikhd->bhrqik', q, k_blk,
                        preferred_element_type=f32) * scale + bias
    scores = jnp.where(col_in[:, None, :], scores, jnp.finfo(f32).min)
    probs = jax.nn.softmax(scores, axis=(-2, -1))
    out = jnp.einsum('bhrqik,brikhd->brqhd', probs.astype(v.dtype), v_blk)
    return out.reshape(bsz, seqlen, D_NA)


def setup_inputs(seed: int = 0) -> dict:
    key = jax.random.key(seed)
    ks = jax.random.split(key, 24)
    f32 = jnp.float32
    G, P, H = N_SSM_GROUPS, SSM_STATE, SSM_GROUP
    nrm = lambda kk, shape, s: jax.random.normal(kk, shape, f32) * s
    n_idx = jnp.arange(P, dtype=f32)
    return {
        "x": nrm(ks[0], (BATCH, SEQ, D_MODEL), 1.0),
        "p": nrm(ks[1], (DEPTH, BATCH, SEQ, D_PLE), 1.0),
        "norm_pre": 1.0 + nrm(ks[2], (DEPTH, D_MODEL), 0.02),
        "norm_post": 1.0 + nrm(ks[3], (DEPTH, D_MODEL), 0.02),
        "w_in": nrm(ks[4], (DEPTH, D_MODEL, D_IN_PROJ), D_MODEL ** -0.5),
        "ssm_a_re": -0.5 + nrm(ks[5], (DEPTH, 2, G, P), 0.01),
        "ssm_a_im": math.pi * n_idx + nrm(ks[6], (DEPTH, 2, G, P), 0.01),
        "ssm_log_dt": jax.random.uniform(ks[7], (DEPTH, 2, G), f32,
                                         minval=math.log(DT_MIN), maxval=math.log(DT_MAX)),
        "ssm_b_re": nrm(ks[8], (DEPTH, 2, G, P, H), H ** -0.5),
        "ssm_b_im": nrm(ks[9], (DEPTH, 2, G, P, H), H ** -0.5),
        "ssm_c_re": nrm(ks[10], (DEPTH, 2, G, H, P), P ** -0.5),
        "ssm_c_im": nrm(ks[11], (DEPTH, 2, G, H, P), P ** -0.5),
        "ssm_d": nrm(ks[12], (DEPTH, G, H), 1.0),
        "w_glu": nrm(ks[13], (DEPTH, D_SSM, D_SSM), D_SSM ** -0.5),
        "b_glu": nrm(ks[14], (DEPTH, D_SSM), 0.01),
        "na_rpb": nrm(ks[15], (DEPTH, NA_HEADS, 2 * NA_ROWS_MAX - 1, 2 * NA_COLS - 1), 0.02),
        "w_out": nrm(ks[16], (DEPTH, D_MIX, D_MODEL), D_MIX ** -0.5),
        "w_ple": nrm(ks[17], (DEPTH, D_PLE, D_MODEL), D_PLE ** -0.5),
        "ple_norm": 1.0 + nrm(ks[18], (DEPTH, D_MODEL), 0.02),
        "w_ple_gate": nrm(ks[19], (DEPTH, D_MODEL, D_MODEL), D_MODEL ** -0.5),
    }


def reference(x, p, norm_pre, norm_post, w_in, ssm_a_re, ssm_a_im, ssm_log_dt,
              ssm_b_re, ssm_b_im, ssm_c_re, ssm_c_im, ssm_d, w_glu, b_glu,
              na_rpb, w_out, w_ple, ple_norm, w_ple_gate):
    h = x
    splits = [D_SSM, 2 * D_SSM, 2 * D_SSM + D_NA, 2 * D_SSM + 2 * D_NA, 2 * D_SSM + 3 * D_NA]
    for i in range(DEPTH):
        hn = rms_norm(h, norm_pre[i])
        proj = hn @ w_in[i]
        u_s, z_s, q, k, v, z_n = jnp.split(proj, splits, axis=-1)
        y_s = s5_bidirectional(u_s, ssm_a_re[i], ssm_a_im[i], ssm_log_dt[i],
                               ssm_b_re[i], ssm_b_im[i], ssm_c_re[i], ssm_c_im[i], ssm_d[i])
        y_s = jax.nn.gelu(y_s.astype(hn.dtype))
        y_s = y_s * jax.nn.sigmoid(y_s @ w_glu[i] + b_glu[i])
        y_s = y_s * jax.nn.silu(z_s)
        y_n = neighborhood_attention_2d(q, k, v, na_rpb[i]) * jax.nn.silu(z_n)
        mix = jnp.concatenate([y_s, y_n], axis=-1) @ w_out[i]
        h = h + rms_norm(mix, norm_post[i])
        e = rms_norm(p[i] @ w_ple[i], ple_norm[i])
        h = h + jax.nn.sigmoid(h @ w_ple_gate[i]) * e
    return h
```

```python
import numpy as np
from contextlib import ExitStack
import concourse.bass as bass
import concourse.mybir as mybir
from concourse.bass_utils import run_bass_kernel_spmd

F32 = mybir.dt.float32
BF16 = mybir.dt.bfloat16
I32 = mybir.dt.int32
AF = mybir.ActivationFunctionType
ALU = mybir.AluOpType

NTOK = 8192
HALF = 4096
D = 1024
EPS = 1e-6
TWO_PI = 6.283185307179586

KSEG = {
    "A": [7 - i for i in range(8)],
    "B": [i for i in range(8)],
    "C": [i + 1 for i in range(8)],
    "D": [8 - i for i in range(8)],
    "E": [-i for i in range(8)],
    "G": [8 * (i + 1) for i in range(7)],
    "H": [64],
}
KOFF = {}
KVALS = []
for _n in "ABCDEGH":
    KOFF[_n] = len(KVALS)
    KVALS += KSEG[_n]
NK = len(KVALS)


class Buf:
    def __init__(self, name=""):
        self.name = name
        self.w = None
        self.r = {}
        self.dsem = None
        self.dcount = 0


class K:
    ENG = ["tensor", "vector", "scalar", "gpsimd", "sync"]

    def __init__(self, nc, stack):
        self.nc = nc
        self.stack = stack
        self.ops = {e: [] for e in self.ENG}
        self.cnt = {e: 0 for e in self.ENG}
        self.waited = {e: {} for e in self.ENG}
        self.sems = {}
        for e in self.ENG:
            self.sems[e] = stack.enter_context(nc.semaphore("s_" + e))
        self.nd = 0
        self.dlast = {}

    def _wait(self, eng, tok):
        if tok is None:
            return
        key, val = tok
        if key == eng and val > self.cnt[eng]:
            return
        if self.waited[eng].get(key, 0) >= val:
            return
        self.waited[eng][key] = val
        self.ops[eng].append(("w", key, val))

    def _deps(self, eng, reads, writes, same_eng_war=False):
        need = {}

        def add(tok):
            if tok is not None and need.get(tok[0], 0) < tok[1]:
                need[tok[0]] = tok[1]
        for b in reads:
            add(b.w)
        for b in writes:
            add(b.w)
            for key, val in b.r.items():
                add((key, val))
        for key, val in need.items():
            self._wait(eng, (key, val))

    def op(self, eng, fn, reads=(), writes=(), sig=True):
        self._deps(eng, reads, writes)
        if sig:
            self.cnt[eng] += 1
            tok = (eng, self.cnt[eng])
            self.ops[eng].append(("o", fn, eng, 1))
            for b in reads:
                b.r[eng] = tok[1]
            for b in writes:
                b.w = tok
                b.r = {}
        else:
            self.ops[eng].append(("o", fn, None, 0))
            nxt = self.cnt[eng] + 1
            for b in reads:
                b.r[eng] = nxt
            for b in writes:
                b.w = (eng, nxt)
                b.r = {}

    def dma(self, eng, fn, reads=(), writes=(), owner=None):
        self._deps(eng, reads, writes, same_eng_war=True)
        if owner is None:
            owner = (list(writes) + list(reads))[0]
        if owner.dsem is None:
            owner.dsem = "d%d" % self.nd
            self.nd += 1
            self.sems[owner.dsem] = self.stack.enter_context(self.nc.semaphore(owner.dsem))
        owner.dcount += 16
        tok = (owner.dsem, owner.dcount)
        self.dlast[owner.dsem] = owner.dcount
        self.ops[eng].append(("o", fn, owner.dsem, 16))
        for b in reads:
            b.r[tok[0]] = tok[1]
        for b in writes:
            b.w = tok
            b.r = {}
        return tok

    def wait_tok(self, eng, tok):
        self._wait(eng, tok)

    def check_deadlock(self):
        if not hasattr(self, "semval"):
            self.semval = {}
        pos = {e: 0 for e in self.ENG}
        progress = True
        while progress:
            progress = False
            for e in self.ENG:
                lst = self.ops[e]
                while pos[e] < len(lst):
                    it = lst[pos[e]]
                    if it[0] == "w":
                        if self.semval.get(it[1], 0) >= it[2]:
                            pos[e] += 1
                            progress = True
                        else:
                            break
                    else:
                        if it[2] is not None:
                            self.semval[it[2]] = self.semval.get(it[2], 0) + it[3]
                        pos[e] += 1
                        progress = True
        for e in self.ENG:
            if pos[e] < len(self.ops[e]):
                it = self.ops[e][pos[e]]
                raise RuntimeError("deadlock: engine %s stuck at item %d/%d waiting %s>=%s (have %s)" % (
                    e, pos[e], len(self.ops[e]), it[1], it[2], self.semval.get(it[1], 0)))

    def emit(self):
        nc = self.nc
        for key, val in self.dlast.items():
            self._wait("sync", (key, val))
        self.check_deadlock()
        with nc.Block() as block:
            for e in self.ENG:
                lst = self.ops[e]
                if not lst:
                    continue

                def body(engine, lst=lst):
                    for it in lst:
                        if it[0] == "w":
                            engine.wait_ge(self.sems[it[1]], it[2])
                        else:
                            ins = it[1](engine)
                            if it[2] is not None:
                                ins.then_inc(self.sems[it[2]], it[3])
                getattr(block, e)(body)
        self.ops = {e: [] for e in self.ENG}


class Ctx:
    pass


def sbt(nc, st, name, shape, dt):
    return st.enter_context(nc.sbuf_tensor("sb_" + name, list(shape), dt))


def ssm_phase(nc, k, st0, dr, PS, bPS, ident, bident, U, bUblk, stage):
    def TT(eng, out, a, b, op, rd, wr):
        k.op(eng, lambda e: e.tensor_tensor(out=out, in0=a, in1=b, op=op), rd, wr)

    def TS(eng, out, a, s1, op0, rd, wr, s2=None, op1=None):
        if op1 is None:
            k.op(eng, lambda e: e.tensor_scalar(out=out, in0=a, scalar1=s1, scalar2=None, op0=op0), rd, wr)
        else:
            k.op(eng, lambda e: e.tensor_scalar(out=out, in0=a, scalar1=s1, scalar2=s2, op0=op0, op1=op1), rd, wr)

    def ACT(out, in_, func, rd, wr, **kw):
        k.op("scalar", lambda e: e.activation(out=out, in_=in_, func=func, **kw), rd, wr)

    def CP(eng, out, in_, rd, wr):
        k.op(eng, lambda e: e.tensor_copy(out=out, in_=in_), rd, wr)

    def MM(out, lhsT, rhs, start, stop, rd, wr, sig):
        k.op("tensor", lambda e: e.matmul(out, lhsT=lhsT, rhs=rhs, start=start, stop=stop), rd, wr, sig=sig)

    pi = [0]

    def nextps():
        pb = pi[0] % 8
        pi[0] += 1
        return pb

    with ExitStack() as stS:
        cst = sbt(nc, stS, "ssmc", [128, NK + 33 + 512], F32); bcst = Buf()
        k.dma("sync", lambda e: e.dma_start(out=cst[:], in_=dr["ssm_c"]), writes=[bcst])
        kv = cst[:, 0:NK]
        dvec = cst[:, NK:NK + 32]
        sgn = cst[:, NK + 32:NK + 33]
        c0 = NK + 33
        IIf = cst[:, c0:c0 + 128]
        maskF = cst[:, c0 + 128:c0 + 256]
        maskB = cst[:, c0 + 256:c0 + 384]
        identF = cst[:, c0 + 384:c0 + 512]
        II = sbt(nc, stS, "II", [128, 128], BF16); bII = Buf()
        CP("vector", II[:], IIf, [bcst], [bII])
        identJ = sbt(nc, stS, "identJ", [128, 128], BF16); bidJ = Buf()
        TS("vector", identJ[:], identF, sgn, ALU.mult, [bcst], [bidJ])
        sc = sbt(nc, stS, "sc", [128, 4, 7, 64], F32); bsc = Buf()
        AA = sbt(nc, stS, "AA", [64, 2, 2, 32], F32); bAA = Buf()
        AB = sbt(nc, stS, "AB", [64, 2, 2, 32], F32); bAB = Buf()
        T8 = sbt(nc, stS, "T8", [128, 32, 128], BF16); bT8 = [Buf() for _ in range(32)]
        BfS = sbt(nc, stS, "BfS", [128, 64, 128], BF16); bBfS = [Buf() for _ in range(2)]
        CfS = sbt(nc, stS, "CfS", [128, 64, 128], BF16); bCfS = [Buf() for _ in range(2)]

        with ExitStack() as st:
            par = sbt(nc, st, "par", [128, 2144], F32); bpar = Buf()
            npi = sbt(nc, st, "npi", [128, 1], F32); bnpi = Buf()
            k.op("vector", lambda e: e.memset(npi[:], -3.141592653589793), [], [bnpi])
            NKG = NK * 32
            tl = {}
            for nm in ["KLre", "KLim", "y", "yf", "Wre", "Wim"]:
                tl[nm] = (sbt(nc, st, nm, [128, NK, 32], F32), Buf())
            yi = sbt(nc, st, "yi", [128, NK, 32], I32); byi = Buf()
            sm = {}
            for nm in ["dt", "lre", "lim", "nre", "den", "t1", "t2", "cre", "cim"]:
                sm[nm] = (sbt(nc, st, "s_" + nm, [128, 32], F32), Buf())
            bbre = sbt(nc, st, "bbre", [128, 32, 16], F32); bbbre = Buf()
            bbim = sbt(nc, st, "bbim", [128, 32, 16], F32); bbbim = Buf()
            tb = sbt(nc, st, "tb", [128, 32, 16], F32); btb = Buf()
            P1 = sbt(nc, st, "P1", [128, 32, 16], F32); bP1 = Buf()
            P2 = sbt(nc, st, "P2", [128, 32, 16], F32); bP2 = Buf()
            P1c = sbt(nc, st, "P1c", [128, 32, 16], F32); bP1c = Buf()
            P2c = sbt(nc, st, "P2c", [128, 32, 16], F32); bP2c = Buf()
            m1 = sbt(nc, st, "m1", [128, 8, 8, 16], F32); bm1 = Buf()
            m2 = sbt(nc, st, "m2", [128, 8, 8, 16], F32); bm2 = Buf()
            BnS = sbt(nc, st, "BnS", [128, 32, 128], BF16); bBnS = Buf()
            CpS = sbt(nc, st, "CpS", [128, 32, 128], BF16); bCpS = Buf()
            tmpm = [sbt(nc, st, "tmpm%d" % i, [128, 128], F32) for i in range(2)]
            btmpm = [Buf() for _ in range(2)]
            for d in range(2):
                k.dma("sync", lambda e, d=d: e.dma_start(out=par[:], in_=dr["ssm_par"][d]), writes=[bpar])
                ar = par[:, 0:32]; ai = par[:, 32:64]; ldt = par[:, 64:96]
                br = par[:, 96:608].rearrange("p (g h) -> p g h", h=16)
                bi = par[:, 608:1120].rearrange("p (g h) -> p g h", h=16)
                cr = par[:, 1120:1632].rearrange("p (g h) -> p g h", h=16)
                ci = par[:, 1632:2144].rearrange("p (g h) -> p g h", h=16)
                dt, bdt = sm["dt"]; lre, blre = sm["lre"]; lim, blim = sm["lim"]
                ACT(dt[:], ldt, AF.Exp, [bpar], [bdt])
                TT("vector", lre[:], dt[:], ar, ALU.mult, [bdt, bpar], [blre])
                TT("vector", lim[:], dt[:], ai, ALU.mult, [bdt, bpar], [blim])
                PWr, bPWr = tl["KLre"]; PWi, bPWi = tl["KLim"]
                sc_, bsc_ = tl["y"]; sc2, bsc2 = tl["yf"]
                cs, sn, m1_, mneg, ta, tb_, tc, td = [sc_[:, i_, :] for i_ in range(8)]
                bq = bsc_
                ACT(sn, lim[:], AF.Sin, [blim], [bq], scale=1.0 / 8.0)
                ACT(ta, lim[:], AF.Sin, [blim], [bq], scale=1.0 / 16.0)
                TT("vector", ta, ta, ta, ALU.mult, [bq], [bq])
                TS("vector", cs, ta, -2.0, ALU.mult, [bq], [bq], s2=1.0, op1=ALU.add)
                for _ in range(3):
                    TT("vector", ta, cs, cs, ALU.mult, [bq], [bq])
                    TT("vector", tb_, sn, sn, ALU.mult, [bq], [bq])
                    TT("vector", tc, cs, sn, ALU.mult, [bq], [bq])
                    TT("vector", cs, ta, tb_, ALU.subtract, [bq], [bq])
                    TS("vector", sn, tc, 2.0, ALU.mult, [bq], [bq])
                ACT(m1_, lre[:], AF.Exp, [blre], [bq])
                ACT(mneg, lre[:], AF.Exp, [blre], [bq], scale=-1.0)
                pidx = {}

                def slot_(k_):
                    if k_ not in pidx:
                        pidx[k_] = len(pidx)
                    return PWr[:, pidx[k_], :], PWi[:, pidx[k_], :]
                bP = [bPWr, bPWi]

                def cmul(dst, a_, b_):
                    (dr_, di_), (ar_, ai_), (br_, bi_) = dst, a_, b_
                    TT("vector", ta, ar_, br_, ALU.mult, bP + [bq], [bq])
                    TT("vector", tb_, ai_, bi_, ALU.mult, bP + [bq], [bq])
                    TT("vector", tc, ar_, bi_, ALU.mult, bP + [bq], [bq])
                    TT("vector", td, ai_, br_, ALU.mult, bP + [bq], [bq])
                    TT("vector", dr_, ta, tb_, ALU.subtract, [bq], bP)
                    TT("vector", di_, tc, td, ALU.add, [bq], bP)
                p0 = slot_(0)
                k.op("vector", lambda e: e.memset(p0[0], 1.0), [], bP)
                k.op("vector", lambda e: e.memset(p0[1], 0.0), [], bP)
                p1 = slot_(1)
                TT("vector", p1[0], m1_, cs, ALU.mult, [bq], bP)
                TT("vector", p1[1], m1_, sn, ALU.mult, [bq], bP)
                for k_ in range(2, 9):
                    cmul(slot_(k_), slot_(k_ - 1), p1)
                n1 = slot_(-1)
                TT("vector", n1[0], mneg, cs, ALU.mult, [bq], bP)
                TT("vector", tc, mneg, sn, ALU.mult, [bq], [bq])
                TS("vector", n1[1], tc, -1.0, ALU.mult, [bq], bP)
                for k_ in range(2, 8):
                    cmul(slot_(-k_), slot_(-(k_ - 1)), n1)
                p8 = slot_(8)
                for j_ in range(2, 9):
                    cmul(slot_(8 * j_), slot_(8 * (j_ - 1)), p8)
                Wre, bWre = tl["Wre"]; Wim, bWim = tl["Wim"]
                for t_, k_ in enumerate(KVALS):
                    sr_, si_ = slot_(k_)
                    CP("vector", Wre[:, t_, :], sr_, bP, [bWre])
                    ACT(Wim[:, t_, :], si_, AF.Copy, bP, [bWim])
                Wre, bWre = tl["Wre"]; Wim, bWim = tl["Wim"]
                G0 = KOFF["G"]
                gds = slice(d * 32, (d + 1) * 32)
                lo = slice(0, 64); hi = slice(64, 128)
                rdW = [bWre, bWim]
                CP("vector", sc[lo, 0, :, gds], Wre[lo, G0:G0 + 7, :], rdW, [bsc])
                CP("vector", sc[hi, 0, :, gds], Wim[hi, G0:G0 + 7, :], rdW, [bsc])
                TS("vector", sc[lo, 1, :, gds], Wim[lo, G0:G0 + 7, :], -1.0, ALU.mult, rdW, [bsc])
                CP("vector", sc[hi, 1, :, gds], Wre[hi, G0:G0 + 7, :], rdW, [bsc])
                CP("vector", sc[lo, 2, :, gds], Wre[lo, G0:G0 + 7, :], rdW, [bsc])
                TS("vector", sc[hi, 2, :, gds], Wim[hi, G0:G0 + 7, :], -1.0, ALU.mult, rdW, [bsc])
                TS("vector", sc[lo, 3, :, gds], Wim[lo, G0:G0 + 7, :], -1.0, ALU.mult, rdW, [bsc])
                TS("vector", sc[hi, 3, :, gds], Wre[hi, G0:G0 + 7, :], -1.0, ALU.mult, rdW, [bsc])
                H0 = KOFF["H"]
                CP("vector", AA[:, d, 0, :], Wre[lo, H0, :], rdW, [bAA])
                CP("vector", AA[:, d, 1, :], Wre[lo, H0, :], rdW, [bAA])
                TS("vector", AB[:, d, 0, :], Wim[lo, H0, :], -1.0, ALU.mult, rdW, [bAB])
                CP("vector", AB[:, d, 1, :], Wim[lo, H0, :], rdW, [bAB])
                k1 = KOFF["B"] + 1
                nre, bnre = sm["nre"]; den, bden = sm["den"]; t1, bt1 = sm["t1"]; t2, bt2 = sm["t2"]
                cre, bcre = sm["cre"]; cim, bcim = sm["cim"]
                TS("vector", nre[:], Wre[:, k1, :], -1.0, ALU.add, rdW, [bnre])
                TT("vector", den[:], ar, ar, ALU.mult, [bpar], [bden])
                TT("vector", t1[:], ai, ai, ALU.mult, [bpar], [bt1])
                TT("vector", den[:], den[:], t1[:], ALU.add, [bden, bt1], [bden])
                k.op("vector", lambda e, den=den: e.reciprocal(out=den[:], in_=den[:]), [bden], [bden])
                TT("vector", t1[:], nre[:], ar, ALU.mult, [bnre, bpar], [bt1])
                TT("vector", t2[:], Wim[:, k1, :], ai, ALU.mult, rdW + [bpar], [bt2])
                TT("vector", t1[:], t1[:], t2[:], ALU.add, [bt1, bt2], [bt1])
                TT("vector", cre[:], t1[:], den[:], ALU.mult, [bt1, bden], [bcre])
                TT("vector", t1[:], Wim[:, k1, :], ar, ALU.mult, rdW + [bpar], [bt1])
                TT("vector", t2[:], nre[:], ai, ALU.mult, [bnre, bpar], [bt2])
                TT("vector", t1[:], t1[:], t2[:], ALU.subtract, [bt1, bt2], [bt1])
                TT("vector", cim[:], t1[:], den[:], ALU.mult, [bt1, bden], [bcim])
                creb = cre[:].unsqueeze(2).broadcast_to([128, 32, 16])
                cimb = cim[:].unsqueeze(2).broadcast_to([128, 32, 16])
                TT("vector", bbre[:], creb, br, ALU.mult, [bcre, bpar], [bbbre])
                TT("vector", tb[:], cimb, bi, ALU.mult, [bcim, bpar], [btb])
                TT("vector", bbre[:], bbre[:], tb[:], ALU.subtract, [bbbre, btb], [bbbre])
                TT("vector", bbim[:], creb, bi, ALU.mult, [bcre, bpar], [bbbim])
                TT("vector", tb[:], cimb, br, ALU.mult, [bcim, bpar], [btb])
                TT("vector", bbim[:], bbim[:], tb[:], ALU.add, [bbbim, btb], [bbbim])
                rb = [bbbre, bbbim]
                CP("vector", P1[lo], bbre[lo], rb, [bP1])
                CP("vector", P1[hi], bbim[hi], rb, [bP1])
                TS("vector", P2[lo], bbim[lo], -1.0, ALU.mult, rb, [bP2])
                CP("vector", P2[hi], bbre[hi], rb, [bP2])
                CP("vector", P1c[lo], cr[lo], [bpar], [bP1c])
                TS("vector", P1c[hi], ci[hi], -1.0, ALU.mult, [bpar], [bP1c])
                TS("vector", P2c[lo], ci[lo], -1.0, ALU.mult, [bpar], [bP2c])
                TS("vector", P2c[hi], cr[hi], -1.0, ALU.mult, [bpar], [bP2c])
                segs = {0: ("A", "C", "E", "B"), 1: ("B", "D", "B", "E")}[d]
                jobs = [(BfS[:, gds, :], bBfS[d], segs[0], P1, P2, bP1, bP2),
                        (CfS[:, gds, :], bCfS[d], segs[1], P1c, P2c, bP1c, bP2c),
                        (BnS[:], bBnS, segs[2], P1, P2, bP1, bP2),
                        (CpS[:], bCpS, segs[3], P1c, P2c, bP1c, bP2c)]
                for ji, (outap, bout, seg, Pa, Pb, bPa, bPb) in enumerate(jobs):
                    o = KOFF[seg]
                    for half in range(4):
                        g0 = half * 8
                        eng = "vector"
                        wre = Wre[:, o:o + 8, g0:g0 + 8].rearrange("p k g -> p g k").unsqueeze(3).broadcast_to([128, 8, 8, 16])
                        wim = Wim[:, o:o + 8, g0:g0 + 8].rearrange("p k g -> p g k").unsqueeze(3).broadcast_to([128, 8, 8, 16])
                        pa = Pa[:, g0:g0 + 8, :].unsqueeze(2).broadcast_to([128, 8, 8, 16])
                        pb_ = Pb[:, g0:g0 + 8, :].unsqueeze(2).broadcast_to([128, 8, 8, 16])
                        TT(eng, m1[:], wre, pa, ALU.mult, rdW + [bPa], [bm1])
                        TT(eng, m2[:], wim, pb_, ALU.mult, rdW + [bPb], [bm2])
                        TT(eng, outap[:, g0:g0 + 8, :].rearrange("p g (k h) -> p g k h", h=16), m1[:], m2[:],
                           ALU.add, [bm1, bm2], [bout])
                for g4 in range(8):
                    pb = nextps()
                    for gg in range(4):
                        g = g4 * 4 + gg
                        MM(PS[pb][:, gg * 128:(gg + 1) * 128], BnS[:, g, :], CpS[:, g, :], True, True,
                           [bBnS, bCpS], [bPS[pb]], sig=(gg == 3))
                    for gg in range(4):
                        g = g4 * 4 + gg
                        tm, btm = tmpm[g % 2], btmpm[g % 2]
                        if d == 0:
                            TT("vector", tm[:], PS[pb][:, gg * 128:(gg + 1) * 128], maskF, ALU.mult,
                               [bPS[pb], bcst], [btm])
                            k.op("vector", lambda e, g=g, tm=tm: e.scalar_tensor_tensor(
                                out=T8[:, g, :], in0=identF, scalar=dvec[:, g:g + 1], in1=tm[:],
                                op0=ALU.mult, op1=ALU.add), [btm, bcst], [bT8[g]])
                        else:
                            TT("vector", tm[:], PS[pb][:, gg * 128:(gg + 1) * 128], maskB, ALU.mult,
                               [bPS[pb], bcst], [btm])
                            TT("vector", T8[:, g, :], tm[:], T8[:, g, :], ALU.add, [btm, bT8[g]], [bT8[g]])
            k.emit()

        with ExitStack() as stZ:
            Zs = sbt(nc, stZ, "Zs", [64, 2, 64, 128], BF16); bZs = [Buf() for _ in range(64)]
            hist = sbt(nc, stZ, "hist", [64, 2, 2, 32, 64], BF16); bhist = [Buf(), Buf()]
            ST = sbt(nc, stZ, "ST", [128, 2, 32, 64], BF16); bST = [Buf(), Buf()]
            with ExitStack() as st:
                JM = [sbt(nc, st, "JM%d" % i, [128, 7, 128], BF16) for i in range(2)]
                bJM = [Buf() for _ in range(2)]
                B64 = [sbt(nc, st, "B64_%d" % i, [128, 8, 128], BF16) for i in range(2)]
                bB64 = [Buf() for _ in range(2)]
                def r1_build(it):
                    g, d = it // 2, it % 2
                    gd = d * 32 + g
                    q = it % 2
                    TT("vector", JM[q][:].rearrange("p k (w c) -> p k w c", w=2),
                       II[:].rearrange("p (w c) -> p w c", w=2).unsqueeze(1).broadcast_to([128, 7, 2, 64]),
                       sc[:, 2:4, :, gd].rearrange("p w k -> p k w").unsqueeze(3).broadcast_to([128, 7, 2, 64]),
                       ALU.mult, [bII, bsc], [bJM[q]])

                def r1_gen(it):
                    g, d = it // 2, it % 2
                    gd = d * 32 + g
                    q = it % 2
                    for half in range(2):
                        pb = nextps()
                        for xx in range(4):
                            x = half * 4 + xx
                            k8 = 7 - x
                            rhs = identJ[:] if k8 == 0 else JM[q][:, k8 - 1, :]
                            MM(PS[pb][:, xx * 128:(xx + 1) * 128], BfS[:, gd, :], rhs, True, True,
                               [bBfS[d], bJM[q], bidJ], [bPS[pb]], sig=(xx == 3))
                        if half == 0:
                            ACT(B64[q][:, 0:4, :], PS[pb][:].rearrange("p (a b) -> p a b", a=4), AF.Copy,
                                [bPS[pb]], [bB64[q]])
                        else:
                            CP("vector", B64[q][:, 4:8, :], PS[pb][:].rearrange("p (a b) -> p a b", a=4),
                               [bPS[pb]], [bB64[q]])

                def r1_z(it):
                    g, d = it // 2, it % 2
                    gd = d * 32 + g
                    q = it % 2
                    pb = nextps()
                    N = 128 if d == 0 else 64
                    for ri in range(2):
                        for x in range(8):
                            if d == 0:
                                rhs = U[:, g, :].rearrange("p (hf j m) -> p hf j m", hf=2, j=8)[:, :, x, :]
                            else:
                                rhs = U[:, g, 512 + (7 - x) * 64:512 + (8 - x) * 64]
                            MM(PS[pb][0:64, ri * 128:ri * 128 + N], B64[q][:, x, ri * 64:(ri + 1) * 64], rhs,
                               x == 0, x == 7, [bB64[q]] + bUblk, [bPS[pb]], sig=(ri == 1 and x == 7))
                    m0 = 0 if d == 0 else 64
                    ACT(Zs[:, 0, gd, m0:m0 + N], PS[pb][0:64, 0:N], AF.Copy, [bPS[pb]], [bZs[gd]])
                    ACT(Zs[:, 1, gd, m0:m0 + N], PS[pb][0:64, 128:128 + N], AF.Copy, [bPS[pb]], [bZs[gd]], scale=-1.0)

                r1_build(0)
                for it in range(65):
                    if it + 1 < 64:
                        r1_build(it + 1)
                    if it < 64:
                        r1_gen(it)
                    if it >= 1:
                        r1_z(it - 1)
                k.emit()
            with ExitStack() as st:
                Sst = [sbt(nc, st, "S_%d" % i, [64, 2, 2, 32], F32) for i in range(2)]
                bS = [Buf() for _ in range(2)]
                T1 = sbt(nc, st, "T1", [64, 2, 2, 32], F32); bT1 = Buf()
                T2 = sbt(nc, st, "T2", [64, 2, 2, 32], F32); bT2 = Buf()
                for i in range(2):
                    k.op("vector", lambda e, i=i: e.memset(Sst[i][:], 0.0), [], [bS[i]])
                for s_ in range(128):
                    cur = s_ % 2
                    S, bSc = Sst[cur], bS[cur]
                    Sn, bSn = Sst[1 - cur], bS[1 - cur]
                    dsl = slice(0, 1) if s_ < 64 else slice(0, 2)
                    mb = 191 - s_
                    if s_ >= 64:
                        ACT(hist[:, 0, :, :, s_ - 64], S[:, 0, :, :], AF.Copy, [bSc], [bhist[0]])
                        ACT(hist[:, 1, :, :, mb - 64], S[:, 1, :, :], AF.Copy, [bSc], [bhist[1]])
                    TT("vector", T1[:, dsl], S[:, dsl], AA[:, dsl], ALU.mult, [bSc, bAA], [bT1])
                    TT("vector", T1[:, 0], T1[:, 0], Zs[:, :, 0:32, s_], ALU.add, [bT1] + bZs[0:32], [bT1])
                    if s_ >= 64:
                        TT("vector", T1[:, 1], T1[:, 1], Zs[:, :, 32:64, mb], ALU.add, [bT1] + bZs[32:64], [bT1])
                    TT("vector", T2[:, dsl, 0, :], S[:, dsl, 1, :], AB[:, dsl, 0, :], ALU.mult, [bSc, bAB], [bT2])
                    TT("vector", T2[:, dsl, 1, :], S[:, dsl, 0, :], AB[:, dsl, 1, :], ALU.mult, [bSc, bAB], [bT2])
                    TT("vector", Sn[:, dsl], T1[:, dsl], T2[:, dsl], ALU.add, [bT1, bT2], [bSn])
                for d in range(2):
                    k.dma("sync", lambda e, d=d: e.dma_start(out=ST[0:64, d, :, :], in_=hist[:, d, 0, :, :]),
                          reads=[bhist[d]], writes=[bST[d]])
                    k.dma("sync", lambda e, d=d: e.dma_start(out=ST[64:128, d, :, :], in_=hist[:, d, 1, :, :]),
                          reads=[bhist[d]], writes=[bST[d]])
                k.emit()
            with ExitStack() as st:
                MMb = [sbt(nc, st, "MMb%d" % i, [128, 7, 128], BF16) for i in range(2)]
                bMMb = [Buf() for _ in range(2)]
                C64 = [sbt(nc, st, "C64_%d" % i, [128, 7, 128], BF16) for i in range(6)]
                bC64 = [Buf() for _ in range(6)]
                Tz = [sbt(nc, st, "Tz%d" % i, [128, 7, 128], BF16) for i in range(4)]
                bTz = [Buf() for _ in range(4)]

                def r2_A(g):
                    for d in range(2):
                        gd = d * 32 + g
                        q = d
                        cs = (g % 3) * 2 + d
                        TT("vector", MMb[q][:].rearrange("p k (w c) -> p k w c", w=2),
                           II[:].rearrange("p (w c) -> p w c", w=2).unsqueeze(1).broadcast_to([128, 7, 2, 64]),
                           sc[:, 0:2, :, gd].rearrange("p w k -> p k w").unsqueeze(3).broadcast_to([128, 7, 2, 64]),
                           ALU.mult, [bII, bsc], [bMMb[q]])
                        for half in range(2):
                            pb = nextps()
                            n = 4 if half == 0 else 3
                            for xx in range(n):
                                x = 1 + half * 4 + xx
                                MM(PS[pb][:, xx * 128:(xx + 1) * 128], MMb[q][:, x - 1, :], CfS[:, gd, :], True, True,
                                   [bMMb[q], bCfS[d]], [bPS[pb]], sig=(xx == n - 1))
                            src_ = PS[pb][:, 0:n * 128].rearrange("p (a b) -> p a b", a=n)
                            dst_ = C64[cs][:, half * 4:half * 4 + n, :]
                            if half == 0:
                                ACT(dst_, src_, AF.Copy, [bPS[pb]], [bC64[cs]])
                            else:
                                CP("vector", dst_, src_, [bPS[pb]], [bC64[cs]])

                def r2_B(g):
                    for d in range(2):
                        gd = d * 32 + g
                        cs = (g % 3) * 2 + d
                        ts = (g % 2) * 2 + d
                        for half in range(2):
                            pb = nextps()
                            n = 4 if half == 0 else 3
                            for xx in range(n):
                                dl = 1 + half * 4 + xx
                                rhs = CfS[:, gd, :] if dl == 1 else C64[cs][:, dl - 2, :]
                                MM(PS[pb][:, xx * 128:(xx + 1) * 128], BfS[:, gd, :], rhs, True, True,
                                   [bBfS[d], bCfS[d], bC64[cs]], [bPS[pb]], sig=(xx == n - 1))
                            src_ = PS[pb][:, 0:n * 128].rearrange("p (a b) -> p a b", a=n)
                            dst_ = Tz[ts][:, half * 4:half * 4 + n, :]
                            if half == 0:
                                CP("vector", dst_, src_, [bPS[pb]], [bTz[ts]])
                            else:
                                ACT(dst_, src_, AF.Copy, [bPS[pb]], [bTz[ts]])

                def r2_C(g):
                    cs0, cs1 = (g % 3) * 2, (g % 3) * 2 + 1
                    ts0, ts1 = (g % 2) * 2, (g % 2) * 2 + 1
                    pb = nextps()
                    rdall = bUblk + [bT8[g], bTz[ts0], bTz[ts1], bC64[cs0], bC64[cs1],
                                     bCfS[0], bCfS[1], bST[0], bST[1]]
                    MM(PS[pb][:], T8[:, g, :], U[:, g, 512:1024], True, False, rdall, [bPS[pb]], sig=False)
                    for dl in range(1, 8):
                        MM(PS[pb][:, dl * 64:512], Tz[ts0][:, dl - 1, :], U[:, g, 512:512 + (8 - dl) * 64], False, False,
                           rdall, [bPS[pb]], sig=False)
                        MM(PS[pb][:, 0:(8 - dl) * 64], Tz[ts1][:, dl - 1, :], U[:, g, 512 + dl * 64:1024], False, False,
                           rdall, [bPS[pb]], sig=False)
                    for j2 in range(8):
                        lf = CfS[:, g, :] if j2 == 0 else C64[cs0][:, j2 - 1, :]
                        MM(PS[pb][:, j2 * 64:(j2 + 1) * 64], lf, ST[:, 0, g, :], False, False, rdall, [bPS[pb]], sig=False)
                        xb_ = 7 - j2
                        lb = CfS[:, 32 + g, :] if xb_ == 0 else C64[cs1][:, xb_ - 1, :]
                        MM(PS[pb][:, j2 * 64:(j2 + 1) * 64], lb, ST[:, 1, g, :], False, j2 == 7, rdall, [bPS[pb]],
                           sig=(j2 == 7))
                    yout = U[:, g, 0:512].rearrange("p (m j) -> p j m", j=8)
                    yin = PS[pb][:].rearrange("p (j m) -> p j m", j=8)
                    if g % 2 == 0:
                        ACT(yout, yin, AF.Copy, [bPS[pb]], [bUblk[0]])
                    else:
                        CP("vector", yout, yin, [bPS[pb]], [bUblk[0]])

                for t in range(34):
                    if t < 32:
                        r2_A(t)
                    if 1 <= t <= 32:
                        r2_B(t - 1)
                    if t >= 2:
                        r2_C(t - 2)
                if stage == 2:
                    t1 = k.dma("gpsimd", lambda e: e.dma_start(
                        out=dr["dbg_y"].rearrange("p (g c) -> p g c", g=32), in_=U[:, :, 0:512]), reads=bUblk)
                    k.wait_tok("gpsimd", t1)
                k.emit()


def pass_S(nc, k, dr, PS, bPS, ident, bident, U, bUblk, stage):
    def TT(eng, out, a, b, op, rd, wr):
        k.op(eng, lambda e: e.tensor_tensor(out=out, in0=a, in1=b, op=op), rd, wr)

    def ACT(out, in_, func, rd, wr, **kw):
        k.op("scalar", lambda e: e.activation(out=out, in_=in_, func=func, **kw), rd, wr)

    def CP(eng, out, in_, rd, wr):
        k.op(eng, lambda e: e.tensor_copy(out=out, in_=in_), rd, wr)

    def MM(out, lhsT, rhs, start, stop, rd, wr, sig):
        k.op("tensor", lambda e: e.matmul(out, lhsT=lhsT, rhs=rhs, start=start, stop=stop), rd, wr, sig=sig)

    pi = [0]

    def nextps():
        pb = pi[0] % 8
        pi[0] += 1
        return pb

    with ExitStack() as st:
        wglu = sbt(nc, st, "wglu", [128, 4, 512], BF16); bwglu = Buf()
        k.dma("gpsimd", lambda e: e.dma_start(out=wglu[:], in_=dr["w_glu"].rearrange("(kt f) c -> f kt c", f=128)),
              writes=[bwglu])
        bglu = sbt(nc, st, "bglu", [1, 512], BF16); bbglu = Buf()
        k.dma("gpsimd", lambda e: e.dma_start(out=bglu[:], in_=dr["b_glu"]), writes=[bbglu])
        ones = sbt(nc, st, "ones", [1, 128], BF16); bones = Buf()
        k.op("vector", lambda e: e.memset(ones[:], 1.0), [], [bones])
        ys = sbt(nc, st, "ys", [128, 8, 512], F32); bys = Buf()
        yg = sbt(nc, st, "yg", [128, 8, 512], BF16); byg = Buf()
        zsl = sbt(nc, st, "zsl", [128, 8, 512], BF16); bzsl = Buf()
        mixs = [sbt(nc, st, "mixs%d" % i, [128, 8, 512], BF16) for i in range(2)]
        bmixs = [Buf() for _ in range(2)]
        ygT = [sbt(nc, st, "ygT%d" % i, [128, 4, 128], BF16) for i in range(2)]
        bygT = [Buf() for _ in range(2)]
        sgl = [sbt(nc, st, "sgl%d" % i, [128, 512], F32) for i in range(2)]
        bsgl = [Buf() for _ in range(2)]
        t1 = [sbt(nc, st, "t1s%d" % i, [128, 512], F32) for i in range(2)]
        bt1 = [Buf() for _ in range(2)]
        mdv = dr["mix_d"].rearrange("(blk c j) f -> blk c j f", c=128, j=8)
        for blk in range(4):
            k.dma("sync", lambda e, blk=blk: e.dma_start(out=zsl[:].rearrange("p a b -> p (a b)"), in_=dr["zs_d"][blk]),
                  writes=[bzsl])
            for g4 in range(4):
                pb = nextps()
                pT = PS[pb][:].bitcast(BF16).rearrange("p (g j h) -> p g j h", g=8, j=8)
                for gg in range(8):
                    g = g4 * 8 + gg
                    k.op("tensor", lambda e, pT=pT, gg=gg, g=g, blk=blk: e.transpose(
                        out=pT[:, gg, :, :].rearrange("p j h -> p (j h)"), in_=U[:, g, blk * 128:(blk + 1) * 128],
                        identity=ident[:]), reads=bUblk + [bident], writes=[bPS[pb]], sig=(gg == 7))
                ACT(ys[:, :, g4 * 128:(g4 + 1) * 128].rearrange("p j (g h) -> p j g h", h=16),
                    pT.rearrange("p g j h -> p j g h"), AF.Copy, [bPS[pb]], [bys])
            for half in range(2):
                ACT(yg[:, half * 4:(half + 1) * 4, :], ys[:, half * 4:(half + 1) * 4, :], AF.Gelu_apprx_tanh,
                    [bys], [byg])
            mb = blk % 2
            for j in range(8):
                q = j % 2
                pb = nextps()
                pT = PS[pb][:].bitcast(BF16).rearrange("p (a b) -> p a b", a=8)
                for kt in range(4):
                    k.op("tensor", lambda e, pT=pT, kt=kt, j=j: e.transpose(
                        out=pT[:, kt, :], in_=yg[:, j, kt * 128:(kt + 1) * 128], identity=ident[:]),
                        reads=[byg, bident], writes=[bPS[pb]], sig=(kt == 3))
                CP("vector", ygT[q][:], pT[:, 0:4, :], [bPS[pb]], [bygT[q]])
                pb = nextps()
                for kt in range(4):
                    MM(PS[pb][:], ygT[q][:, kt, :], wglu[:, kt, :], kt == 0, False, [bygT[q], bwglu], [bPS[pb]], False)
                MM(PS[pb][:], ones[:], bglu[:], False, True, [bones, bbglu], [bPS[pb]], True)
                ACT(sgl[q][:], PS[pb][:], AF.Tanh, [bPS[pb]], [bsgl[q]], scale=0.5)
                k.op("vector", lambda e, q=q, j=j: e.scalar_tensor_tensor(
                    out=t1[q][:], in0=yg[:, j, :], scalar=0.25, in1=zsl[:, j, :], op0=ALU.mult, op1=ALU.mult),
                    [byg, bzsl], [bt1[q]])
                k.op("vector", lambda e, q=q, j=j, mb=mb: e.scalar_tensor_tensor(
                    out=mixs[mb][:, j, :], in0=sgl[q][:], scalar=1.0, in1=t1[q][:], op0=ALU.add, op1=ALU.mult),
                    [bt1[q], bsgl[q]], [bmixs[mb]])
            k.dma("sync", lambda e, blk=blk, mb=mb: e.dma_start(out=mdv[blk][:, :, 0:512], in_=mixs[mb][:]),
                  reads=[bmixs[mb]])
        if stage == 3:
            t_ = k.dma("sync", lambda e: e.dma_start(out=dr["dbg_ms"], in_=mixs[1][:].rearrange("p a b -> p (a b)")),
                       reads=[bmixs[1]])
            k.wait_tok("sync", t_)
        k.emit()


def prefetch_N(nc, k, dr, st):
    W = {}
    wsrc = dr["w_in"].rearrange("(kt f) c -> f kt c", f=128)
    W["wq"] = sbt(nc, st, "wq", [128, 8, 2048], BF16)
    W["bwq"] = {nm: [Buf() for _ in range(8)] for nm in ("v", "k", "zn", "q")}
    own = {nm: Buf() for nm in ("v", "k", "zn", "q")}
    for nm, c0 in (("v", 1024), ("k", 512), ("zn", 1536), ("q", 0)):
        for kt in range(8):
            k.dma("gpsimd", lambda e, kt=kt, c0=c0: e.dma_start(
                out=W["wq"][:, kt, c0:c0 + 512], in_=wsrc[:, kt, 1024 + c0:1024 + c0 + 512]),
                writes=[W["bwq"][nm][kt]], owner=own[nm])
    W["tab"] = sbt(nc, st, "tab", [128, 40, 128], BF16); W["btab"] = Buf()
    tabv = dr["na_tab"].rearrange("kd h i k q -> kd k (h i) q")
    tparts = [Buf() for _ in range(4)]
    for hi_, h in enumerate(range(0, 8, 2)):
        k.dma("gpsimd", lambda e, h=h: e.dma_start(
            out=W["tab"][:, h * 5:(h + 2) * 5, :], in_=tabv[0][:, h * 5:(h + 2) * 5, :]),
            writes=[tparts[hi_]], owner=W["btab"])
    W["tparts"] = tparts
    for nm, key, n_ in (("w_out", "wout", 8), ("w_gate", "wgate", 8), ("w_ple", "wple", 2)):
        W[key] = sbt(nc, st, key, [128, n_, 1024], BF16); W["b" + key] = [Buf() for _ in range(n_)]
        ownb = Buf()
        sv = dr[nm].rearrange("(kt f) c -> f kt c", f=128)
        for kt in range(n_):
            k.dma("gpsimd", lambda e, kt=kt, key=key, sv=sv: e.dma_start(out=W[key][:, kt, :], in_=sv[:, kt, :]),
                  writes=[W["b" + key][kt]], owner=ownb)
    return W


def pass_N(nc, k, dr, PS, bPS, ident, bident, stage, W):
    def TT(eng, out, a, b, op, rd, wr):
        k.op(eng, lambda e: e.tensor_tensor(out=out, in0=a, in1=b, op=op), rd, wr)

    def ACT(out, in_, func, rd, wr, **kw):
        k.op("scalar", lambda e: e.activation(out=out, in_=in_, func=func, **kw), rd, wr)

    def CP(eng, out, in_, rd, wr):
        k.op(eng, lambda e: e.tensor_copy(out=out, in_=in_), rd, wr)

    def MM(out, lhsT, rhs, start, stop, rd, wr, sig):
        k.op("tensor", lambda e: e.matmul(out, lhsT=lhsT, rhs=rhs, start=start, stop=stop), rd, wr, sig=sig)

    def TR(out, in_, rd, wr, sig):
        k.op("tensor", lambda e: e.transpose(out=out, in_=in_, identity=ident[:]), rd + [bident], wr, sig=sig)

    pi = [0]

    def nextps():
        pb = pi[0] % 6
        pi[0] += 1
        return pb

    out_toks = []
    with ExitStack() as st:
        wq, bwq, tab, btab = W["wq"], W["bwq"], W["tab"], W["btab"]
        wout, bwout, wgate, bwgate, wple, bwple = W["wout"], W["bwout"], W["wgate"], W["bwgate"], W["wple"], W["bwple"]
        tabv = dr["na_tab"].rearrange("kd h i k q -> kd k (h i) q")

        def load_tab(kind):
            for h in range(0, 8, 2):
                k.dma("gpsimd", lambda e, h=h, kind=kind: e.dma_start(
                    out=tab[:, h * 5:(h + 2) * 5, :], in_=tabv[kind][:, h * 5:(h + 2) * 5, :]), writes=[btab])
            ACT(tab[:], tab[:], AF.Exp, [btab], [btab])
        ACT(tab[:], tab[:], AF.Exp, W["tparts"], [btab])
        npre = sbt(nc, st, "npre2", [128, 8], F32); bnpre = Buf()
        k.dma("sync", lambda e: e.dma_start(out=npre[:], in_=dr["npre"]), writes=[bnpre])
        npost = sbt(nc, st, "npost", [128, 1024], F32); bnpost = Buf()
        k.dma("sync", lambda e: e.dma_start(out=npost[:], in_=dr["npost_b"]), writes=[bnpost])
        plen = sbt(nc, st, "plen", [128, 1024], F32); bplen = Buf()
        k.dma("sync", lambda e: e.dma_start(out=plen[:], in_=dr["plen_b"]), writes=[bplen])
        KT = sbt(nc, st, "KT", [128, 4, 1536], BF16); bKT = [Buf() for _ in range(3)]
        V = sbt(nc, st, "V", [128, 12, 8, 65], BF16); bV = [Buf() for _ in range(3)]
        k.op("vector", lambda e: e.memset(V[:], 2.0), [], bV)
        nhalf = sbt(nc, st, "nhalfN", [128, 1], F32); bnhalf = Buf()
        k.op("gpsimd", lambda e: e.memset(nhalf[:], -0.5), [], [bnhalf])
        tz = [sbt(nc, st, "tz%d" % i, [128, 512], BF16) for i in range(2)]; btz = [Buf() for _ in range(2)]

        def rsqrt_mean(dst, srcap, rd, wr):
            k.op("gpsimd", lambda e: e.tensor_scalar(out=dst, in0=srcap, scalar1=1.0 / D, scalar2=EPS,
                                                      op0=ALU.mult, op1=ALU.add), rd, wr)
            k.op("gpsimd", lambda e: e.tensor_tensor(out=dst, in0=dst, in1=nhalf[:], op=ALU.pow),
                 wr + [bnhalf], wr)
        QT = [sbt(nc, st, "QT%d" % i, [128, 4, 512], BF16) for i in range(2)]; bQT = [Buf() for _ in range(2)]
        zn = [sbt(nc, st, "zn%d" % i, [128, 4, 512], BF16) for i in range(2)]; bzn = [Buf() for _ in range(2)]
        hnTg = [sbt(nc, st, "hnTg%d" % i, [128, 8, 512], BF16) for i in range(2)]
        bhn = [[Buf() for _ in range(4)] for _ in range(2)]
        xin = [sbt(nc, st, "xinN%d" % i, [128, 1024], F32) for i in range(2)]; bxin = [Buf() for _ in range(2)]
        xs = [sbt(nc, st, "xsN%d" % i, [128, 1024], BF16) for i in range(2)]; bxs = [Buf() for _ in range(2)]
        sst = sbt(nc, st, "sstN", [128, 40], F32); bss = [Buf() for _ in range(40)]
        rst = sbt(nc, st, "rstN", [128, 40], F32); brs = [Buf() for _ in range(40)]
        PT = [sbt(nc, st, "PT%d" % i, [128, 5, 128], BF16) for i in range(4)]; bPT = [Buf() for _ in range(4)]
        rden = [sbt(nc, st, "rden%d" % i, [128, 8], F32) for i in range(2)]; brden = [Buf() for _ in range(2)]
        wt = [sbt(nc, st, "wt%d" % i, [128, 4, 64], BF16) for i in range(2)]; bwt = [Buf() for _ in range(2)]
        mixt = [sbt(nc, st, "mixt%d" % i, [128, 1024], BF16) for i in range(2)]
        bmixA = [Buf() for _ in range(2)]; bmixB = [[Buf(), Buf()] for _ in range(2)]
        mixT = sbt(nc, st, "mixT", [128, 8, 128], BF16); bmixT = Buf()
        xr = [sbt(nc, st, "xr%d" % i, [128, 1024], F32) for i in range(2)]; bxr = [Buf() for _ in range(2)]
        ptl = [sbt(nc, st, "ptl%d" % i, [128, 256], F32) for i in range(2)]; bptl = [Buf() for _ in range(2)]
        pb16 = sbt(nc, st, "pb16", [128, 256], BF16); bpb16 = Buf()
        pT = sbt(nc, st, "pT", [128, 2, 128], BF16); bpT = Buf()
        h1 = [sbt(nc, st, "h1_%d" % i, [128, 1024], F32) for i in range(2)]; bh1 = [Buf() for _ in range(2)]
        h1b = [sbt(nc, st, "h1b%d" % i, [128, 1024], BF16) for i in range(2)]; bh1b = [Buf() for _ in range(2)]
        h1T = sbt(nc, st, "h1T", [128, 8, 128], BF16); bh1T = Buf()
        sg = [sbt(nc, st, "sg%d" % i, [128, 1024], BF16) for i in range(2)]; bsg = [Buf() for _ in range(2)]
        et = [sbt(nc, st, "et%d" % i, [128, 1024], F32) for i in range(2)]; bet = [Buf() for _ in range(2)]
        s2 = [sbt(nc, st, "s2_%d" % i, [128, 8], F32) for i in range(2)]; bs2 = [Buf() for _ in range(2)]

        def tile_ok(tt):
            return -2 <= tt <= 31

        def grp(tt):
            G = tt // 4
            return G, tt - 4 * G, (G + 1) % 3

        def pre(tt):
            if not tile_ok(tt):
                return
            tok0 = HALF + 128 * tt
            si = (tt + 2) % 40
            xb = (tt + 2) % 2
            k.dma("sync", lambda e: e.dma_start(out=xin[xb][:], in_=dr["x_all"][tok0:tok0 + 128, :]), writes=[bxin[xb]])

        def pre_act(tt):
            if not tile_ok(tt):
                return
            si = (tt + 2) % 40
            xb = (tt + 2) % 2
            ACT(xs[xb][:], xin[xb][:], AF.Square, [bxin[xb]], [bxs[xb], bss[si]], accum_out=sst[:, si:si + 1])
            rsqrt_mean(rst[:, si:si + 1], sst[:, si:si + 1], [bss[si]], [brs[si]])
            ACT(xs[xb][:], xin[xb][:], AF.Copy, [bxin[xb], brs[si]], [bxs[xb]], scale=rst[:, si:si + 1])

        def trn(tt):
            if not tile_ok(tt):
                return
            G, pos, sg3 = grp(tt)
            xb = (tt + 2) % 2
            hb = G % 2
            pb = nextps()
            pTt = PS[pb][:].bitcast(BF16).rearrange("p (a b) -> p a b", a=8)
            for kt in range(8):
                TR(pTt[:, kt, :], xs[xb][:, kt * 128:(kt + 1) * 128], [bxs[xb]], [bPS[pb]], kt == 7)
            TT("vector", hnTg[hb][:, :, pos * 128:(pos + 1) * 128], pTt,
               npre[:].unsqueeze(2).broadcast_to([128, 8, 128]), ALU.mult, [bPS[pb], bnpre], [bhn[hb][pos]])

        def mmv(tt):
            if not tile_ok(tt):
                return
            G, pos, sg3 = grp(tt)
            hb = G % 2
            qg = G % 2
            hT = hnTg[hb]
            pb = nextps()
            for kt in range(8):
                MM(PS[pb][:], hT[:, kt, pos * 128:(pos + 1) * 128], wq[:, kt, 1024:1536], kt == 0, kt == 7,
                   [bhn[hb][pos]] + bwq["v"], [bPS[pb]], kt == 7)
            CP("vector", V[:, sg3 * 4 + pos, :, 0:64], PS[pb][:].rearrange("p (h d) -> p h d", d=64),
               [bPS[pb]], [bV[sg3]])
            if G >= 0:
                pb = nextps()
                for kt in range(8):
                    MM(PS[pb][:], hT[:, kt, pos * 128:(pos + 1) * 128], wq[:, kt, 1536:2048], kt == 0, kt == 7,
                       [bhn[hb][pos]] + bwq["zn"], [bPS[pb]], kt == 7)
                tzi = tt % 2
                ACT(tz[tzi][:], PS[pb][:], AF.Tanh, [bPS[pb]], [btz[tzi]], scale=0.5)
                k.op("vector", lambda e, pb=pb: e.scalar_tensor_tensor(
                    out=zn[qg][:, pos, :], in0=tz[tzi][:], scalar=1.0, in1=PS[pb][:], op0=ALU.add, op1=ALU.mult),
                    [bPS[pb], btz[tzi]], [bzn[qg]])
            if pos == 3:
                poss = [0, 1, 2, 3] if G >= 0 else [2, 3]
                c0 = poss[0] * 128
                N = len(poss) * 128
                rdh = [bhn[hb][p_] for p_ in poss]
                if G >= 0:
                    for ct in range(4):
                        pb = nextps()
                        for kt in range(8):
                            MM(PS[pb][:], wq[:, kt, ct * 128:(ct + 1) * 128], hT[:, kt, :], kt == 0, kt == 7,
                               rdh + bwq["q"], [bPS[pb]], kt == 7)
                        ACT(QT[qg][:, ct, :], PS[pb][:], AF.Copy, [bPS[pb]], [bQT[qg]], scale=0.125)
                for ct in range(4):
                    pb = nextps()
                    for kt in range(8):
                        MM(PS[pb][:, 0:N], wq[:, kt, 512 + ct * 128:512 + (ct + 1) * 128], hT[:, kt, c0:c0 + N],
                           kt == 0, kt == 7, rdh + bwq["k"], [bPS[pb]], kt == 7)
                    CP("vector", KT[:, ct, sg3 * 512 + c0:sg3 * 512 + c0 + N], PS[pb][:, 0:N], [bPS[pb]], [bKT[sg3]])

        def key_tiles(R):
            tiles = [R - 2 + i for i in range(5)] if R <= 29 else [28, 29, 30, 31]
            info = []
            for tl in tiles:
                Gt, pt_, sgt = grp(tl)
                info.append((sgt, pt_))
            return info

        def attn_start(R):
            if not (0 <= R <= 31):
                return
            q = R % 2
            if R == 30:
                load_tab(1)
            if R == 31:
                load_tab(2)
            k.dma("sync", lambda e: e.dma_start(out=mixt[q][:, 0:512], in_=dr["mix_d"][R * 128:(R + 1) * 128, 0:512]),
                  writes=[bmixA[q]])
            k.dma("sync", lambda e: e.dma_start(out=xr[q][:], in_=dr["x_all"][HALF + R * 128:HALF + (R + 1) * 128, :]),
                  writes=[bxr[q]])
            k.dma("sync", lambda e: e.dma_start(out=ptl[q][:], in_=dr["p_own"][R * 128:(R + 1) * 128, :]),
                  writes=[bptl[q]])

        def qk(R, h):
            if not (0 <= R <= 31):
                return
            G, pos, _ = grp(R)
            qg = G % 2
            kinfo = key_tiles(R)
            nk = len(kinfo)
            hp = h // 2
            lo = 64 * (h % 2)
            hq = h % 4
            pbA = nextps()
            pbB = nextps() if nk == 5 else None
            for idx in range(nk):
                sgt, pt_ = kinfo[idx]
                if idx < 4:
                    tgt = PS[pbA][:, idx * 128:(idx + 1) * 128]; wr_ = [bPS[pbA]]
                else:
                    tgt = PS[pbB][:, 0:128]; wr_ = [bPS[pbB]]
                last = (idx == min(nk, 4) - 1) or idx == 4
                MM(tgt, KT[lo:lo + 64, hp, sgt * 512 + pt_ * 128:sgt * 512 + (pt_ + 1) * 128],
                   QT[qg][lo:lo + 64, hp, pos * 128:(pos + 1) * 128], True, True,
                   [bKT[sgt], bQT[qg]], wr_, last)
            n4 = min(nk, 4)
            ACT(PT[hq][:, 0:n4, :], PS[pbA][:, 0:n4 * 128].rearrange("p (a b) -> p a b", a=n4), AF.Exp,
                [bPS[pbA]], [bPT[hq]])
            if nk == 5:
                ACT(PT[hq][:, 4, :], PS[pbB][:, 0:128], AF.Exp, [bPS[pbB]], [bPT[hq]])
            TT("vector", PT[hq][:, 0:nk, :], PT[hq][:, 0:nk, :], tab[:, h * 5:h * 5 + nk, :], ALU.mult,
               [bPT[hq], btab], [bPT[hq]])

        def pv(R, h):
            if not (0 <= R <= 31):
                return
            kinfo = key_tiles(R)
            nk = len(kinfo)
            hq = h % 4
            ob = 6 + h // 4
            for idx in range(nk):
                sgt, pt_ = kinfo[idx]
                MM(PS[ob][:, (h % 4) * 65:(h % 4) * 65 + 65], PT[hq][:, idx, :], V[:, sgt * 4 + pt_, h, :],
                   idx == 0, idx == nk - 1, [bPT[hq], bV[sgt]], [bPS[ob]], (idx == nk - 1))

        def norm(R, quad):
            if not (0 <= R <= 31):
                return
            G, pos, _ = grp(R)
            qg = G % 2
            q = R % 2
            ob = 6 + quad
            pv_ = PS[ob][:, 0:260].rearrange("p (h e) -> p h e", e=65)
            k.op("vector", lambda e: e.reciprocal(out=rden[q][:, quad * 4:(quad + 1) * 4], in_=pv_[:, :, 64]),
                 [bPS[ob]], [brden[q]])
            TT("vector", wt[quad][:], zn[qg][:, pos, quad * 256:(quad + 1) * 256].rearrange("p (h d) -> p h d", d=64),
               rden[q][:, quad * 4:(quad + 1) * 4].unsqueeze(2).broadcast_to([128, 4, 64]), ALU.mult,
               [bzn[qg], brden[q]], [bwt[quad]])
            TT("vector", mixt[q][:, 512 + quad * 256:512 + (quad + 1) * 256].rearrange("p (h d) -> p h d", d=64),
               pv_[:, :, 0:64], wt[quad][:], ALU.mult, [bPS[ob], bwt[quad]], [bmixB[q][quad]])

        def pcopy(R):
            if not (0 <= R <= 31):
                return
            CP("vector", pb16[:], ptl[R % 2][:], [bptl[R % 2]], [bpb16])

        def tail1a(R):
            if not (0 <= R <= 31):
                return
            q = R % 2
            pb = nextps()
            pTt = PS[pb][:].bitcast(BF16).rearrange("p (a b) -> p a b", a=8)
            for kt in range(8):
                TR(pTt[:, kt, :], mixt[q][:, kt * 128:(kt + 1) * 128], [bmixA[q]] + bmixB[q], [bPS[pb]], kt == 7)
            ACT(mixT[:], pTt, AF.Copy, [bPS[pb]], [bmixT])
            pb = nextps()
            pTt = PS[pb][:].bitcast(BF16).rearrange("p (a b) -> p a b", a=8)
            for kt in range(2):
                TR(pTt[:, kt, :], pb16[:, kt * 128:(kt + 1) * 128], [bpb16], [bPS[pb]], kt == 1)
            CP("vector", pT[:], pTt[:, 0:2, :], [bPS[pb]], [bpT])

        def tail1b(R):
            if not (0 <= R <= 31):
                return
            q = R % 2
            for half in range(2):
                hs = slice(half * 512, (half + 1) * 512)
                pb = nextps()
                for kt in range(8):
                    MM(PS[pb][:], mixT[:, kt, :], wout[:, kt, hs], kt == 0, kt == 7, [bmixT] + bwout, [bPS[pb]], kt == 7)
                ACT(h1[q][:, hs], PS[pb][:], AF.Copy, [bPS[pb]], [bh1[q]])
                ACT(h1b[q][:, hs], h1[q][:, hs], AF.Square, [bh1[q]], [bh1b[q], bs2[q]], accum_out=s2[q][:, half:half + 1])
            for half in range(2):
                hs = slice(half * 512, (half + 1) * 512)
                pb = nextps()
                for kt in range(2):
                    MM(PS[pb][:], pT[:, kt, :], wple[:, kt, hs], kt == 0, kt == 1, [bpT] + bwple, [bPS[pb]], kt == 1)
                ACT(et[q][:, hs], PS[pb][:], AF.Copy, [bPS[pb]], [bet[q]])
                ACT(h1b[q][:, hs], et[q][:, hs], AF.Square, [bet[q]], [bh1b[q], bs2[q]],
                    accum_out=s2[q][:, 4 + half:5 + half])
            TT("vector", s2[q][:, 2:3], s2[q][:, 0:1], s2[q][:, 1:2], ALU.add, [bs2[q]], [bs2[q]])
            rsqrt_mean(s2[q][:, 3:4], s2[q][:, 2:3], [bs2[q]], [bs2[q]])
            for half in range(2):
                hs = slice(half * 512, (half + 1) * 512)
                k.op("vector", lambda e, hs=hs: e.scalar_tensor_tensor(
                    out=h1[q][:, hs], in0=h1[q][:, hs], scalar=s2[q][:, 3:4], in1=npost[:, hs],
                    op0=ALU.mult, op1=ALU.mult), [bh1[q], bs2[q], bnpost], [bh1[q]])
            TT("vector", h1[q][:], h1[q][:], xr[q][:], ALU.add, [bh1[q], bxr[q]], [bh1[q]])
            ACT(h1b[q][:], h1[q][:], AF.Copy, [bh1[q]], [bh1b[q]])
            TT("vector", s2[q][:, 6:7], s2[q][:, 4:5], s2[q][:, 5:6], ALU.add, [bs2[q]], [bs2[q]])
            rsqrt_mean(s2[q][:, 7:8], s2[q][:, 6:7], [bs2[q]], [bs2[q]])
            for half in range(2):
                hs = slice(half * 512, (half + 1) * 512)
                k.op("vector", lambda e, hs=hs: e.scalar_tensor_tensor(
                    out=et[q][:, hs], in0=et[q][:, hs], scalar=s2[q][:, 7:8], in1=plen[:, hs],
                    op0=ALU.mult, op1=ALU.mult), [bet[q], bs2[q], bplen], [bet[q]])

        def tail2a(R):
            if not (0 <= R <= 31):
                return
            q = R % 2
            pb = nextps()
            pTt = PS[pb][:].bitcast(BF16).rearrange("p (a b) -> p a b", a=8)
            for kt in range(8):
                TR(pTt[:, kt, :], h1b[q][:, kt * 128:(kt + 1) * 128], [bh1b[q]], [bPS[pb]], kt == 7)
            CP("vector", h1T[:], pTt, [bPS[pb]], [bh1T])

        def tail2b(R):
            if not (0 <= R <= 31):
                return
            q = R % 2
            for half in range(2):
                hs = slice(half * 512, (half + 1) * 512)
                pb = nextps()
                for kt in range(8):
                    MM(PS[pb][:], h1T[:, kt, :], wgate[:, kt, hs], kt == 0, kt == 7, [bh1T] + bwgate, [bPS[pb]], kt == 7)
                ACT(sg[q][:, hs], PS[pb][:], AF.Tanh, [bPS[pb]], [bsg[q]], scale=0.5)
            k.op("vector", lambda e: e.scalar_tensor_tensor(
                out=et[q][:], in0=sg[q][:], scalar=1.0, in1=et[q][:], op0=ALU.add, op1=ALU.mult),
                [bet[q], bsg[q]], [bet[q]])
            k.op("vector", lambda e: e.scalar_tensor_tensor(
                out=et[q][:], in0=et[q][:], scalar=0.5, in1=h1[q][:], op0=ALU.mult, op1=ALU.add),
                [bet[q], bh1[q]], [bet[q]])
            tok = k.dma("gpsimd", lambda e: e.dma_start(out=dr["out"][R * 128:(R + 1) * 128, :], in_=et[q][:]),
                        reads=[bet[q]])
            out_toks.append(tok)

        def slot(i, h):
            qk(i, h)
            Hh = 8 * i + h - 3
            Ri, hh = Hh // 8, Hh % 8
            pv(Ri, hh)
            if hh == 3:
                norm(Ri, 0)
            if hh == 7:
                norm(Ri, 1)

        for i in range(-10, 35):
            pre(i + 8)
            attn_start(i)
            pcopy(i - 1)
            slot(i, 0); slot(i, 1)
            tail2a(i - 2)
            slot(i, 2); slot(i, 3)
            pre_act(i + 8)
            trn(i + 7)
            slot(i, 4)
            tail1a(i - 1)
            slot(i, 5)
            tail2b(i - 2)
            slot(i, 6)
            tail1b(i - 1)
            slot(i, 7)
            mmv(i + 6)
        for tok in out_toks[-4:]:
            k.wait_tok("gpsimd", tok)
        k.emit()


def build(stage=99):
    nc = bass.Bass("TRN2", target_bir_lowering=False)
    dr = {}

    def din(name, shape, dt=F32):
        dr[name] = nc.dram_tensor(name, list(shape), dt, kind="ExternalInput").ap()

    def dout(name, shape, dt=F32):
        dr[name] = nc.dram_tensor(name, list(shape), dt, kind="ExternalOutput").ap()

    def dint(name, shape, dt):
        dr[name] = nc.dram_tensor(name, list(shape), dt, kind="Internal").ap()

    din("x_all", [NTOK, D])
    din("w_in", [D, 3072])
    din("npre", [128, 8])
    din("ident", [128, 128])
    dint("zs_d", [4, 128, 8 * 512], BF16)
    din("ssm_par", [2, 128, 2144])
    din("ssm_c", [128, NK + 32 + 1 + 4 * 128])
    if stage == 2:
        dout("dbg_y", [128, 32 * 512])
    din("w_glu", [512, 512])
    din("b_glu", [1, 512])
    dint("mix_d", [HALF, D], BF16)
    if stage == 3:
        dout("dbg_ms", [128, 8 * 512], BF16)
    din("w_out", [D, D])
    din("w_gate", [D, D])
    din("w_ple", [256, D])
    din("npost_b", [128, D])
    din("plen_b", [128, D])
    din("na_tab", [3, 8, 5, 128, 128])
    din("p_own", [HALF, 256])
    if stage >= 4:
        dout("out", [HALF, D])
    if stage == 1:
        dout("dbg_u", [128, 32 * 1024])
        dout("dbg_zs", [128, 8 * 512])

    with ExitStack() as st0:
        k = K(nc, st0)
        PS = [st0.enter_context(nc.psum_tensor("ps%d" % i, [128, 512], F32)) for i in range(8)]
        bPS = [Buf("ps%d" % i) for i in range(8)]
        ident = sbt(nc, st0, "ident", [128, 128], BF16)
        bident = Buf("ident")
        k.dma("gpsimd", lambda e: e.dma_start(out=ident[:], in_=dr["ident"]), writes=[bident])
        stU = ExitStack()
        U = sbt(nc, stU, "U", [128, 32, 1024], BF16)
        bUblk = [Buf("U%d" % i) for i in range(8)]

        with ExitStack() as st:
          if stage != 5:
              npre = sbt(nc, st, "npre", [128, 8], F32); bnpre = Buf()
              k.dma("sync", lambda e: e.dma_start(out=npre[:], in_=dr["npre"]), writes=[bnpre])
              wu = sbt(nc, st, "wu", [128, 8, 1024], BF16); bwu = Buf(); bwul = [Buf() for _ in range(8)]
              wsrc = dr["w_in"].rearrange("(kt f) c -> f kt c", f=128)
              for kt in range(8):
                  k.dma("gpsimd", lambda e, kt=kt: e.dma_start(out=wu[:, kt, :], in_=wsrc[:, kt, 0:1024]),
                        writes=[bwul[kt]], owner=bwu)
              xin = [sbt(nc, st, "xin%d" % i, [128, 2, D], F32) for i in range(2)]
              bxin = [Buf() for _ in range(2)]
              junk = sbt(nc, st, "junk", [128, D], BF16); bjunk = Buf()
              xs = [sbt(nc, st, "xs%d" % i, [128, D], BF16) for i in range(2)]
              bxs = [Buf() for _ in range(2)]
              ss = sbt(nc, st, "ss", [128, 64], F32)
              bss = [Buf() for _ in range(64)]
              rs = sbt(nc, st, "rs", [128, 64], F32)
              brs = [Buf() for _ in range(64)]
              hnT = [sbt(nc, st, "hnT%d" % i, [128, 8, 8, 128], BF16) for i in range(2)]
              bhn = [[Buf() for _ in range(8)] for _ in range(2)]
              Tt = [sbt(nc, st, "Tt%d" % i, [128, 32, 8, 16], BF16) for i in range(2)]
              bTt = [[Buf() for _ in range(8)] for _ in range(2)]
              zst = [sbt(nc, st, "zst%d" % i, [128, 8, 512], BF16) for i in range(2)]
              bzs = [Buf(), Buf()]
              nhalf = sbt(nc, st, "nhalfA", [128, 1], F32); bnhalf = Buf()
              k.op("gpsimd", lambda e: e.memset(nhalf[:], -0.5), [], [bnhalf])
              tza = [sbt(nc, st, "tza%d" % i, [128, 512], BF16) for i in range(2)]
              btza = [Buf() for _ in range(2)]
              xv = dr["x_all"].rearrange("(blk c j) f -> blk c j f", c=128, j=8)
              pi = [0]

              def nps():
                  pb = pi[0] % 8
                  pi[0] += 1
                  return pb

              def a_pre(sl):
                  blk, j = sl // 8, sl % 8
                  xb = (sl // 2) % 2
                  if j % 2 == 0:
                      k.dma("sync", lambda e: e.dma_start(out=xin[xb][:], in_=xv[blk, :, j:j + 2, :]), writes=[bxin[xb]])
                  xt = xin[xb][:, j % 2, :]
                  sx = sl % 2
                  k.op("scalar", lambda e: e.activation(
                      out=xs[sx][:], in_=xt, func=AF.Square, accum_out=ss[:, sl:sl + 1]),
                      reads=[bxin[xb]], writes=[bxs[sx], bss[sl]])
                  k.op("gpsimd", lambda e: e.tensor_scalar(
                      out=rs[:, sl:sl + 1], in0=ss[:, sl:sl + 1], scalar1=1.0 / D, scalar2=EPS,
                      op0=ALU.mult, op1=ALU.add), reads=[bss[sl]], writes=[brs[sl]])
                  k.op("gpsimd", lambda e: e.tensor_tensor(
                      out=rs[:, sl:sl + 1], in0=rs[:, sl:sl + 1], in1=nhalf[:], op=ALU.pow),
                      reads=[brs[sl], bnhalf], writes=[brs[sl]])
                  k.op("scalar", lambda e: e.activation(
                      out=xs[sx][:], in_=xt, func=AF.Copy, scale=rs[:, sl:sl + 1]),
                      reads=[bxin[xb], brs[sl]], writes=[bxs[sx]])

              def a_trn(sl):
                  blk, j = sl // 8, sl % 8
                  hb = blk % 2
                  sx = sl % 2
                  pb = nps()
                  pT = PS[pb][:].bitcast(BF16).rearrange("p (a b) -> p a b", a=8)
                  for kt in range(8):
                      k.op("tensor", lambda e, kt=kt: e.transpose(
                          out=pT[:, kt, :], in_=xs[sx][:, kt * 128:(kt + 1) * 128], identity=ident[:]),
                          reads=[bxs[sx], bident], writes=[bPS[pb]], sig=(kt == 7))
                  k.op("vector", lambda e: e.tensor_tensor(
                      out=hnT[hb][:, :, j, :], in0=pT, in1=npre[:].unsqueeze(2).broadcast_to([128, 8, 128]),
                      op=ALU.mult), reads=[bPS[pb], bnpre], writes=[bhn[hb][j]])

              def a_mm(sl):
                  blk, j = sl // 8, sl % 8
                  hb = blk % 2
                  own = blk >= 4
                  pb = nps()
                  for kt in range(8):
                      k.op("tensor", lambda e, kt=kt: e.matmul(
                          PS[pb][:], lhsT=hnT[hb][:, kt, j, :], rhs=wu[:, kt, 0:512],
                          start=(kt == 0), stop=(kt == 7)),
                          reads=[bhn[hb][j]] + bwul, writes=[bPS[pb]], sig=(kt == 7))
                  k.op("vector", lambda e: e.tensor_copy(
                      out=Tt[hb][:, :, j, :], in_=PS[pb][:].rearrange("p (g h) -> p g h", h=16)),
                      reads=[bPS[pb]], writes=[bTt[hb][j]])
                  if own:
                      pb2 = nps()
                      for kt in range(8):
                          k.op("tensor", lambda e, kt=kt: e.matmul(
                              PS[pb2][:], lhsT=hnT[hb][:, kt, j, :], rhs=wu[:, kt, 512:1024],
                              start=(kt == 0), stop=(kt == 7)),
                              reads=[bhn[hb][j]] + bwul, writes=[bPS[pb2]], sig=(kt == 7))
                      k.op("scalar", lambda e: e.activation(
                          out=tza[j % 2][:], in_=PS[pb2][:], func=AF.Tanh, scale=0.5),
                          reads=[bPS[pb2]], writes=[btza[j % 2]])
                      k.op("vector", lambda e: e.scalar_tensor_tensor(
                          out=zst[hb][:, j, :], in0=tza[j % 2][:], scalar=1.0, in1=PS[pb2][:],
                          op0=ALU.add, op1=ALU.mult), reads=[bPS[pb2], btza[j % 2]], writes=[bzs[hb]])

              def a_fin(blk):
                  hb = blk % 2
                  if blk >= 4:
                      k.dma("sync", lambda e: e.dma_start(
                          out=dr["zs_d"][blk - 4], in_=zst[hb][:].rearrange("p a b -> p (a b)")), reads=[bzs[hb]])
                  for g4 in range(4):
                      pb = nps()
                      pT = PS[pb][:].bitcast(BF16).rearrange("p (a b) -> p a b", a=8)
                      for gg in range(8):
                          g = g4 * 8 + gg
                          k.op("tensor", lambda e, gg=gg, g=g, pT=pT: e.transpose(
                              out=pT[:, gg, :], in_=Tt[hb][:, g, :, :].rearrange("p a b -> p (a b)"), identity=ident[:]),
                              reads=bTt[hb] + [bident], writes=[bPS[pb]], sig=(gg == 7))
                      hf, b4 = blk // 4, blk % 4
                      uo = U[:, g4 * 8:(g4 + 1) * 8, hf * 512:(hf + 1) * 512].rearrange(
                          "p g (j m) -> p g j m", j=8)[:, :, :, b4 * 16:(b4 + 1) * 16]
                      k.op("vector", lambda e, uo=uo, pT=pT: e.tensor_copy(
                          out=uo, in_=pT.rearrange("p g (m j) -> p g j m", j=8)),
                          reads=[bPS[pb]], writes=[bUblk[blk]])

              a_pre(0)
              for s_ in range(72):
                  if s_ + 1 < 64:
                      a_pre(s_ + 1)
                  if s_ < 64:
                      a_trn(s_)
                  if s_ >= 8:
                      a_mm(s_ - 8)
                      if (s_ - 8) % 8 == 7:
                          a_fin((s_ - 8) // 8)
              if stage == 1:
                  t1 = k.dma("gpsimd", lambda e: e.dma_start(
                      out=dr["dbg_u"], in_=U[:].rearrange("p a b -> p (a b)")), reads=bUblk)
                  t2 = k.dma("gpsimd", lambda e: e.dma_start(
                      out=dr["dbg_zs"], in_=zst[1][:].rearrange("p a b -> p (a b)")), reads=[bzs[1]])
                  k.wait_tok("gpsimd", t1)
                  k.wait_tok("gpsimd", t2)
              k.emit()
        if stage >= 2 and stage != 5:
            ssm_phase(nc, k, st0, dr, PS, bPS, ident, bident, U, bUblk, stage)
        if stage >= 3 and stage != 5:
            pass_S(nc, k, dr, PS, bPS, ident, bident, U, bUblk, stage)
        stU.close()
        if stage >= 4:
            stW = ExitStack()
            W = prefetch_N(nc, k, dr, stW)
            pass_N(nc, k, dr, PS, bPS, ident, bident, stage, W)
            stW.close()
    return nc


def core_inputs(inp, b, s):
    x = inp["x"][b]
    if s == 1:
        x_all = x
    else:
        x_all = x[::-1]
    d = {}
    d["x_all"] = np.ascontiguousarray(x_all, dtype=np.float32)
    d["w_in"] = np.ascontiguousarray(inp["w_in"][0], dtype=np.float32)
    d["npre"] = np.ascontiguousarray(inp["norm_pre"][0].reshape(8, 128).T, dtype=np.float32)
    d["ident"] = np.eye(128, dtype=np.float32)
    par = np.zeros((2, 128, 2144), np.float32)
    for dd in range(2):
        sd = dd if s == 1 else 1 - dd
        a_re = inp["ssm_a_re"][0, sd]; a_im = inp["ssm_a_im"][0, sd]
        ldt = inp["ssm_log_dt"][0, sd]
        b_re = inp["ssm_b_re"][0, sd]; b_im = inp["ssm_b_im"][0, sd]
        c_re = inp["ssm_c_re"][0, sd]; c_im = inp["ssm_c_im"][0, sd]
        blk = np.concatenate([
            a_re.T, a_im.T, np.broadcast_to(ldt[None, :], (64, 32)),
            b_re.transpose(1, 0, 2).reshape(64, 512), b_im.transpose(1, 0, 2).reshape(64, 512),
            c_re.transpose(2, 0, 1).reshape(64, 512), c_im.transpose(2, 0, 1).reshape(64, 512)], axis=1)
        par[dd, 0:64] = blk
        par[dd, 64:128] = blk
    d["ssm_par"] = par
    cst = np.zeros((128, NK + 33 + 512), np.float32)
    cst[:, 0:NK] = np.asarray(KVALS, np.float32)[None, :]
    cst[:, NK:NK + 32] = np.tile(inp["ssm_d"][0].T, (8, 1))
    cst[0:64, NK + 32] = 1.0
    cst[64:128, NK + 32] = -1.0
    c0 = NK + 33
    r = np.arange(128)
    cst[:, c0:c0 + 128] = (r[:, None] % 64 == r[None, :] % 64)
    cst[:, c0 + 128:c0 + 256] = (r[None, :] // 16 >= r[:, None] // 16)
    cst[:, c0 + 256:c0 + 384] = (r[:, None] // 16 >= r[None, :] // 16)
    cst[:, c0 + 384:c0 + 512] = np.eye(128)
    d["ssm_c"] = cst
    d["w_glu"] = np.ascontiguousarray(inp["w_glu"][0], dtype=np.float32)
    d["b_glu"] = np.ascontiguousarray(inp["b_glu"][0][None, :], dtype=np.float32)
    d["w_out"] = np.ascontiguousarray(inp["w_out"][0], dtype=np.float32)
    d["w_gate"] = np.ascontiguousarray(inp["w_ple_gate"][0], dtype=np.float32)
    d["w_ple"] = np.ascontiguousarray(inp["w_ple"][0], dtype=np.float32)
    d["npost_b"] = np.ascontiguousarray(np.broadcast_to(inp["norm_post"][0][None, :], (128, D)), dtype=np.float32)
    d["plen_b"] = np.ascontiguousarray(np.broadcast_to(inp["ple_norm"][0][None, :], (128, D)), dtype=np.float32)
    p = inp["p"][0, b]
    p_own = p[HALF:] if s == 1 else p[:HALF][::-1]
    d["p_own"] = np.ascontiguousarray(p_own, dtype=np.float32)
    d["na_tab"] = bias_tables(inp["na_rpb"][0], s)
    return d


def bias_tables(rpb, s):
    NEG = np.float32(-30000.0)
    tab = np.full((3, 8, 5, 128, 128), NEG, np.float32)
    kb = (np.arange(128) // 64)[:, None]
    kc = (np.arange(128) % 64)[:, None]
    qa = (np.arange(128) // 64)[None, :]
    qc = (np.arange(128) % 64)[None, :]
    for kind, R in enumerate([10, 30, 31]):
        tiles = [R - 2 + i for i in range(5)] if R <= 29 else [28, 29, 30, 31]
        for idx, tl in enumerate(tiles):
            kl = 2 * tl + kb
            ql = 2 * R + qa
            if s == 1:
                rk, rq, ck, cq = 64 + kl, 64 + ql, kc, qc
            else:
                rk, rq, ck, cq = 63 - kl, 63 - ql, 63 - kc, 63 - qc
            rs = np.clip(rq - 4, 0, 120)
            cs = np.clip(cq - 8, 0, 48)
            valid = (rk >= rs) & (rk < rs + 8) & (ck >= cs) & (ck < cs + 16)
            dr_ = np.clip(rk - rq + 7, 0, 14)
            dc_ = np.clip(ck - cq + 15, 0, 30)
            vals = rpb[:, dr_, dc_]
            tab[kind, :, idx] = np.where(valid[None], vals, NEG)
    return tab


_NC_CACHE = {}


def kernel(**inputs):
    inp = {k_: np.asarray(v) for k_, v in inputs.items()}
    if "nc" not in _NC_CACHE:
        _NC_CACHE["nc"] = build(stage=4)
    nc = _NC_CACHE["nc"]
    in_maps = []
    for c in range(8):
        in_maps.append(core_inputs(inp, c // 2, c % 2))
    res = run_bass_kernel_spmd(nc, in_maps, core_ids=list(range(8)))
    out = np.zeros((4, NTOK, D), np.float32)
    for c in range(8):
        b, s = c // 2, c % 2
        o = res.results[c]["out"]
        if s == 1:
            out[b, HALF:] = o
        else:
            out[b, :HALF] = o[::-1]
    return out
```

```python
import numpy as np
from contextlib import ExitStack
import concourse.bass as bass
import concourse.mybir as mybir
from concourse.bass_utils import run_bass_kernel_spmd

F32 = mybir.dt.float32
BF16 = mybir.dt.bfloat16
I32 = mybir.dt.int32
AF = mybir.ActivationFunctionType
ALU = mybir.AluOpType

NTOK = 8192
HALF = 4096
D = 1024
EPS = 1e-6
TWO_PI = 6.283185307179586

KSEG = {
    "A": [7 - i for i in range(8)],
    "B": [i for i in range(8)],
    "C": [i + 1 for i in range(8)],
    "D": [8 - i for i in range(8)],
    "E": [-i for i in range(8)],
    "G": [8 * (i + 1) for i in range(7)],
    "H": [64],
}
KOFF = {}
KVALS = []
for _n in "ABCDEGH":
    KOFF[_n] = len(KVALS)
    KVALS += KSEG[_n]
NK = len(KVALS)


class Buf:
    def __init__(self, name=""):
        self.name = name
        self.w = None
        self.r = {}
        self.dsem = None
        self.dcount = 0


class K:
    ENG = ["tensor", "vector", "scalar", "gpsimd", "sync"]

    def __init__(self, nc, stack):
        self.nc = nc
        self.stack = stack
        self.ops = {e: [] for e in self.ENG}
        self.cnt = {e: 0 for e in self.ENG}
        self.waited = {e: {} for e in self.ENG}
        self.sems = {}
        for e in self.ENG:
            self.sems[e] = stack.enter_context(nc.semaphore("s_" + e))
        self.nd = 0
        self.dlast = {}

    def _wait(self, eng, tok):
        if tok is None:
            return
        key, val = tok
        if key == eng and val > self.cnt[eng]:
            return
        if self.waited[eng].get(key, 0) >= val:
            return
        self.waited[eng][key] = val
        self.ops[eng].append(("w", key, val))

    def _deps(self, eng, reads, writes, same_eng_war=False):
        need = {}

        def add(tok):
            if tok is not None and need.get(tok[0], 0) < tok[1]:
                need[tok[0]] = tok[1]
        for b in reads:
            add(b.w)
        for b in writes:
            add(b.w)
            for key, val in b.r.items():
                add((key, val))
        for key, val in need.items():
            self._wait(eng, (key, val))

    def op(self, eng, fn, reads=(), writes=(), sig=True):
        self._deps(eng, reads, writes)
        if sig:
            self.cnt[eng] += 1
            tok = (eng, self.cnt[eng])
            self.ops[eng].append(("o", fn, eng, 1))
            for b in reads:
                b.r[eng] = tok[1]
            for b in writes:
                b.w = tok
                b.r = {}
        else:
            self.ops[eng].append(("o", fn, None, 0))
            nxt = self.cnt[eng] + 1
            for b in reads:
                b.r[eng] = nxt
            for b in writes:
                b.w = (eng, nxt)
                b.r = {}

    def dma(self, eng, fn, reads=(), writes=(), owner=None):
        self._deps(eng, reads, writes, same_eng_war=True)
        if owner is None:
            owner = (list(writes) + list(reads))[0]
        if owner.dsem is None:
            owner.dsem = "d%d" % self.nd
            self.nd += 1
            self.sems[owner.dsem] = self.stack.enter_context(self.nc.semaphore(owner.dsem))
        owner.dcount += 16
        tok = (owner.dsem, owner.dcount)
        self.dlast[owner.dsem] = owner.dcount
        self.ops[eng].append(("o", fn, owner.dsem, 16))
        for b in reads:
            b.r[tok[0]] = tok[1]
        for b in writes:
            b.w = tok
            b.r = {}
        return tok

    def wait_tok(self, eng, tok):
        self._wait(eng, tok)

    def check_deadlock(self):
        if not hasattr(self, "semval"):
            self.semval = {}
        pos = {e: 0 for e in self.ENG}
        progress = True
        while progress:
            progress = False
            for e in self.ENG:
                lst = self.ops[e]
                while pos[e] < len(lst):
                    it = lst[pos[e]]
                    if it[0] == "w":
                        if self.semval.get(it[1], 0) >= it[2]:
                            pos[e] += 1
                            progress = True
                        else:
                            break
                    else:
                        if it[2] is not None:
                            self.semval[it[2]] = self.semval.get(it[2], 0) + it[3]
                        pos[e] += 1
                        progress = True
        for e in self.ENG:
            if pos[e] < len(self.ops[e]):
                it = self.ops[e][pos[e]]
                raise RuntimeError("deadlock: engine %s stuck at item %d/%d waiting %s>=%s (have %s)" % (
                    e, pos[e], len(self.ops[e]), it[1], it[2], self.semval.get(it[1], 0)))

    def emit(self):
        nc = self.nc
        for key, val in self.dlast.items():
            self._wait("sync", (key, val))
        self.check_deadlock()
        with nc.Block() as block:
            for e in self.ENG:
                lst = self.ops[e]
                if not lst:
                    continue

                def body(engine, lst=lst):
                    for it in lst:
                        if it[0] == "w":
                            engine.wait_ge(self.sems[it[1]], it[2])
                        else:
                            ins = it[1](engine)
                            if it[2] is not None:
                                ins.then_inc(self.sems[it[2]], it[3])
                getattr(block, e)(body)
        self.ops = {e: [] for e in self.ENG}


class Ctx:
    pass


def sbt(nc, st, name, shape, dt):
    return st.enter_context(nc.sbuf_tensor("sb_" + name, list(shape), dt))


def ssm_phase(nc, k, st0, dr, PS, bPS, ident, bident, U, bUblk, stage):
    def TT(eng, out, a, b, op, rd, wr):
        k.op(eng, lambda e: e.tensor_tensor(out=out, in0=a, in1=b, op=op), rd, wr)

    def TS(eng, out, a, s1, op0, rd, wr, s2=None, op1=None):
        if op1 is None:
            k.op(eng, lambda e: e.tensor_scalar(out=out, in0=a, scalar1=s1, scalar2=None, op0=op0), rd, wr)
        else:
            k.op(eng, lambda e: e.tensor_scalar(out=out, in0=a, scalar1=s1, scalar2=s2, op0=op0, op1=op1), rd, wr)

    def ACT(out, in_, func, rd, wr, **kw):
        k.op("scalar", lambda e: e.activation(out=out, in_=in_, func=func, **kw), rd, wr)

    def CP(eng, out, in_, rd, wr):
        k.op(eng, lambda e: e.tensor_copy(out=out, in_=in_), rd, wr)

    def MM(out, lhsT, rhs, start, stop, rd, wr, sig):
        k.op("tensor", lambda e: e.matmul(out, lhsT=lhsT, rhs=rhs, start=start, stop=stop), rd, wr, sig=sig)

    pi = [0]

    def nextps():
        pb = pi[0] % 8
        pi[0] += 1
        return pb

    with ExitStack() as stS:
        cst = sbt(nc, stS, "ssmc", [128, NK + 33 + 512], F32); bcst = Buf()
        k.dma("sync", lambda e: e.dma_start(out=cst[:], in_=dr["ssm_c"]), writes=[bcst])
        kv = cst[:, 0:NK]
        dvec = cst[:, NK:NK + 32]
        sgn = cst[:, NK + 32:NK + 33]
        c0 = NK + 33
        IIf = cst[:, c0:c0 + 128]
        maskF = cst[:, c0 + 128:c0 + 256]
        maskB = cst[:, c0 + 256:c0 + 384]
        identF = cst[:, c0 + 384:c0 + 512]
        II = sbt(nc, stS, "II", [128, 128], BF16); bII = Buf()
        CP("vector", II[:], IIf, [bcst], [bII])
        identJ = sbt(nc, stS, "identJ", [128, 128], BF16); bidJ = Buf()
        TS("vector", identJ[:], identF, sgn, ALU.mult, [bcst], [bidJ])
        sc = sbt(nc, stS, "sc", [128, 4, 7, 64], F32); bsc = Buf()
        AA = sbt(nc, stS, "AA", [64, 2, 2, 32], F32); bAA = Buf()
        AB = sbt(nc, stS, "AB", [64, 2, 2, 32], F32); bAB = Buf()
        T8 = sbt(nc, stS, "T8", [128, 32, 128], BF16); bT8 = [Buf() for _ in range(32)]
        BfS = sbt(nc, stS, "BfS", [128, 64, 128], BF16); bBfS = [Buf() for _ in range(2)]
        CfS = sbt(nc, stS, "CfS", [128, 64, 128], BF16); bCfS = [Buf() for _ in range(2)]

        with ExitStack() as st:
            par = sbt(nc, st, "par", [128, 2144], F32); bpar = Buf()
            npi = sbt(nc, st, "npi", [128, 1], F32); bnpi = Buf()
            k.op("vector", lambda e: e.memset(npi[:], -3.141592653589793), [], [bnpi])
            NKG = NK * 32
            tl = {}
            for nm in ["KLre", "KLim", "y", "yf", "Wre", "Wim"]:
                tl[nm] = (sbt(nc, st, nm, [128, NK, 32], F32), Buf())
            yi = sbt(nc, st, "yi", [128, NK, 32], I32); byi = Buf()
            sm = {}
            for nm in ["dt", "lre", "lim", "nre", "den", "t1", "t2", "cre", "cim"]:
                sm[nm] = (sbt(nc, st, "s_" + nm, [128, 32], F32), Buf())
            bbre = sbt(nc, st, "bbre", [128, 32, 16], F32); bbbre = Buf()
            bbim = sbt(nc, st, "bbim", [128, 32, 16], F32); bbbim = Buf()
            tb = sbt(nc, st, "tb", [128, 32, 16], F32); btb = Buf()
            P1 = sbt(nc, st, "P1", [128, 32, 16], F32); bP1 = Buf()
            P2 = sbt(nc, st, "P2", [128, 32, 16], F32); bP2 = Buf()
            P1c = sbt(nc, st, "P1c", [128, 32, 16], F32); bP1c = Buf()
            P2c = sbt(nc, st, "P2c", [128, 32, 16], F32); bP2c = Buf()
            m1 = sbt(nc, st, "m1", [128, 8, 8, 16], F32); bm1 = Buf()
            m2 = sbt(nc, st, "m2", [128, 8, 8, 16], F32); bm2 = Buf()
            BnS = sbt(nc, st, "BnS", [128, 32, 128], BF16); bBnS = Buf()
            CpS = sbt(nc, st, "CpS", [128, 32, 128], BF16); bCpS = Buf()
            tmpm = [sbt(nc, st, "tmpm%d" % i, [128, 128], F32) for i in range(2)]
            btmpm = [Buf() for _ in range(2)]
            for d in range(2):
                k.dma("sync", lambda e, d=d: e.dma_start(out=par[:], in_=dr["ssm_par"][d]), writes=[bpar])
                ar = par[:, 0:32]; ai = par[:, 32:64]; ldt = par[:, 64:96]
                br = par[:, 96:608].rearrange("p (g h) -> p g h", h=16)
                bi = par[:, 608:1120].rearrange("p (g h) -> p g h", h=16)
                cr = par[:, 1120:1632].rearrange("p (g h) -> p g h", h=16)
                ci = par[:, 1632:2144].rearrange("p (g h) -> p g h", h=16)
                dt, bdt = sm["dt"]; lre, blre = sm["lre"]; lim, blim = sm["lim"]
                ACT(dt[:], ldt, AF.Exp, [bpar], [bdt])
                TT("vector", lre[:], dt[:], ar, ALU.mult, [bdt, bpar], [blre])
                TT("vector", lim[:], dt[:], ai, ALU.mult, [bdt, bpar], [blim])
                KLre, bKLre = tl["KLre"]; KLim, bKLim = tl["KLim"]
                kvb = kv.unsqueeze(2).broadcast_to([128, NK, 32])
                TT("vector", KLre[:], kvb, lre[:].unsqueeze(1).broadcast_to([128, NK, 32]), ALU.mult,
                   [bcst, blre], [bKLre])
                TT("vector", KLim[:], kvb, lim[:].unsqueeze(1).broadcast_to([128, NK, 32]), ALU.mult,
                   [bcst, blim], [bKLim])
                E, bE = KLre, bKLre
                ACT(E[:], KLre[:], AF.Exp, [bKLre], [bE])
                y, by = tl["y"]; yf, byf = tl["yf"]
                for which, shift in (("Wim", 64.5), ("Wre", 64.75)):
                    W, bW = tl[which]
                    TS("vector", y[:], KLim[:], 1.0 / TWO_PI, ALU.mult, [bKLim], [by], s2=shift, op1=ALU.add)
                    CP("vector", yi[:], y[:], [by], [byi])
                    CP("vector", yf[:], yi[:], [byi], [byf])
                    TT("vector", y[:], y[:], yf[:], ALU.subtract, [by, byf], [by])
                    TS("vector", yf[:], y[:], 0.0, ALU.is_lt, [by], [byf])
                    TT("vector", y[:], y[:], yf[:], ALU.add, [by, byf], [by])
                    TS("vector", y[:], y[:], 1e-6, ALU.max, [by], [by], s2=1.0 - 1e-6, op1=ALU.min)
                    ACT(yf[:], y[:], AF.Sin, [by, bnpi], [byf], scale=TWO_PI, bias=npi[:])
                    TT("vector", W[:], E[:], yf[:], ALU.mult, [bE, byf], [bW])
                Wre, bWre = tl["Wre"]; Wim, bWim = tl["Wim"]
                G0 = KOFF["G"]
                gds = slice(d * 32, (d + 1) * 32)
                lo = slice(0, 64); hi = slice(64, 128)
                rdW = [bWre, bWim]
                CP("vector", sc[lo, 0, :, gds], Wre[lo, G0:G0 + 7, :], rdW, [bsc])
                CP("vector", sc[hi, 0, :, gds], Wim[hi, G0:G0 + 7, :], rdW, [bsc])
                TS("vector", sc[lo, 1, :, gds], Wim[lo, G0:G0 + 7, :], -1.0, ALU.mult, rdW, [bsc])
                CP("vector", sc[hi, 1, :, gds], Wre[hi, G0:G0 + 7, :], rdW, [bsc])
                CP("vector", sc[lo, 2, :, gds], Wre[lo, G0:G0 + 7, :], rdW, [bsc])
                TS("vector", sc[hi, 2, :, gds], Wim[hi, G0:G0 + 7, :], -1.0, ALU.mult, rdW, [bsc])
                TS("vector", sc[lo, 3, :, gds], Wim[lo, G0:G0 + 7, :], -1.0, ALU.mult, rdW, [bsc])
                TS("vector", sc[hi, 3, :, gds], Wre[hi, G0:G0 + 7, :], -1.0, ALU.mult, rdW, [bsc])
                H0 = KOFF["H"]
                CP("vector", AA[:, d, 0, :], Wre[lo, H0, :], rdW, [bAA])
                CP("vector", AA[:, d, 1, :], Wre[lo, H0, :], rdW, [bAA])
                TS("vector", AB[:, d, 0, :], Wim[lo, H0, :], -1.0, ALU.mult, rdW, [bAB])
                CP("vector", AB[:, d, 1, :], Wim[lo, H0, :], rdW, [bAB])
                k1 = KOFF["B"] + 1
                nre, bnre = sm["nre"]; den, bden = sm["den"]; t1, bt1 = sm["t1"]; t2, bt2 = sm["t2"]
                cre, bcre = sm["cre"]; cim, bcim = sm["cim"]
                TS("vector", nre[:], Wre[:, k1, :], -1.0, ALU.add, rdW, [bnre])
                TT("vector", den[:], ar, ar, ALU.mult, [bpar], [bden])
                TT("vector", t1[:], ai, ai, ALU.mult, [bpar], [bt1])
                TT("vector", den[:], den[:], t1[:], ALU.add, [bden, bt1], [bden])
                k.op("vector", lambda e, den=den: e.reciprocal(out=den[:], in_=den[:]), [bden], [bden])
                TT("vector", t1[:], nre[:], ar, ALU.mult, [bnre, bpar], [bt1])
                TT("vector", t2[:], Wim[:, k1, :], ai, ALU.mult, rdW + [bpar], [bt2])
                TT("vector", t1[:], t1[:], t2[:], ALU.add, [bt1, bt2], [bt1])
                TT("vector", cre[:], t1[:], den[:], ALU.mult, [bt1, bden], [bcre])
                TT("vector", t1[:], Wim[:, k1, :], ar, ALU.mult, rdW + [bpar], [bt1])
                TT("vector", t2[:], nre[:], ai, ALU.mult, [bnre, bpar], [bt2])
                TT("vector", t1[:], t1[:], t2[:], ALU.subtract, [bt1, bt2], [bt1])
                TT("vector", cim[:], t1[:], den[:], ALU.mult, [bt1, bden], [bcim])
                creb = cre[:].unsqueeze(2).broadcast_to([128, 32, 16])
                cimb = cim[:].unsqueeze(2).broadcast_to([128, 32, 16])
                TT("vector", bbre[:], creb, br, ALU.mult, [bcre, bpar], [bbbre])
                TT("vector", tb[:], cimb, bi, ALU.mult, [bcim, bpar], [btb])
                TT("vector", bbre[:], bbre[:], tb[:], ALU.subtract, [bbbre, btb], [bbbre])
                TT("vector", bbim[:], creb, bi, ALU.mult, [bcre, bpar], [bbbim])
                TT("vector", tb[:], cimb, br, ALU.mult, [bcim, bpar], [btb])
                TT("vector", bbim[:], bbim[:], tb[:], ALU.add, [bbbim, btb], [bbbim])
                rb = [bbbre, bbbim]
                CP("vector", P1[lo], bbre[lo], rb, [bP1])
                CP("vector", P1[hi], bbim[hi], rb, [bP1])
                TS("vector", P2[lo], bbim[lo], -1.0, ALU.mult, rb, [bP2])
                CP("vector", P2[hi], bbre[hi], rb, [bP2])
                CP("vector", P1c[lo], cr[lo], [bpar], [bP1c])
                TS("vector", P1c[hi], ci[hi], -1.0, ALU.mult, [bpar], [bP1c])
                TS("vector", P2c[lo], ci[lo], -1.0, ALU.mult, [bpar], [bP2c])
                TS("vector", P2c[hi], cr[hi], -1.0, ALU.mult, [bpar], [bP2c])
                segs = {0: ("A", "C", "E", "B"), 1: ("B", "D", "B", "E")}[d]
                jobs = [(BfS[:, gds, :], bBfS[d], segs[0], P1, P2, bP1, bP2),
                        (CfS[:, gds, :], bCfS[d], segs[1], P1c, P2c, bP1c, bP2c),
                        (BnS[:], bBnS, segs[2], P1, P2, bP1, bP2),
                        (CpS[:], bCpS, segs[3], P1c, P2c, bP1c, bP2c)]
                for ji, (outap, bout, seg, Pa, Pb, bPa, bPb) in enumerate(jobs):
                    o = KOFF[seg]
                    for half in range(4):
                        g0 = half * 8
                        eng = "vector"
                        wre = Wre[:, o:o + 8, g0:g0 + 8].rearrange("p k g -> p g k").unsqueeze(3).broadcast_to([128, 8, 8, 16])
                        wim = Wim[:, o:o + 8, g0:g0 + 8].rearrange("p k g -> p g k").unsqueeze(3).broadcast_to([128, 8, 8, 16])
                        pa = Pa[:, g0:g0 + 8, :].unsqueeze(2).broadcast_to([128, 8, 8, 16])
                        pb_ = Pb[:, g0:g0 + 8, :].unsqueeze(2).broadcast_to([128, 8, 8, 16])
                        TT(eng, m1[:], wre, pa, ALU.mult, rdW + [bPa], [bm1])
                        TT(eng, m2[:], wim, pb_, ALU.mult, rdW + [bPb], [bm2])
                        TT(eng, outap[:, g0:g0 + 8, :].rearrange("p g (k h) -> p g k h", h=16), m1[:], m2[:],
                           ALU.add, [bm1, bm2], [bout])
                for g4 in range(8):
                    pb = nextps()
                    for gg in range(4):
                        g = g4 * 4 + gg
                        MM(PS[pb][:, gg * 128:(gg + 1) * 128], BnS[:, g, :], CpS[:, g, :], True, True,
                           [bBnS, bCpS], [bPS[pb]], sig=(gg == 3))
                    for gg in range(4):
                        g = g4 * 4 + gg
                        tm, btm = tmpm[g % 2], btmpm[g % 2]
                        if d == 0:
                            TT("vector", tm[:], PS[pb][:, gg * 128:(gg + 1) * 128], maskF, ALU.mult,
                               [bPS[pb], bcst], [btm])
                            k.op("vector", lambda e, g=g, tm=tm: e.scalar_tensor_tensor(
                                out=T8[:, g, :], in0=identF, scalar=dvec[:, g:g + 1], in1=tm[:],
                                op0=ALU.mult, op1=ALU.add), [btm, bcst], [bT8[g]])
                        else:
                            TT("vector", tm[:], PS[pb][:, gg * 128:(gg + 1) * 128], maskB, ALU.mult,
                               [bPS[pb], bcst], [btm])
                            TT("vector", T8[:, g, :], tm[:], T8[:, g, :], ALU.add, [btm, bT8[g]], [bT8[g]])
            k.emit()

        with ExitStack() as stZ:
            Zs = sbt(nc, stZ, "Zs", [64, 2, 64, 128], BF16); bZs = [Buf() for _ in range(64)]
            hist = sbt(nc, stZ, "hist", [64, 2, 2, 32, 64], BF16); bhist = [Buf(), Buf()]
            ST = sbt(nc, stZ, "ST", [128, 2, 32, 64], BF16); bST = [Buf(), Buf()]
            with ExitStack() as st:
                JM = [sbt(nc, st, "JM%d" % i, [128, 7, 128], BF16) for i in range(2)]
                bJM = [Buf() for _ in range(2)]
                B64 = [sbt(nc, st, "B64_%d" % i, [128, 8, 128], BF16) for i in range(2)]
                bB64 = [Buf() for _ in range(2)]
                def r1_build(it):
                    g, d = it // 2, it % 2
                    gd = d * 32 + g
                    q = it % 2
                    TT("vector", JM[q][:].rearrange("p k (w c) -> p k w c", w=2),
                       II[:].rearrange("p (w c) -> p w c", w=2).unsqueeze(1).broadcast_to([128, 7, 2, 64]),
                       sc[:, 2:4, :, gd].rearrange("p w k -> p k w").unsqueeze(3).broadcast_to([128, 7, 2, 64]),
                       ALU.mult, [bII, bsc], [bJM[q]])

                def r1_gen(it):
                    g, d = it // 2, it % 2
                    gd = d * 32 + g
                    q = it % 2
                    for half in range(2):
                        pb = nextps()
                        for xx in range(4):
                            x = half * 4 + xx
                            k8 = 7 - x
                            rhs = identJ[:] if k8 == 0 else JM[q][:, k8 - 1, :]
                            MM(PS[pb][:, xx * 128:(xx + 1) * 128], BfS[:, gd, :], rhs, True, True,
                               [bBfS[d], bJM[q], bidJ], [bPS[pb]], sig=(xx == 3))
                        if half == 0:
                            ACT(B64[q][:, 0:4, :], PS[pb][:].rearrange("p (a b) -> p a b", a=4), AF.Copy,
                                [bPS[pb]], [bB64[q]])
                        else:
                            CP("vector", B64[q][:, 4:8, :], PS[pb][:].rearrange("p (a b) -> p a b", a=4),
                               [bPS[pb]], [bB64[q]])

                def r1_z(it):
                    g, d = it // 2, it % 2
                    gd = d * 32 + g
                    q = it % 2
                    pb = nextps()
                    N = 128 if d == 0 else 64
                    for ri in range(2):
                        for x in range(8):
                            if d == 0:
                                rhs = U[:, g, :].rearrange("p (hf j m) -> p hf j m", hf=2, j=8)[:, :, x, :]
                            else:
                                rhs = U[:, g, 512 + (7 - x) * 64:512 + (8 - x) * 64]
                            MM(PS[pb][0:64, ri * 128:ri * 128 + N], B64[q][:, x, ri * 64:(ri + 1) * 64], rhs,
                               x == 0, x == 7, [bB64[q]] + bUblk, [bPS[pb]], sig=(ri == 1 and x == 7))
                    m0 = 0 if d == 0 else 64
                    ACT(Zs[:, 0, gd, m0:m0 + N], PS[pb][0:64, 0:N], AF.Copy, [bPS[pb]], [bZs[gd]])
                    ACT(Zs[:, 1, gd, m0:m0 + N], PS[pb][0:64, 128:128 + N], AF.Copy, [bPS[pb]], [bZs[gd]], scale=-1.0)

                r1_build(0)
                for it in range(65):
                    if it + 1 < 64:
                        r1_build(it + 1)
                    if it < 64:
                        r1_gen(it)
                    if it >= 1:
                        r1_z(it - 1)
                k.emit()
            with ExitStack() as st:
                Sst = [sbt(nc, st, "S_%d" % i, [64, 2, 2, 32], F32) for i in range(2)]
                bS = [Buf() for _ in range(2)]
                T1 = sbt(nc, st, "T1", [64, 2, 2, 32], F32); bT1 = Buf()
                T2 = sbt(nc, st, "T2", [64, 2, 2, 32], F32); bT2 = Buf()
                for i in range(2):
                    k.op("vector", lambda e, i=i: e.memset(Sst[i][:], 0.0), [], [bS[i]])
                for s_ in range(128):
                    cur = s_ % 2
                    S, bSc = Sst[cur], bS[cur]
                    Sn, bSn = Sst[1 - cur], bS[1 - cur]
                    dsl = slice(0, 1) if s_ < 64 else slice(0, 2)
                    mb = 191 - s_
                    if s_ >= 64:
                        ACT(hist[:, 0, :, :, s_ - 64], S[:, 0, :, :], AF.Copy, [bSc], [bhist[0]])
                        ACT(hist[:, 1, :, :, mb - 64], S[:, 1, :, :], AF.Copy, [bSc], [bhist[1]])
                    TT("vector", T1[:, dsl], S[:, dsl], AA[:, dsl], ALU.mult, [bSc, bAA], [bT1])
                    TT("vector", T1[:, 0], T1[:, 0], Zs[:, :, 0:32, s_], ALU.add, [bT1] + bZs[0:32], [bT1])
                    if s_ >= 64:
                        TT("vector", T1[:, 1], T1[:, 1], Zs[:, :, 32:64, mb], ALU.add, [bT1] + bZs[32:64], [bT1])
                    TT("vector", T2[:, dsl, 0, :], S[:, dsl, 1, :], AB[:, dsl, 0, :], ALU.mult, [bSc, bAB], [bT2])
                    TT("vector", T2[:, dsl, 1, :], S[:, dsl, 0, :], AB[:, dsl, 1, :], ALU.mult, [bSc, bAB], [bT2])
                    TT("vector", Sn[:, dsl], T1[:, dsl], T2[:, dsl], ALU.add, [bT1, bT2], [bSn])
                for d in range(2):
                    k.dma("sync", lambda e, d=d: e.dma_start(out=ST[0:64, d, :, :], in_=hist[:, d, 0, :, :]),
                          reads=[bhist[d]], writes=[bST[d]])
                    k.dma("sync", lambda e, d=d: e.dma_start(out=ST[64:128, d, :, :], in_=hist[:, d, 1, :, :]),
                          reads=[bhist[d]], writes=[bST[d]])
                k.emit()
            with ExitStack() as st:
                MMb = [sbt(nc, st, "MMb%d" % i, [128, 7, 128], BF16) for i in range(2)]
                bMMb = [Buf() for _ in range(2)]
                C64 = [sbt(nc, st, "C64_%d" % i, [128, 7, 128], BF16) for i in range(6)]
                bC64 = [Buf() for _ in range(6)]
                Tz = [sbt(nc, st, "Tz%d" % i, [128, 7, 128], BF16) for i in range(4)]
                bTz = [Buf() for _ in range(4)]

                def r2_A(g):
                    for d in range(2):
                        gd = d * 32 + g
                        q = d
                        cs = (g % 3) * 2 + d
                        TT("vector", MMb[q][:].rearrange("p k (w c) -> p k w c", w=2),
                           II[:].rearrange("p (w c) -> p w c", w=2).unsqueeze(1).broadcast_to([128, 7, 2, 64]),
                           sc[:, 0:2, :, gd].rearrange("p w k -> p k w").unsqueeze(3).broadcast_to([128, 7, 2, 64]),
                           ALU.mult, [bII, bsc], [bMMb[q]])
                        for half in range(2):
                            pb = nextps()
                            n = 4 if half == 0 else 3
                            for xx in range(n):
                                x = 1 + half * 4 + xx
                                MM(PS[pb][:, xx * 128:(xx + 1) * 128], MMb[q][:, x - 1, :], CfS[:, gd, :], True, True,
                                   [bMMb[q], bCfS[d]], [bPS[pb]], sig=(xx == n - 1))
                            src_ = PS[pb][:, 0:n * 128].rearrange("p (a b) -> p a b", a=n)
                            dst_ = C64[cs][:, half * 4:half * 4 + n, :]
                            if half == 0:
                                ACT(dst_, src_, AF.Copy, [bPS[pb]], [bC64[cs]])
                            else:
                                CP("vector", dst_, src_, [bPS[pb]], [bC64[cs]])

                def r2_B(g):
                    for d in range(2):
                        gd = d * 32 + g
                        cs = (g % 3) * 2 + d
                        ts = (g % 2) * 2 + d
                        for half in range(2):
                            pb = nextps()
                            n = 4 if half == 0 else 3
                            for xx in range(n):
                                dl = 1 + half * 4 + xx
                                rhs = CfS[:, gd, :] if dl == 1 else C64[cs][:, dl - 2, :]
                                MM(PS[pb][:, xx * 128:(xx + 1) * 128], BfS[:, gd, :], rhs, True, True,
                                   [bBfS[d], bCfS[d], bC64[cs]], [bPS[pb]], sig=(xx == n - 1))
                            src_ = PS[pb][:, 0:n * 128].rearrange("p (a b) -> p a b", a=n)
                            dst_ = Tz[ts][:, half * 4:half * 4 + n, :]
                            if half == 0:
                                CP("vector", dst_, src_, [bPS[pb]], [bTz[ts]])
                            else:
                                ACT(dst_, src_, AF.Copy, [bPS[pb]], [bTz[ts]])

                def r2_C(g):
                    cs0, cs1 = (g % 3) * 2, (g % 3) * 2 + 1
                    ts0, ts1 = (g % 2) * 2, (g % 2) * 2 + 1
                    pb = nextps()
                    rdall = bUblk + [bT8[g], bTz[ts0], bTz[ts1], bC64[cs0], bC64[cs1],
                                     bCfS[0], bCfS[1], bST[0], bST[1]]
                    MM(PS[pb][:], T8[:, g, :], U[:, g, 512:1024], True, False, rdall, [bPS[pb]], sig=False)
                    for dl in range(1, 8):
                        MM(PS[pb][:, dl * 64:512], Tz[ts0][:, dl - 1, :], U[:, g, 512:512 + (8 - dl) * 64], False, False,
                           rdall, [bPS[pb]], sig=False)
                        MM(PS[pb][:, 0:(8 - dl) * 64], Tz[ts1][:, dl - 1, :], U[:, g, 512 + dl * 64:1024], False, False,
                           rdall, [bPS[pb]], sig=False)
                    for j2 in range(8):
                        lf = CfS[:, g, :] if j2 == 0 else C64[cs0][:, j2 - 1, :]
                        MM(PS[pb][:, j2 * 64:(j2 + 1) * 64], lf, ST[:, 0, g, :], False, False, rdall, [bPS[pb]], sig=False)
                        xb_ = 7 - j2
                        lb = CfS[:, 32 + g, :] if xb_ == 0 else C64[cs1][:, xb_ - 1, :]
                        MM(PS[pb][:, j2 * 64:(j2 + 1) * 64], lb, ST[:, 1, g, :], False, j2 == 7, rdall, [bPS[pb]],
                           sig=(j2 == 7))
                    yout = U[:, g, 0:512].rearrange("p (m j) -> p j m", j=8)
                    yin = PS[pb][:].rearrange("p (j m) -> p j m", j=8)
                    if g % 2 == 0:
                        ACT(yout, yin, AF.Copy, [bPS[pb]], [bUblk[0]])
                    else:
                        CP("vector", yout, yin, [bPS[pb]], [bUblk[0]])

                for t in range(34):
                    if t < 32:
                        r2_A(t)
                    if 1 <= t <= 32:
                        r2_B(t - 1)
                    if t >= 2:
                        r2_C(t - 2)
                if stage == 2:
                    t1 = k.dma("gpsimd", lambda e: e.dma_start(
                        out=dr["dbg_y"].rearrange("p (g c) -> p g c", g=32), in_=U[:, :, 0:512]), reads=bUblk)
                    k.wait_tok("gpsimd", t1)
                k.emit()


def pass_S(nc, k, dr, PS, bPS, ident, bident, U, bUblk, stage):
    def TT(eng, out, a, b, op, rd, wr):
        k.op(eng, lambda e: e.tensor_tensor(out=out, in0=a, in1=b, op=op), rd, wr)

    def ACT(out, in_, func, rd, wr, **kw):
        k.op("scalar", lambda e: e.activation(out=out, in_=in_, func=func, **kw), rd, wr)

    def CP(eng, out, in_, rd, wr):
        k.op(eng, lambda e: e.tensor_copy(out=out, in_=in_), rd, wr)

    def MM(out, lhsT, rhs, start, stop, rd, wr, sig):
        k.op("tensor", lambda e: e.matmul(out, lhsT=lhsT, rhs=rhs, start=start, stop=stop), rd, wr, sig=sig)

    pi = [0]

    def nextps():
        pb = pi[0] % 8
        pi[0] += 1
        return pb

    with ExitStack() as st:
        wglu = sbt(nc, st, "wglu", [128, 4, 512], BF16); bwglu = Buf()
        k.dma("gpsimd", lambda e: e.dma_start(out=wglu[:], in_=dr["w_glu"].rearrange("(kt f) c -> f kt c", f=128)),
              writes=[bwglu])
        bglu = sbt(nc, st, "bglu", [1, 512], BF16); bbglu = Buf()
        k.dma("gpsimd", lambda e: e.dma_start(out=bglu[:], in_=dr["b_glu"]), writes=[bbglu])
        ones = sbt(nc, st, "ones", [1, 128], BF16); bones = Buf()
        k.op("vector", lambda e: e.memset(ones[:], 1.0), [], [bones])
        ys = [sbt(nc, st, "ys%d" % i, [128, 8, 512], F32) for i in range(2)]; bys = [Buf(), Buf()]
        yg = [sbt(nc, st, "yg%d" % i, [128, 8, 512], BF16) for i in range(2)]; byg = [Buf(), Buf()]
        zsl = [sbt(nc, st, "zsl%d" % i, [128, 8, 512], BF16) for i in range(2)]; bzsl = [Buf(), Buf()]
        mixs = [sbt(nc, st, "mixs%d" % i, [128, 8, 512], BF16) for i in range(2)]
        bmixs = [Buf() for _ in range(2)]
        ygT = [sbt(nc, st, "ygT%d" % i, [128, 4, 128], BF16) for i in range(2)]
        bygT = [Buf() for _ in range(2)]
        sgl = [sbt(nc, st, "sgl%d" % i, [128, 512], F32) for i in range(2)]
        bsgl = [Buf() for _ in range(2)]
        t1 = [sbt(nc, st, "t1s%d" % i, [128, 512], F32) for i in range(2)]
        bt1 = [Buf() for _ in range(2)]
        mdv = dr["mix_d"].rearrange("(blk c j) f -> blk c j f", c=128, j=8)

        def s_A(blk):
            b2 = blk % 2
            k.dma("sync", lambda e: e.dma_start(out=zsl[b2][:].rearrange("p a b -> p (a b)"), in_=dr["zs_d"][blk]),
                  writes=[bzsl[b2]])
            for g4 in range(4):
                pb = nextps()
                pT = PS[pb][:].bitcast(BF16).rearrange("p (g j h) -> p g j h", g=8, j=8)
                for gg in range(8):
                    g = g4 * 8 + gg
                    k.op("tensor", lambda e, pT=pT, gg=gg, g=g: e.transpose(
                        out=pT[:, gg, :, :].rearrange("p j h -> p (j h)"), in_=U[:, g, blk * 128:(blk + 1) * 128],
                        identity=ident[:]), reads=bUblk + [bident], writes=[bPS[pb]], sig=(gg == 7))
                ACT(ys[b2][:, :, g4 * 128:(g4 + 1) * 128].rearrange("p j (g h) -> p j g h", h=16),
                    pT.rearrange("p g j h -> p j g h"), AF.Copy, [bPS[pb]], [bys[b2]])
            for half in range(2):
                ACT(yg[b2][:, half * 4:(half + 1) * 4, :], ys[b2][:, half * 4:(half + 1) * 4, :], AF.Gelu_apprx_tanh,
                    [bys[b2]], [byg[b2]])

        def s_T(blk, j):
            b2 = blk % 2
            q = j % 2
            pb = nextps()
            pT = PS[pb][:].bitcast(BF16).rearrange("p (a b) -> p a b", a=8)
            for kt in range(4):
                k.op("tensor", lambda e, kt=kt: e.transpose(
                    out=pT[:, kt, :], in_=yg[b2][:, j, kt * 128:(kt + 1) * 128], identity=ident[:]),
                    reads=[byg[b2], bident], writes=[bPS[pb]], sig=(kt == 3))
            CP("vector", ygT[q][:], pT[:, 0:4, :], [bPS[pb]], [bygT[q]])

        def s_M(blk, j):
            b2 = blk % 2
            mb = blk % 2
            q = j % 2
            pb = nextps()
            for kt in range(4):
                MM(PS[pb][:], ygT[q][:, kt, :], wglu[:, kt, :], kt == 0, False, [bygT[q], bwglu], [bPS[pb]], False)
            MM(PS[pb][:], ones[:], bglu[:], False, True, [bones, bbglu], [bPS[pb]], True)
            ACT(sgl[q][:], PS[pb][:], AF.Tanh, [bPS[pb]], [bsgl[q]], scale=0.5)
            k.op("vector", lambda e: e.scalar_tensor_tensor(
                out=t1[q][:], in0=yg[b2][:, j, :], scalar=0.25, in1=zsl[b2][:, j, :], op0=ALU.mult, op1=ALU.mult),
                [byg[b2], bzsl[b2]], [bt1[q]])
            k.op("vector", lambda e: e.scalar_tensor_tensor(
                out=mixs[mb][:, j, :], in0=sgl[q][:], scalar=1.0, in1=t1[q][:], op0=ALU.add, op1=ALU.mult),
                [bt1[q], bsgl[q]], [bmixs[mb]])
            if j == 7:
                k.dma("sync", lambda e: e.dma_start(out=mdv[blk][:, :, 0:512], in_=mixs[mb][:]), reads=[bmixs[mb]])

        s_A(0)
        for blk in range(4):
            if blk + 1 < 4:
                s_A(blk + 1)
            s_T(blk, 0)
            for j in range(8):
                if j + 1 < 8:
                    s_T(blk, j + 1)
                s_M(blk, j)
        if stage == 3:
            t_ = k.dma("sync", lambda e: e.dma_start(out=dr["dbg_ms"], in_=mixs[1][:].rearrange("p a b -> p (a b)")),
                       reads=[bmixs[1]])
            k.wait_tok("sync", t_)
        k.emit()


def prefetch_N(nc, k, dr, st):
    W = {}
    wsrc = dr["w_in"].rearrange("(kt f) c -> f kt c", f=128)
    W["wq"] = sbt(nc, st, "wq", [128, 8, 2048], BF16)
    W["bwq"] = {nm: [Buf() for _ in range(8)] for nm in ("v", "k", "zn", "q")}
    own = {nm: Buf() for nm in ("v", "k", "zn", "q")}
    for nm, c0 in (("v", 1024), ("k", 512), ("zn", 1536), ("q", 0)):
        for kt in range(8):
            k.dma("gpsimd", lambda e, kt=kt, c0=c0: e.dma_start(
                out=W["wq"][:, kt, c0:c0 + 512], in_=wsrc[:, kt, 1024 + c0:1024 + c0 + 512]),
                writes=[W["bwq"][nm][kt]], owner=own[nm])
    W["tab"] = sbt(nc, st, "tab", [128, 40, 128], BF16); W["btab"] = Buf()
    tabv = dr["na_tab"].rearrange("kd h i k q -> kd k (h i) q")
    tparts = [Buf() for _ in range(4)]
    for hi_, h in enumerate(range(0, 8, 2)):
        k.dma("gpsimd", lambda e, h=h: e.dma_start(
            out=W["tab"][:, h * 5:(h + 2) * 5, :], in_=tabv[0][:, h * 5:(h + 2) * 5, :]),
            writes=[tparts[hi_]], owner=W["btab"])
    W["tparts"] = tparts
    for nm, key, n_ in (("w_out", "wout", 8), ("w_gate", "wgate", 8), ("w_ple", "wple", 2)):
        W[key] = sbt(nc, st, key, [128, n_, 1024], BF16); W["b" + key] = [Buf() for _ in range(n_)]
        ownb = Buf()
        sv = dr[nm].rearrange("(kt f) c -> f kt c", f=128)
        for kt in range(n_):
            k.dma("gpsimd", lambda e, kt=kt, key=key, sv=sv: e.dma_start(out=W[key][:, kt, :], in_=sv[:, kt, :]),
                  writes=[W["b" + key][kt]], owner=ownb)
    return W


def pass_N(nc, k, dr, PS, bPS, ident, bident, stage, W):
    def TT(eng, out, a, b, op, rd, wr):
        k.op(eng, lambda e: e.tensor_tensor(out=out, in0=a, in1=b, op=op), rd, wr)

    def ACT(out, in_, func, rd, wr, **kw):
        k.op("scalar", lambda e: e.activation(out=out, in_=in_, func=func, **kw), rd, wr)

    def CP(eng, out, in_, rd, wr):
        k.op(eng, lambda e: e.tensor_copy(out=out, in_=in_), rd, wr)

    def MM(out, lhsT, rhs, start, stop, rd, wr, sig):
        k.op("tensor", lambda e: e.matmul(out, lhsT=lhsT, rhs=rhs, start=start, stop=stop), rd, wr, sig=sig)

    def TR(out, in_, rd, wr, sig):
        k.op("tensor", lambda e: e.transpose(out=out, in_=in_, identity=ident[:]), rd + [bident], wr, sig=sig)

    pi = [0]

    def nextps():
        pb = pi[0] % 6
        pi[0] += 1
        return pb

    out_toks = []
    with ExitStack() as st:
        wq, bwq, tab, btab = W["wq"], W["bwq"], W["tab"], W["btab"]
        wout, bwout, wgate, bwgate, wple, bwple = W["wout"], W["bwout"], W["wgate"], W["bwgate"], W["wple"], W["bwple"]
        tabv = dr["na_tab"].rearrange("kd h i k q -> kd k (h i) q")

        def load_tab(kind):
            for h in range(0, 8, 2):
                k.dma("gpsimd", lambda e, h=h, kind=kind: e.dma_start(
                    out=tab[:, h * 5:(h + 2) * 5, :], in_=tabv[kind][:, h * 5:(h + 2) * 5, :]), writes=[btab])
            ACT(tab[:], tab[:], AF.Exp, [btab], [btab])
        ACT(tab[:], tab[:], AF.Exp, W["tparts"], [btab])
        npre = sbt(nc, st, "npre2", [128, 8], F32); bnpre = Buf()
        k.dma("sync", lambda e: e.dma_start(out=npre[:], in_=dr["npre"]), writes=[bnpre])
        npost = sbt(nc, st, "npost", [128, 1024], F32); bnpost = Buf()
        k.dma("sync", lambda e: e.dma_start(out=npost[:], in_=dr["npost_b"]), writes=[bnpost])
        plen = sbt(nc, st, "plen", [128, 1024], F32); bplen = Buf()
        k.dma("sync", lambda e: e.dma_start(out=plen[:], in_=dr["plen_b"]), writes=[bplen])
        KT = sbt(nc, st, "KT", [128, 4, 1536], BF16); bKT = [Buf() for _ in range(3)]
        V = sbt(nc, st, "V", [128, 12, 8, 65], BF16); bV = [Buf() for _ in range(3)]
        k.op("vector", lambda e: e.memset(V[:], 2.0), [], bV)
        nhalf = sbt(nc, st, "nhalfN", [128, 1], F32); bnhalf = Buf()
        k.op("gpsimd", lambda e: e.memset(nhalf[:], -0.5), [], [bnhalf])
        tz = [sbt(nc, st, "tz%d" % i, [128, 512], BF16) for i in range(2)]; btz = [Buf() for _ in range(2)]

        def rsqrt_mean(dst, srcap, rd, wr):
            k.op("gpsimd", lambda e: e.tensor_scalar(out=dst, in0=srcap, scalar1=1.0 / D, scalar2=EPS,
                                                      op0=ALU.mult, op1=ALU.add), rd, wr)
            k.op("gpsimd", lambda e: e.tensor_tensor(out=dst, in0=dst, in1=nhalf[:], op=ALU.pow),
                 wr + [bnhalf], wr)
        QT = [sbt(nc, st, "QT%d" % i, [128, 4, 512], BF16) for i in range(2)]; bQT = [Buf() for _ in range(2)]
        zn = [sbt(nc, st, "zn%d" % i, [128, 4, 512], BF16) for i in range(2)]; bzn = [Buf() for _ in range(2)]
        hnTg = [sbt(nc, st, "hnTg%d" % i, [128, 8, 512], BF16) for i in range(2)]
        bhn = [[Buf() for _ in range(4)] for _ in range(2)]
        xin = [sbt(nc, st, "xinN%d" % i, [128, 1024], F32) for i in range(2)]; bxin = [Buf() for _ in range(2)]
        xs = [sbt(nc, st, "xsN%d" % i, [128, 1024], BF16) for i in range(2)]; bxs = [Buf() for _ in range(2)]
        sst = sbt(nc, st, "sstN", [128, 40], F32); bss = [Buf() for _ in range(40)]
        rst = sbt(nc, st, "rstN", [128, 40], F32); brs = [Buf() for _ in range(40)]
        PT = [sbt(nc, st, "PT%d" % i, [128, 5, 128], BF16) for i in range(4)]; bPT = [Buf() for _ in range(4)]
        rden = [sbt(nc, st, "rden%d" % i, [128, 8], F32) for i in range(2)]; brden = [Buf() for _ in range(2)]
        wt = [sbt(nc, st, "wt%d" % i, [128, 4, 64], BF16) for i in range(2)]; bwt = [Buf() for _ in range(2)]
        mixt = [sbt(nc, st, "mixt%d" % i, [128, 1024], BF16) for i in range(2)]
        bmixA = [Buf() for _ in range(2)]; bmixB = [[Buf(), Buf()] for _ in range(2)]
        mixT = sbt(nc, st, "mixT", [128, 8, 128], BF16); bmixT = Buf()
        xr = [sbt(nc, st, "xr%d" % i, [128, 1024], F32) for i in range(2)]; bxr = [Buf() for _ in range(2)]
        ptl = [sbt(nc, st, "ptl%d" % i, [128, 256], F32) for i in range(2)]; bptl = [Buf() for _ in range(2)]
        pb16 = sbt(nc, st, "pb16", [128, 256], BF16); bpb16 = Buf()
        pT = sbt(nc, st, "pT", [128, 2, 128], BF16); bpT = Buf()
        h1 = [sbt(nc, st, "h1_%d" % i, [128, 1024], F32) for i in range(2)]; bh1 = [Buf() for _ in range(2)]
        h1b = [sbt(nc, st, "h1b%d" % i, [128, 1024], BF16) for i in range(2)]; bh1b = [Buf() for _ in range(2)]
        h1T = sbt(nc, st, "h1T", [128, 8, 128], BF16); bh1T = Buf()
        sg = [sbt(nc, st, "sg%d" % i, [128, 1024], BF16) for i in range(2)]; bsg = [Buf() for _ in range(2)]
        et = [sbt(nc, st, "et%d" % i, [128, 1024], F32) for i in range(2)]; bet = [Buf() for _ in range(2)]
        s2 = [sbt(nc, st, "s2_%d" % i, [128, 8], F32) for i in range(2)]; bs2 = [Buf() for _ in range(2)]

        def tile_ok(tt):
            return -2 <= tt <= 31

        def grp(tt):
            G = tt // 4
            return G, tt - 4 * G, (G + 1) % 3

        def pre(tt):
            if not tile_ok(tt):
                return
            tok0 = HALF + 128 * tt
            si = (tt + 2) % 40
            xb = (tt + 2) % 2
            k.dma("sync", lambda e: e.dma_start(out=xin[xb][:], in_=dr["x_all"][tok0:tok0 + 128, :]), writes=[bxin[xb]])

        def pre_act(tt):
            if not tile_ok(tt):
                return
            si = (tt + 2) % 40
            xb = (tt + 2) % 2
            ACT(xs[xb][:], xin[xb][:], AF.Square, [bxin[xb]], [bxs[xb], bss[si]], accum_out=sst[:, si:si + 1])
            rsqrt_mean(rst[:, si:si + 1], sst[:, si:si + 1], [bss[si]], [brs[si]])
            ACT(xs[xb][:], xin[xb][:], AF.Copy, [bxin[xb], brs[si]], [bxs[xb]], scale=rst[:, si:si + 1])

        def trn(tt):
            if not tile_ok(tt):
                return
            G, pos, sg3 = grp(tt)
            xb = (tt + 2) % 2
            hb = G % 2
            pb = nextps()
            pTt = PS[pb][:].bitcast(BF16).rearrange("p (a b) -> p a b", a=8)
            for kt in range(8):
                TR(pTt[:, kt, :], xs[xb][:, kt * 128:(kt + 1) * 128], [bxs[xb]], [bPS[pb]], kt == 7)
            TT("vector", hnTg[hb][:, :, pos * 128:(pos + 1) * 128], pTt,
               npre[:].unsqueeze(2).broadcast_to([128, 8, 128]), ALU.mult, [bPS[pb], bnpre], [bhn[hb][pos]])

        def mmv(tt):
            if not tile_ok(tt):
                return
            G, pos, sg3 = grp(tt)
            hb = G % 2
            qg = G % 2
            hT = hnTg[hb]
            pb = nextps()
            for kt in range(8):
                MM(PS[pb][:], hT[:, kt, pos * 128:(pos + 1) * 128], wq[:, kt, 1024:1536], kt == 0, kt == 7,
                   [bhn[hb][pos]] + bwq["v"], [bPS[pb]], kt == 7)
            CP("vector", V[:, sg3 * 4 + pos, :, 0:64], PS[pb][:].rearrange("p (h d) -> p h d", d=64),
               [bPS[pb]], [bV[sg3]])
            if G >= 0:
                pb = nextps()
                for kt in range(8):
                    MM(PS[pb][:], hT[:, kt, pos * 128:(pos + 1) * 128], wq[:, kt, 1536:2048], kt == 0, kt == 7,
                       [bhn[hb][pos]] + bwq["zn"], [bPS[pb]], kt == 7)
                tzi = tt % 2
                ACT(tz[tzi][:], PS[pb][:], AF.Tanh, [bPS[pb]], [btz[tzi]], scale=0.5)
                k.op("vector", lambda e, pb=pb: e.scalar_tensor_tensor(
                    out=zn[qg][:, pos, :], in0=tz[tzi][:], scalar=1.0, in1=PS[pb][:], op0=ALU.add, op1=ALU.mult),
                    [bPS[pb], btz[tzi]], [bzn[qg]])
            if pos == 3:
                poss = [0, 1, 2, 3] if G >= 0 else [2, 3]
                c0 = poss[0] * 128
                N = len(poss) * 128
                rdh = [bhn[hb][p_] for p_ in poss]
                if G >= 0:
                    for ct in range(4):
                        pb = nextps()
                        for kt in range(8):
                            MM(PS[pb][:], wq[:, kt, ct * 128:(ct + 1) * 128], hT[:, kt, :], kt == 0, kt == 7,
                               rdh + bwq["q"], [bPS[pb]], kt == 7)
                        ACT(QT[qg][:, ct, :], PS[pb][:], AF.Copy, [bPS[pb]], [bQT[qg]], scale=0.125)
                for ct in range(4):
                    pb = nextps()
                    for kt in range(8):
                        MM(PS[pb][:, 0:N], wq[:, kt, 512 + ct * 128:512 + (ct + 1) * 128], hT[:, kt, c0:c0 + N],
                           kt == 0, kt == 7, rdh + bwq["k"], [bPS[pb]], kt == 7)
                    CP("vector", KT[:, ct, sg3 * 512 + c0:sg3 * 512 + c0 + N], PS[pb][:, 0:N], [bPS[pb]], [bKT[sg3]])

        def key_tiles(R):
            tiles = [R - 2 + i for i in range(5)] if R <= 29 else [28, 29, 30, 31]
            info = []
            for tl in tiles:
                Gt, pt_, sgt = grp(tl)
                info.append((sgt, pt_))
            return info

        def attn_start(R):
            if not (0 <= R <= 31):
                return
            q = R % 2
            if R == 30:
                load_tab(1)
            if R == 31:
                load_tab(2)
            k.dma("sync", lambda e: e.dma_start(out=mixt[q][:, 0:512], in_=dr["mix_d"][R * 128:(R + 1) * 128, 0:512]),
                  writes=[bmixA[q]])
            k.dma("sync", lambda e: e.dma_start(out=xr[q][:], in_=dr["x_all"][HALF + R * 128:HALF + (R + 1) * 128, :]),
                  writes=[bxr[q]])
            k.dma("sync", lambda e: e.dma_start(out=ptl[q][:], in_=dr["p_own"][R * 128:(R + 1) * 128, :]),
                  writes=[bptl[q]])

        def qk(R, h):
            if not (0 <= R <= 31):
                return
            G, pos, _ = grp(R)
            qg = G % 2
            kinfo = key_tiles(R)
            nk = len(kinfo)
            hp = h // 2
            lo = 64 * (h % 2)
            hq = h % 4
            pbA = nextps()
            pbB = nextps() if nk == 5 else None
            for idx in range(nk):
                sgt, pt_ = kinfo[idx]
                if idx < 4:
                    tgt = PS[pbA][:, idx * 128:(idx + 1) * 128]; wr_ = [bPS[pbA]]
                else:
                    tgt = PS[pbB][:, 0:128]; wr_ = [bPS[pbB]]
                last = (idx == min(nk, 4) - 1) or idx == 4
                MM(tgt, KT[lo:lo + 64, hp, sgt * 512 + pt_ * 128:sgt * 512 + (pt_ + 1) * 128],
                   QT[qg][lo:lo + 64, hp, pos * 128:(pos + 1) * 128], True, True,
                   [bKT[sgt], bQT[qg]], wr_, last)
            n4 = min(nk, 4)
            ACT(PT[hq][:, 0:n4, :], PS[pbA][:, 0:n4 * 128].rearrange("p (a b) -> p a b", a=n4), AF.Exp,
                [bPS[pbA]], [bPT[hq]])
            if nk == 5:
                ACT(PT[hq][:, 4, :], PS[pbB][:, 0:128], AF.Exp, [bPS[pbB]], [bPT[hq]])
            TT("vector", PT[hq][:, 0:nk, :], PT[hq][:, 0:nk, :], tab[:, h * 5:h * 5 + nk, :], ALU.mult,
               [bPT[hq], btab], [bPT[hq]])

        def pv(R, h):
            if not (0 <= R <= 31):
                return
            kinfo = key_tiles(R)
            nk = len(kinfo)
            hq = h % 4
            ob = 6 + h // 4
            for idx in range(nk):
                sgt, pt_ = kinfo[idx]
                MM(PS[ob][:, (h % 4) * 65:(h % 4) * 65 + 65], PT[hq][:, idx, :], V[:, sgt * 4 + pt_, h, :],
                   idx == 0, idx == nk - 1, [bPT[hq], bV[sgt]], [bPS[ob]], (idx == nk - 1))

        def norm(R, quad):
            if not (0 <= R <= 31):
                return
            G, pos, _ = grp(R)
            qg = G % 2
            q = R % 2
            ob = 6 + quad
            pv_ = PS[ob][:, 0:260].rearrange("p (h e) -> p h e", e=65)
            k.op("vector", lambda e: e.reciprocal(out=rden[q][:, quad * 4:(quad + 1) * 4], in_=pv_[:, :, 64]),
                 [bPS[ob]], [brden[q]])
            TT("vector", wt[quad][:], zn[qg][:, pos, quad * 256:(quad + 1) * 256].rearrange("p (h d) -> p h d", d=64),
               rden[q][:, quad * 4:(quad + 1) * 4].unsqueeze(2).broadcast_to([128, 4, 64]), ALU.mult,
               [bzn[qg], brden[q]], [bwt[quad]])
            TT("vector", mixt[q][:, 512 + quad * 256:512 + (quad + 1) * 256].rearrange("p (h d) -> p h d", d=64),
               pv_[:, :, 0:64], wt[quad][:], ALU.mult, [bPS[ob], bwt[quad]], [bmixB[q][quad]])

        def pcopy(R):
            if not (0 <= R <= 31):
                return
            CP("vector", pb16[:], ptl[R % 2][:], [bptl[R % 2]], [bpb16])

        def tail1a(R):
            if not (0 <= R <= 31):
                return
            q = R % 2
            pb = nextps()
            pTt = PS[pb][:].bitcast(BF16).rearrange("p (a b) -> p a b", a=8)
            for kt in range(8):
                TR(pTt[:, kt, :], mixt[q][:, kt * 128:(kt + 1) * 128], [bmixA[q]] + bmixB[q], [bPS[pb]], kt == 7)
            ACT(mixT[:], pTt, AF.Copy, [bPS[pb]], [bmixT])
            pb = nextps()
            pTt = PS[pb][:].bitcast(BF16).rearrange("p (a b) -> p a b", a=8)
            for kt in range(2):
                TR(pTt[:, kt, :], pb16[:, kt * 128:(kt + 1) * 128], [bpb16], [bPS[pb]], kt == 1)
            CP("vector", pT[:], pTt[:, 0:2, :], [bPS[pb]], [bpT])

        def tail1b(R):
            if not (0 <= R <= 31):
                return
            q = R % 2
            for half in range(2):
                hs = slice(half * 512, (half + 1) * 512)
                pb = nextps()
                for kt in range(8):
                    MM(PS[pb][:], mixT[:, kt, :], wout[:, kt, hs], kt == 0, kt == 7, [bmixT] + bwout, [bPS[pb]], kt == 7)
                ACT(h1[q][:, hs], PS[pb][:], AF.Copy, [bPS[pb]], [bh1[q]])
                ACT(h1b[q][:, hs], h1[q][:, hs], AF.Square, [bh1[q]], [bh1b[q], bs2[q]], accum_out=s2[q][:, half:half + 1])
            for half in range(2):
                hs = slice(half * 512, (half + 1) * 512)
                pb = nextps()
                for kt in range(2):
                    MM(PS[pb][:], pT[:, kt, :], wple[:, kt, hs], kt == 0, kt == 1, [bpT] + bwple, [bPS[pb]], kt == 1)
                ACT(et[q][:, hs], PS[pb][:], AF.Copy, [bPS[pb]], [bet[q]])
                ACT(h1b[q][:, hs], et[q][:, hs], AF.Square, [bet[q]], [bh1b[q], bs2[q]],
                    accum_out=s2[q][:, 4 + half:5 + half])
            TT("vector", s2[q][:, 2:3], s2[q][:, 0:1], s2[q][:, 1:2], ALU.add, [bs2[q]], [bs2[q]])
            rsqrt_mean(s2[q][:, 3:4], s2[q][:, 2:3], [bs2[q]], [bs2[q]])
            for half in range(2):
                hs = slice(half * 512, (half + 1) * 512)
                k.op("vector", lambda e, hs=hs: e.scalar_tensor_tensor(
                    out=h1[q][:, hs], in0=h1[q][:, hs], scalar=s2[q][:, 3:4], in1=npost[:, hs],
                    op0=ALU.mult, op1=ALU.mult), [bh1[q], bs2[q], bnpost], [bh1[q]])
            TT("vector", h1[q][:], h1[q][:], xr[q][:], ALU.add, [bh1[q], bxr[q]], [bh1[q]])
            ACT(h1b[q][:], h1[q][:], AF.Copy, [bh1[q]], [bh1b[q]])
            TT("vector", s2[q][:, 6:7], s2[q][:, 4:5], s2[q][:, 5:6], ALU.add, [bs2[q]], [bs2[q]])
            rsqrt_mean(s2[q][:, 7:8], s2[q][:, 6:7], [bs2[q]], [bs2[q]])
            for half in range(2):
                hs = slice(half * 512, (half + 1) * 512)
                k.op("vector", lambda e, hs=hs: e.scalar_tensor_tensor(
                    out=et[q][:, hs], in0=et[q][:, hs], scalar=s2[q][:, 7:8], in1=plen[:, hs],
                    op0=ALU.mult, op1=ALU.mult), [bet[q], bs2[q], bplen], [bet[q]])

        def tail2a(R):
            if not (0 <= R <= 31):
                return
            q = R % 2
            pb = nextps()
            pTt = PS[pb][:].bitcast(BF16).rearrange("p (a b) -> p a b", a=8)
            for kt in range(8):
                TR(pTt[:, kt, :], h1b[q][:, kt * 128:(kt + 1) * 128], [bh1b[q]], [bPS[pb]], kt == 7)
            CP("vector", h1T[:], pTt, [bPS[pb]], [bh1T])

        def tail2b(R):
            if not (0 <= R <= 31):
                return
            q = R % 2
            for half in range(2):
                hs = slice(half * 512, (half + 1) * 512)
                pb = nextps()
                for kt in range(8):
                    MM(PS[pb][:], h1T[:, kt, :], wgate[:, kt, hs], kt == 0, kt == 7, [bh1T] + bwgate, [bPS[pb]], kt == 7)
                ACT(sg[q][:, hs], PS[pb][:], AF.Tanh, [bPS[pb]], [bsg[q]], scale=0.5)
            k.op("vector", lambda e: e.scalar_tensor_tensor(
                out=et[q][:], in0=sg[q][:], scalar=1.0, in1=et[q][:], op0=ALU.add, op1=ALU.mult),
                [bet[q], bsg[q]], [bet[q]])
            k.op("vector", lambda e: e.scalar_tensor_tensor(
                out=et[q][:], in0=et[q][:], scalar=0.5, in1=h1[q][:], op0=ALU.mult, op1=ALU.add),
                [bet[q], bh1[q]], [bet[q]])
            tok = k.dma("gpsimd", lambda e: e.dma_start(out=dr["out"][R * 128:(R + 1) * 128, :], in_=et[q][:]),
                        reads=[bet[q]])
            out_toks.append(tok)

        def slot(i, h):
            qk(i, h)
            Hh = 8 * i + h - 3
            Ri, hh = Hh // 8, Hh % 8
            pv(Ri, hh)
            if hh == 3:
                norm(Ri, 0)
            if hh == 7:
                norm(Ri, 1)

        for i in range(-10, 35):
            pre(i + 8)
            attn_start(i)
            pcopy(i - 1)
            slot(i, 0); slot(i, 1)
            tail2a(i - 2)
            slot(i, 2); slot(i, 3)
            pre_act(i + 8)
            trn(i + 7)
            slot(i, 4)
            tail1a(i - 1)
            slot(i, 5)
            tail2b(i - 2)
            slot(i, 6)
            tail1b(i - 1)
            slot(i, 7)
            mmv(i + 6)
        for tok in out_toks[-4:]:
            k.wait_tok("gpsimd", tok)
        k.emit()


def build(stage=99):
    nc = bass.Bass("TRN2", target_bir_lowering=False)
    dr = {}

    def din(name, shape, dt=F32):
        dr[name] = nc.dram_tensor(name, list(shape), dt, kind="ExternalInput").ap()

    def dout(name, shape, dt=F32):
        dr[name] = nc.dram_tensor(name, list(shape), dt, kind="ExternalOutput").ap()

    def dint(name, shape, dt):
        dr[name] = nc.dram_tensor(name, list(shape), dt, kind="Internal").ap()

    din("x_all", [NTOK, D])
    din("w_in", [D, 3072])
    din("npre", [128, 8])
    din("ident", [128, 128])
    dint("zs_d", [4, 128, 8 * 512], BF16)
    din("ssm_par", [2, 128, 2144])
    din("ssm_c", [128, NK + 32 + 1 + 4 * 128])
    if stage == 2:
        dout("dbg_y", [128, 32 * 512])
    din("w_glu", [512, 512])
    din("b_glu", [1, 512])
    dint("mix_d", [HALF, D], BF16)
    if stage == 3:
        dout("dbg_ms", [128, 8 * 512], BF16)
    din("w_out", [D, D])
    din("w_gate", [D, D])
    din("w_ple", [256, D])
    din("npost_b", [128, D])
    din("plen_b", [128, D])
    din("na_tab", [3, 8, 5, 128, 128])
    din("p_own", [HALF, 256])
    if stage >= 4:
        dout("out", [HALF, D])
    if stage == 1:
        dout("dbg_u", [128, 32 * 1024])
        dout("dbg_zs", [128, 8 * 512])

    with ExitStack() as st0:
        k = K(nc, st0)
        PS = [st0.enter_context(nc.psum_tensor("ps%d" % i, [128, 512], F32)) for i in range(8)]
        bPS = [Buf("ps%d" % i) for i in range(8)]
        ident = sbt(nc, st0, "ident", [128, 128], BF16)
        bident = Buf("ident")
        k.dma("gpsimd", lambda e: e.dma_start(out=ident[:], in_=dr["ident"]), writes=[bident])
        stU = ExitStack()
        U = sbt(nc, stU, "U", [128, 32, 1024], BF16)
        bUblk = [Buf("U%d" % i) for i in range(8)]

        with ExitStack() as st:
          if stage != 5:
              npre = sbt(nc, st, "npre", [128, 8], F32); bnpre = Buf()
              k.dma("sync", lambda e: e.dma_start(out=npre[:], in_=dr["npre"]), writes=[bnpre])
              wu = sbt(nc, st, "wu", [128, 8, 1024], BF16); bwu = Buf(); bwul = [Buf() for _ in range(8)]
              wsrc = dr["w_in"].rearrange("(kt f) c -> f kt c", f=128)
              for kt in range(8):
                  k.dma("gpsimd", lambda e, kt=kt: e.dma_start(out=wu[:, kt, :], in_=wsrc[:, kt, 0:1024]),
                        writes=[bwul[kt]], owner=bwu)
              xin = [sbt(nc, st, "xin%d" % i, [128, 2, D], F32) for i in range(2)]
              bxin = [Buf() for _ in range(2)]
              junk = sbt(nc, st, "junk", [128, D], BF16); bjunk = Buf()
              xs = [sbt(nc, st, "xs%d" % i, [128, D], BF16) for i in range(2)]
              bxs = [Buf() for _ in range(2)]
              ss = sbt(nc, st, "ss", [128, 64], F32)
              bss = [Buf() for _ in range(64)]
              rs = sbt(nc, st, "rs", [128, 64], F32)
              brs = [Buf() for _ in range(64)]
              hnT = [sbt(nc, st, "hnT%d" % i, [128, 8, 8, 128], BF16) for i in range(2)]
              bhn = [[Buf() for _ in range(8)] for _ in range(2)]
              Tt = [sbt(nc, st, "Tt%d" % i, [128, 32, 8, 16], BF16) for i in range(2)]
              bTt = [[Buf() for _ in range(8)] for _ in range(2)]
              zst = [sbt(nc, st, "zst%d" % i, [128, 8, 512], BF16) for i in range(2)]
              bzs = [Buf(), Buf()]
              nhalf = sbt(nc, st, "nhalfA", [128, 1], F32); bnhalf = Buf()
              k.op("gpsimd", lambda e: e.memset(nhalf[:], -0.5), [], [bnhalf])
              tza = [sbt(nc, st, "tza%d" % i, [128, 512], BF16) for i in range(2)]
              btza = [Buf() for _ in range(2)]
              xv = dr["x_all"].rearrange("(blk c j) f -> blk c j f", c=128, j=8)
              pi = [0]

              def nps():
                  pb = pi[0] % 8
                  pi[0] += 1
                  return pb

              def a_pre(sl):
                  blk, j = sl // 8, sl % 8
                  xb = (sl // 2) % 2
                  if j % 2 == 0:
                      k.dma("sync", lambda e: e.dma_start(out=xin[xb][:], in_=xv[blk, :, j:j + 2, :]), writes=[bxin[xb]])
                  xt = xin[xb][:, j % 2, :]
                  sx = sl % 2
                  k.op("scalar", lambda e: e.activation(
                      out=xs[sx][:], in_=xt, func=AF.Square, accum_out=ss[:, sl:sl + 1]),
                      reads=[bxin[xb]], writes=[bxs[sx], bss[sl]])
                  k.op("gpsimd", lambda e: e.tensor_scalar(
                      out=rs[:, sl:sl + 1], in0=ss[:, sl:sl + 1], scalar1=1.0 / D, scalar2=EPS,
                      op0=ALU.mult, op1=ALU.add), reads=[bss[sl]], writes=[brs[sl]])
                  k.op("gpsimd", lambda e: e.tensor_tensor(
                      out=rs[:, sl:sl + 1], in0=rs[:, sl:sl + 1], in1=nhalf[:], op=ALU.pow),
                      reads=[brs[sl], bnhalf], writes=[brs[sl]])
                  k.op("scalar", lambda e: e.activation(
                      out=xs[sx][:], in_=xt, func=AF.Copy, scale=rs[:, sl:sl + 1]),
                      reads=[bxin[xb], brs[sl]], writes=[bxs[sx]])

              def a_trn(sl):
                  blk, j = sl // 8, sl % 8
                  hb = blk % 2
                  sx = sl % 2
                  pb = nps()
                  pT = PS[pb][:].bitcast(BF16).rearrange("p (a b) -> p a b", a=8)
                  for kt in range(8):
                      k.op("tensor", lambda e, kt=kt: e.transpose(
                          out=pT[:, kt, :], in_=xs[sx][:, kt * 128:(kt + 1) * 128], identity=ident[:]),
                          reads=[bxs[sx], bident], writes=[bPS[pb]], sig=(kt == 7))
                  k.op("vector", lambda e: e.tensor_tensor(
                      out=hnT[hb][:, :, j, :], in0=pT, in1=npre[:].unsqueeze(2).broadcast_to([128, 8, 128]),
                      op=ALU.mult), reads=[bPS[pb], bnpre], writes=[bhn[hb][j]])

              def a_mm(sl):
                  blk, j = sl // 8, sl % 8
                  hb = blk % 2
                  own = blk >= 4
                  pb = nps()
                  for kt in range(8):
                      k.op("tensor", lambda e, kt=kt: e.matmul(
                          PS[pb][:], lhsT=hnT[hb][:, kt, j, :], rhs=wu[:, kt, 0:512],
                          start=(kt == 0), stop=(kt == 7)),
                          reads=[bhn[hb][j]] + bwul, writes=[bPS[pb]], sig=(kt == 7))
                  k.op("vector", lambda e: e.tensor_copy(
                      out=Tt[hb][:, :, j, :], in_=PS[pb][:].rearrange("p (g h) -> p g h", h=16)),
                      reads=[bPS[pb]], writes=[bTt[hb][j]])
                  if own:
                      pb2 = nps()
                      for kt in range(8):
                          k.op("tensor", lambda e, kt=kt: e.matmul(
                              PS[pb2][:], lhsT=hnT[hb][:, kt, j, :], rhs=wu[:, kt, 512:1024],
                              start=(kt == 0), stop=(kt == 7)),
                              reads=[bhn[hb][j]] + bwul, writes=[bPS[pb2]], sig=(kt == 7))
                      k.op("scalar", lambda e: e.activation(
                          out=tza[j % 2][:], in_=PS[pb2][:], func=AF.Tanh, scale=0.5),
                          reads=[bPS[pb2]], writes=[btza[j % 2]])
                      k.op("vector", lambda e: e.scalar_tensor_tensor(
                          out=zst[hb][:, j, :], in0=tza[j % 2][:], scalar=1.0, in1=PS[pb2][:],
                          op0=ALU.add, op1=ALU.mult), reads=[bPS[pb2], btza[j % 2]], writes=[bzs[hb]])

              def a_fin(blk):
                  hb = blk % 2
                  if blk >= 4:
                      k.dma("sync", lambda e: e.dma_start(
                          out=dr["zs_d"][blk - 4], in_=zst[hb][:].rearrange("p a b -> p (a b)")), reads=[bzs[hb]])
                  for g4 in range(4):
                      pb = nps()
                      pT = PS[pb][:].bitcast(BF16).rearrange("p (a b) -> p a b", a=8)
                      for gg in range(8):
                          g = g4 * 8 + gg
                          k.op("tensor", lambda e, gg=gg, g=g, pT=pT: e.transpose(
                              out=pT[:, gg, :], in_=Tt[hb][:, g, :, :].rearrange("p a b -> p (a b)"), identity=ident[:]),
                              reads=bTt[hb] + [bident], writes=[bPS[pb]], sig=(gg == 7))
                      hf, b4 = blk // 4, blk % 4
                      uo = U[:, g4 * 8:(g4 + 1) * 8, hf * 512:(hf + 1) * 512].rearrange(
                          "p g (j m) -> p g j m", j=8)[:, :, :, b4 * 16:(b4 + 1) * 16]
                      k.op("vector", lambda e, uo=uo, pT=pT: e.tensor_copy(
                          out=uo, in_=pT.rearrange("p g (m j) -> p g j m", j=8)),
                          reads=[bPS[pb]], writes=[bUblk[blk]])

              a_pre(0)
              for s_ in range(72):
                  if s_ + 1 < 64:
                      a_pre(s_ + 1)
                  if s_ < 64:
                      a_trn(s_)
                  if s_ >= 8:
                      a_mm(s_ - 8)
                      if (s_ - 8) % 8 == 7:
                          a_fin((s_ - 8) // 8)
              if stage == 1:
                  t1 = k.dma("gpsimd", lambda e: e.dma_start(
                      out=dr["dbg_u"], in_=U[:].rearrange("p a b -> p (a b)")), reads=bUblk)
                  t2 = k.dma("gpsimd", lambda e: e.dma_start(
                      out=dr["dbg_zs"], in_=zst[1][:].rearrange("p a b -> p (a b)")), reads=[bzs[1]])
                  k.wait_tok("gpsimd", t1)
                  k.wait_tok("gpsimd", t2)
              k.emit()
        if stage >= 2 and stage != 5:
            ssm_phase(nc, k, st0, dr, PS, bPS, ident, bident, U, bUblk, stage)
        if stage >= 3 and stage != 5:
            pass_S(nc, k, dr, PS, bPS, ident, bident, U, bUblk, stage)
        stU.close()
        if stage >= 4:
            stW = ExitStack()
            W = prefetch_N(nc, k, dr, stW)
            pass_N(nc, k, dr, PS, bPS, ident, bident, stage, W)
            stW.close()
    return nc


def core_inputs(inp, b, s):
    x = inp["x"][b]
    if s == 1:
        x_all = x
    else:
        x_all = x[::-1]
    d = {}
    d["x_all"] = np.ascontiguousarray(x_all, dtype=np.float32)
    d["w_in"] = np.ascontiguousarray(inp["w_in"][0], dtype=np.float32)
    d["npre"] = np.ascontiguousarray(inp["norm_pre"][0].reshape(8, 128).T, dtype=np.float32)
    d["ident"] = np.eye(128, dtype=np.float32)
    par = np.zeros((2, 128, 2144), np.float32)
    for dd in range(2):
        sd = dd if s == 1 else 1 - dd
        a_re = inp["ssm_a_re"][0, sd]; a_im = inp["ssm_a_im"][0, sd]
        ldt = inp["ssm_log_dt"][0, sd]
        b_re = inp["ssm_b_re"][0, sd]; b_im = inp["ssm_b_im"][0, sd]
        c_re = inp["ssm_c_re"][0, sd]; c_im = inp["ssm_c_im"][0, sd]
        blk = np.concatenate([
            a_re.T, a_im.T, np.broadcast_to(ldt[None, :], (64, 32)),
            b_re.transpose(1, 0, 2).reshape(64, 512), b_im.transpose(1, 0, 2).reshape(64, 512),
            c_re.transpose(2, 0, 1).reshape(64, 512), c_im.transpose(2, 0, 1).reshape(64, 512)], axis=1)
        par[dd, 0:64] = blk
        par[dd, 64:128] = blk
    d["ssm_par"] = par
    cst = np.zeros((128, NK + 33 + 512), np.float32)
    cst[:, 0:NK] = np.asarray(KVALS, np.float32)[None, :]
    cst[:, NK:NK + 32] = np.tile(inp["ssm_d"][0].T, (8, 1))
    cst[0:64, NK + 32] = 1.0
    cst[64:128, NK + 32] = -1.0
    c0 = NK + 33
    r = np.arange(128)
    cst[:, c0:c0 + 128] = (r[:, None] % 64 == r[None, :] % 64)
    cst[:, c0 + 128:c0 + 256] = (r[None, :] // 16 >= r[:, None] // 16)
    cst[:, c0 + 256:c0 + 384] = (r[:, None] // 16 >= r[None, :] // 16)
    cst[:, c0 + 384:c0 + 512] = np.eye(128)
    d["ssm_c"] = cst
    d["w_glu"] = np.ascontiguousarray(inp["w_glu"][0], dtype=np.float32)
    d["b_glu"] = np.ascontiguousarray(inp["b_glu"][0][None, :], dtype=np.float32)
    d["w_out"] = np.ascontiguousarray(inp["w_out"][0], dtype=np.float32)
    d["w_gate"] = np.ascontiguousarray(inp["w_ple_gate"][0], dtype=np.float32)
    d["w_ple"] = np.ascontiguousarray(inp["w_ple"][0], dtype=np.float32)
    d["npost_b"] = np.ascontiguousarray(np.broadcast_to(inp["norm_post"][0][None, :], (128, D)), dtype=np.float32)
    d["plen_b"] = np.ascontiguousarray(np.broadcast_to(inp["ple_norm"][0][None, :], (128, D)), dtype=np.float32)
    p = inp["p"][0, b]
    p_own = p[HALF:] if s == 1 else p[:HALF][::-1]
    d["p_own"] = np.ascontiguousarray(p_own, dtype=np.float32)
    d["na_tab"] = bias_tables(inp["na_rpb"][0], s)
    return d


def bias_tables(rpb, s):
    NEG = np.float32(-30000.0)
    tab = np.full((3, 8, 5, 128, 128), NEG, np.float32)
    kb = (np.arange(128) // 64)[:, None]
    kc = (np.arange(128) % 64)[:, None]
    qa = (np.arange(128) // 64)[None, :]
    qc = (np.arange(128) % 64)[None, :]
    for kind, R in enumerate([10, 30, 31]):
        tiles = [R - 2 + i for i in range(5)] if R <= 29 else [28, 29, 30, 31]
        for idx, tl in enumerate(tiles):
            kl = 2 * tl + kb
            ql = 2 * R + qa
            if s == 1:
                rk, rq, ck, cq = 64 + kl, 64 + ql, kc, qc
            else:
                rk, rq, ck, cq = 63 - kl, 63 - ql, 63 - kc, 63 - qc
            rs = np.clip(rq - 4, 0, 120)
            cs = np.clip(cq - 8, 0, 48)
            valid = (rk >= rs) & (rk < rs + 8) & (ck >= cs) & (ck < cs + 16)
            dr_ = np.clip(rk - rq + 7, 0, 14)
            dc_ = np.clip(ck - cq + 15, 0, 30)
            vals = rpb[:, dr_, dc_]
            tab[kind, :, idx] = np.where(valid[None], vals, NEG)
    return tab


_NC_CACHE = {}


def kernel(**inputs):
    inp = {k_: np.asarray(v) for k_, v in inputs.items()}
    if "nc" not in _NC_CACHE:
        _NC_CACHE["nc"] = build(stage=4)
    nc = _NC_CACHE["nc"]
    in_maps = []
    for c in range(8):
        in_maps.append(core_inputs(inp, c // 2, c % 2))
    res = run_bass_kernel_spmd(nc, in_maps, core_ids=list(range(8)))
    out = np.zeros((4, NTOK, D), np.float32)
    for c in range(8):
        b, s = c // 2, c % 2
        o = res.results[c]["out"]
        if s == 1:
            out[b, HALF:] = o
        else:
            out[b, :HALF] = o[::-1]
    return out
```

```python
import numpy as np
from contextlib import ExitStack
import concourse.bass as bass
import concourse.mybir as mybir
from concourse.bass_utils import run_bass_kernel_spmd

F32 = mybir.dt.float32
BF16 = mybir.dt.bfloat16
I32 = mybir.dt.int32
AF = mybir.ActivationFunctionType
ALU = mybir.AluOpType

NTOK = 8192
HALF = 4096
D = 1024
EPS = 1e-6
TWO_PI = 6.283185307179586

KSEG = {
    "A": [7 - i for i in range(8)],
    "B": [i for i in range(8)],
    "C": [i + 1 for i in range(8)],
    "D": [8 - i for i in range(8)],
    "E": [-i for i in range(8)],
    "G": [8 * (i + 1) for i in range(7)],
    "H": [64],
}
KOFF = {}
KVALS = []
for _n in "ABCDEGH":
    KOFF[_n] = len(KVALS)
    KVALS += KSEG[_n]
NK = len(KVALS)


class Buf:
    def __init__(self, name=""):
        self.name = name
        self.w = None
        self.r = {}
        self.dsem = None
        self.dcount = 0


class K:
    ENG = ["tensor", "vector", "scalar", "gpsimd", "sync"]

    def __init__(self, nc, stack):
        self.nc = nc
        self.stack = stack
        self.ops = {e: [] for e in self.ENG}
        self.cnt = {e: 0 for e in self.ENG}
        self.waited = {e: {} for e in self.ENG}
        self.sems = {}
        for e in self.ENG:
            self.sems[e] = stack.enter_context(nc.semaphore("s_" + e))
        self.nd = 0
        self.dlast = {}

    def _wait(self, eng, tok):
        if tok is None:
            return
        key, val = tok
        if key == eng and val > self.cnt[eng]:
            return
        if self.waited[eng].get(key, 0) >= val:
            return
        self.waited[eng][key] = val
        self.ops[eng].append(("w", key, val))

    def _deps(self, eng, reads, writes, same_eng_war=False):
        need = {}

        def add(tok):
            if tok is not None and need.get(tok[0], 0) < tok[1]:
                need[tok[0]] = tok[1]
        for b in reads:
            add(b.w)
        for b in writes:
            add(b.w)
            for key, val in b.r.items():
                add((key, val))
        for key, val in need.items():
            self._wait(eng, (key, val))

    def op(self, eng, fn, reads=(), writes=(), sig=True):
        self._deps(eng, reads, writes)
        if sig:
            self.cnt[eng] += 1
            tok = (eng, self.cnt[eng])
            self.ops[eng].append(("o", fn, eng, 1))
            for b in reads:
                b.r[eng] = tok[1]
            for b in writes:
                b.w = tok
                b.r = {}
        else:
            self.ops[eng].append(("o", fn, None, 0))
            nxt = self.cnt[eng] + 1
            for b in reads:
                b.r[eng] = nxt
            for b in writes:
                b.w = (eng, nxt)
                b.r = {}

    def dma(self, eng, fn, reads=(), writes=(), owner=None):
        self._deps(eng, reads, writes, same_eng_war=True)
        if owner is None:
            owner = (list(writes) + list(reads))[0]
        if owner.dsem is None:
            owner.dsem = "d%d" % self.nd
            self.nd += 1
            self.sems[owner.dsem] = self.stack.enter_context(self.nc.semaphore(owner.dsem))
        owner.dcount += 16
        tok = (owner.dsem, owner.dcount)
        self.dlast[owner.dsem] = owner.dcount
        self.ops[eng].append(("o", fn, owner.dsem, 16))
        for b in reads:
            b.r[tok[0]] = tok[1]
        for b in writes:
            b.w = tok
            b.r = {}
        return tok

    def wait_tok(self, eng, tok):
        self._wait(eng, tok)

    def check_deadlock(self):
        if not hasattr(self, "semval"):
            self.semval = {}
        pos = {e: 0 for e in self.ENG}
        progress = True
        while progress:
            progress = False
            for e in self.ENG:
                lst = self.ops[e]
                while pos[e] < len(lst):
                    it = lst[pos[e]]
                    if it[0] == "w":
                        if self.semval.get(it[1], 0) >= it[2]:
                            pos[e] += 1
                            progress = True
                        else:
                            break
                    else:
                        if it[2] is not None:
                            self.semval[it[2]] = self.semval.get(it[2], 0) + it[3]
                        pos[e] += 1
                        progress = True
        for e in self.ENG:
            if pos[e] < len(self.ops[e]):
                it = self.ops[e][pos[e]]
                raise RuntimeError("deadlock: engine %s stuck at item %d/%d waiting %s>=%s (have %s)" % (
                    e, pos[e], len(self.ops[e]), it[1], it[2], self.semval.get(it[1], 0)))

    def emit(self):
        nc = self.nc
        for key, val in self.dlast.items():
            self._wait("sync", (key, val))
        self.check_deadlock()
        with nc.Block() as block:
            for e in self.ENG:
                lst = self.ops[e]
                if not lst:
                    continue

                def body(engine, lst=lst):
                    for it in lst:
                        if it[0] == "w":
                            engine.wait_ge(self.sems[it[1]], it[2])
                        else:
                            ins = it[1](engine)
                            if it[2] is not None:
                                ins.then_inc(self.sems[it[2]], it[3])
                getattr(block, e)(body)
        self.ops = {e: [] for e in self.ENG}


class Ctx:
    pass


def sbt(nc, st, name, shape, dt):
    return st.enter_context(nc.sbuf_tensor("sb_" + name, list(shape), dt))


def ssm_phase(nc, k, st0, dr, PS, bPS, ident, bident, U, bUblk, stage):
    def TT(eng, out, a, b, op, rd, wr):
        k.op(eng, lambda e: e.tensor_tensor(out=out, in0=a, in1=b, op=op), rd, wr)

    def TS(eng, out, a, s1, op0, rd, wr, s2=None, op1=None):
        if op1 is None:
            k.op(eng, lambda e: e.tensor_scalar(out=out, in0=a, scalar1=s1, scalar2=None, op0=op0), rd, wr)
        else:
            k.op(eng, lambda e: e.tensor_scalar(out=out, in0=a, scalar1=s1, scalar2=s2, op0=op0, op1=op1), rd, wr)

    def ACT(out, in_, func, rd, wr, **kw):
        k.op("scalar", lambda e: e.activation(out=out, in_=in_, func=func, **kw), rd, wr)

    def CP(eng, out, in_, rd, wr):
        k.op(eng, lambda e: e.tensor_copy(out=out, in_=in_), rd, wr)

    def MM(out, lhsT, rhs, start, stop, rd, wr, sig):
        k.op("tensor", lambda e: e.matmul(out, lhsT=lhsT, rhs=rhs, start=start, stop=stop), rd, wr, sig=sig)

    pi = [0]

    def nextps():
        pb = pi[0] % 8
        pi[0] += 1
        return pb

    with ExitStack() as stS:
        cst = sbt(nc, stS, "ssmc", [128, NK + 33 + 512], F32); bcst = Buf()
        k.dma("sync", lambda e: e.dma_start(out=cst[:], in_=dr["ssm_c"]), writes=[bcst])
        kv = cst[:, 0:NK]
        dvec = cst[:, NK:NK + 32]
        sgn = cst[:, NK + 32:NK + 33]
        c0 = NK + 33
        IIf = cst[:, c0:c0 + 128]
        maskF = cst[:, c0 + 128:c0 + 256]
        maskB = cst[:, c0 + 256:c0 + 384]
        identF = cst[:, c0 + 384:c0 + 512]
        II = sbt(nc, stS, "II", [128, 128], BF16); bII = Buf()
        CP("vector", II[:], IIf, [bcst], [bII])
        identJ = sbt(nc, stS, "identJ", [128, 128], BF16); bidJ = Buf()
        TS("vector", identJ[:], identF, sgn, ALU.mult, [bcst], [bidJ])
        sc = sbt(nc, stS, "sc", [128, 4, 7, 64], F32); bsc = Buf()
        AA = sbt(nc, stS, "AA", [64, 2, 2, 32], F32); bAA = Buf()
        AB = sbt(nc, stS, "AB", [64, 2, 2, 32], F32); bAB = Buf()
        T8 = sbt(nc, stS, "T8", [128, 32, 128], BF16); bT8 = [Buf() for _ in range(32)]
        BfS = sbt(nc, stS, "BfS", [128, 64, 128], BF16); bBfS = [Buf() for _ in range(2)]
        CfS = sbt(nc, stS, "CfS", [128, 64, 128], BF16); bCfS = [Buf() for _ in range(2)]

        with ExitStack() as st:
            par = sbt(nc, st, "par", [128, 2144], F32); bpar = Buf()
            npi = sbt(nc, st, "npi", [128, 1], F32); bnpi = Buf()
            k.op("vector", lambda e: e.memset(npi[:], -3.141592653589793), [], [bnpi])
            NKG = NK * 32
            tl = {}
            for nm in ["KLre", "KLim", "y", "yf", "Wre", "Wim"]:
                tl[nm] = (sbt(nc, st, nm, [128, NK, 32], F32), Buf())
            yi = sbt(nc, st, "yi", [128, NK, 32], I32); byi = Buf()
            sm = {}
            for nm in ["dt", "lre", "lim", "nre", "den", "t1", "t2", "cre", "cim"]:
                sm[nm] = (sbt(nc, st, "s_" + nm, [128, 32], F32), Buf())
            bbre = sbt(nc, st, "bbre", [128, 32, 16], F32); bbbre = Buf()
            bbim = sbt(nc, st, "bbim", [128, 32, 16], F32); bbbim = Buf()
            tb = sbt(nc, st, "tb", [128, 32, 16], F32); btb = Buf()
            P1 = sbt(nc, st, "P1", [128, 32, 16], F32); bP1 = Buf()
            P2 = sbt(nc, st, "P2", [128, 32, 16], F32); bP2 = Buf()
            P1c = sbt(nc, st, "P1c", [128, 32, 16], F32); bP1c = Buf()
            P2c = sbt(nc, st, "P2c", [128, 32, 16], F32); bP2c = Buf()
            m1 = sbt(nc, st, "m1", [128, 8, 8, 16], F32); bm1 = Buf()
            m2 = sbt(nc, st, "m2", [128, 8, 8, 16], F32); bm2 = Buf()
            BnS = sbt(nc, st, "BnS", [128, 32, 128], BF16); bBnS = Buf()
            CpS = sbt(nc, st, "CpS", [128, 32, 128], BF16); bCpS = Buf()
            tmpm = [sbt(nc, st, "tmpm%d" % i, [128, 128], F32) for i in range(2)]
            btmpm = [Buf() for _ in range(2)]
            for d in range(2):
                k.dma("sync", lambda e, d=d: e.dma_start(out=par[:], in_=dr["ssm_par"][d]), writes=[bpar])
                ar = par[:, 0:32]; ai = par[:, 32:64]; ldt = par[:, 64:96]
                br = par[:, 96:608].rearrange("p (g h) -> p g h", h=16)
                bi = par[:, 608:1120].rearrange("p (g h) -> p g h", h=16)
                cr = par[:, 1120:1632].rearrange("p (g h) -> p g h", h=16)
                ci = par[:, 1632:2144].rearrange("p (g h) -> p g h", h=16)
                dt, bdt = sm["dt"]; lre, blre = sm["lre"]; lim, blim = sm["lim"]
                ACT(dt[:], ldt, AF.Exp, [bpar], [bdt])
                TT("vector", lre[:], dt[:], ar, ALU.mult, [bdt, bpar], [blre])
                TT("vector", lim[:], dt[:], ai, ALU.mult, [bdt, bpar], [blim])
                KLre, bKLre = tl["KLre"]; KLim, bKLim = tl["KLim"]
                kvb = kv.unsqueeze(2).broadcast_to([128, NK, 32])
                TT("vector", KLre[:], kvb, lre[:].unsqueeze(1).broadcast_to([128, NK, 32]), ALU.mult,
                   [bcst, blre], [bKLre])
                TT("vector", KLim[:], kvb, lim[:].unsqueeze(1).broadcast_to([128, NK, 32]), ALU.mult,
                   [bcst, blim], [bKLim])
                E, bE = KLre, bKLre
                ACT(E[:], KLre[:], AF.Exp, [bKLre], [bE])
                y, by = tl["y"]; yf, byf = tl["yf"]
                for which, shift in (("Wim", 64.5), ("Wre", 64.75)):
                    W, bW = tl[which]
                    TS("vector", y[:], KLim[:], 1.0 / TWO_PI, ALU.mult, [bKLim], [by], s2=shift, op1=ALU.add)
                    CP("vector", yi[:], y[:], [by], [byi])
                    CP("vector", yf[:], yi[:], [byi], [byf])
                    TT("vector", y[:], y[:], yf[:], ALU.subtract, [by, byf], [by])
                    TS("vector", yf[:], y[:], 0.0, ALU.is_lt, [by], [byf])
                    TT("vector", y[:], y[:], yf[:], ALU.add, [by, byf], [by])
                    TS("vector", y[:], y[:], 1e-6, ALU.max, [by], [by], s2=1.0 - 1e-6, op1=ALU.min)
                    ACT(yf[:], y[:], AF.Sin, [by, bnpi], [byf], scale=TWO_PI, bias=npi[:])
                    TT("vector", W[:], E[:], yf[:], ALU.mult, [bE, byf], [bW])
                Wre, bWre = tl["Wre"]; Wim, bWim = tl["Wim"]
                G0 = KOFF["G"]
                gds = slice(d * 32, (d + 1) * 32)
                lo = slice(0, 64); hi = slice(64, 128)
                rdW = [bWre, bWim]
                CP("vector", sc[lo, 0, :, gds], Wre[lo, G0:G0 + 7, :], rdW, [bsc])
                CP("vector", sc[hi, 0, :, gds], Wim[hi, G0:G0 + 7, :], rdW, [bsc])
                TS("vector", sc[lo, 1, :, gds], Wim[lo, G0:G0 + 7, :], -1.0, ALU.mult, rdW, [bsc])
                CP("vector", sc[hi, 1, :, gds], Wre[hi, G0:G0 + 7, :], rdW, [bsc])
                CP("vector", sc[lo, 2, :, gds], Wre[lo, G0:G0 + 7, :], rdW, [bsc])
                TS("vector", sc[hi, 2, :, gds], Wim[hi, G0:G0 + 7, :], -1.0, ALU.mult, rdW, [bsc])
                TS("vector", sc[lo, 3, :, gds], Wim[lo, G0:G0 + 7, :], -1.0, ALU.mult, rdW, [bsc])
                TS("vector", sc[hi, 3, :, gds], Wre[hi, G0:G0 + 7, :], -1.0, ALU.mult, rdW, [bsc])
                H0 = KOFF["H"]
                CP("vector", AA[:, d, 0, :], Wre[lo, H0, :], rdW, [bAA])
                CP("vector", AA[:, d, 1, :], Wre[lo, H0, :], rdW, [bAA])
                TS("vector", AB[:, d, 0, :], Wim[lo, H0, :], -1.0, ALU.mult, rdW, [bAB])
                CP("vector", AB[:, d, 1, :], Wim[lo, H0, :], rdW, [bAB])
                k1 = KOFF["B"] + 1
                nre, bnre = sm["nre"]; den, bden = sm["den"]; t1, bt1 = sm["t1"]; t2, bt2 = sm["t2"]
                cre, bcre = sm["cre"]; cim, bcim = sm["cim"]
                TS("vector", nre[:], Wre[:, k1, :], -1.0, ALU.add, rdW, [bnre])
                TT("vector", den[:], ar, ar, ALU.mult, [bpar], [bden])
                TT("vector", t1[:], ai, ai, ALU.mult, [bpar], [bt1])
                TT("vector", den[:], den[:], t1[:], ALU.add, [bden, bt1], [bden])
                k.op("vector", lambda e, den=den: e.reciprocal(out=den[:], in_=den[:]), [bden], [bden])
                TT("vector", t1[:], nre[:], ar, ALU.mult, [bnre, bpar], [bt1])
                TT("vector", t2[:], Wim[:, k1, :], ai, ALU.mult, rdW + [bpar], [bt2])
                TT("vector", t1[:], t1[:], t2[:], ALU.add, [bt1, bt2], [bt1])
                TT("vector", cre[:], t1[:], den[:], ALU.mult, [bt1, bden], [bcre])
                TT("vector", t1[:], Wim[:, k1, :], ar, ALU.mult, rdW + [bpar], [bt1])
                TT("vector", t2[:], nre[:], ai, ALU.mult, [bnre, bpar], [bt2])
                TT("vector", t1[:], t1[:], t2[:], ALU.subtract, [bt1, bt2], [bt1])
                TT("vector", cim[:], t1[:], den[:], ALU.mult, [bt1, bden], [bcim])
                creb = cre[:].unsqueeze(2).broadcast_to([128, 32, 16])
                cimb = cim[:].unsqueeze(2).broadcast_to([128, 32, 16])
                TT("vector", bbre[:], creb, br, ALU.mult, [bcre, bpar], [bbbre])
                TT("vector", tb[:], cimb, bi, ALU.mult, [bcim, bpar], [btb])
                TT("vector", bbre[:], bbre[:], tb[:], ALU.subtract, [bbbre, btb], [bbbre])
                TT("vector", bbim[:], creb, bi, ALU.mult, [bcre, bpar], [bbbim])
                TT("vector", tb[:], cimb, br, ALU.mult, [bcim, bpar], [btb])
                TT("vector", bbim[:], bbim[:], tb[:], ALU.add, [bbbim, btb], [bbbim])
                rb = [bbbre, bbbim]
                CP("vector", P1[lo], bbre[lo], rb, [bP1])
                CP("vector", P1[hi], bbim[hi], rb, [bP1])
                TS("vector", P2[lo], bbim[lo], -1.0, ALU.mult, rb, [bP2])
                CP("vector", P2[hi], bbre[hi], rb, [bP2])
                CP("vector", P1c[lo], cr[lo], [bpar], [bP1c])
                TS("vector", P1c[hi], ci[hi], -1.0, ALU.mult, [bpar], [bP1c])
                TS("vector", P2c[lo], ci[lo], -1.0, ALU.mult, [bpar], [bP2c])
                TS("vector", P2c[hi], cr[hi], -1.0, ALU.mult, [bpar], [bP2c])
                segs = {0: ("A", "C", "E", "B"), 1: ("B", "D", "B", "E")}[d]
                jobs = [(BfS[:, gds, :], bBfS[d], segs[0], P1, P2, bP1, bP2),
                        (CfS[:, gds, :], bCfS[d], segs[1], P1c, P2c, bP1c, bP2c),
                        (BnS[:], bBnS, segs[2], P1, P2, bP1, bP2),
                        (CpS[:], bCpS, segs[3], P1c, P2c, bP1c, bP2c)]
                for ji, (outap, bout, seg, Pa, Pb, bPa, bPb) in enumerate(jobs):
                    o = KOFF[seg]
                    for half in range(4):
                        g0 = half * 8
                        eng = "vector"
                        wre = Wre[:, o:o + 8, g0:g0 + 8].rearrange("p k g -> p g k").unsqueeze(3).broadcast_to([128, 8, 8, 16])
                        wim = Wim[:, o:o + 8, g0:g0 + 8].rearrange("p k g -> p g k").unsqueeze(3).broadcast_to([128, 8, 8, 16])
                        pa = Pa[:, g0:g0 + 8, :].unsqueeze(2).broadcast_to([128, 8, 8, 16])
                        pb_ = Pb[:, g0:g0 + 8, :].unsqueeze(2).broadcast_to([128, 8, 8, 16])
                        TT(eng, m1[:], wre, pa, ALU.mult, rdW + [bPa], [bm1])
                        TT(eng, m2[:], wim, pb_, ALU.mult, rdW + [bPb], [bm2])
                        TT(eng, outap[:, g0:g0 + 8, :].rearrange("p g (k h) -> p g k h", h=16), m1[:], m2[:],
                           ALU.add, [bm1, bm2], [bout])
                for g4 in range(8):
                    pb = nextps()
                    for gg in range(4):
                        g = g4 * 4 + gg
                        MM(PS[pb][:, gg * 128:(gg + 1) * 128], BnS[:, g, :], CpS[:, g, :], True, True,
                           [bBnS, bCpS], [bPS[pb]], sig=(gg == 3))
                    for gg in range(4):
                        g = g4 * 4 + gg
                        tm, btm = tmpm[g % 2], btmpm[g % 2]
                        if d == 0:
                            TT("vector", tm[:], PS[pb][:, gg * 128:(gg + 1) * 128], maskF, ALU.mult,
                               [bPS[pb], bcst], [btm])
                            k.op("vector", lambda e, g=g, tm=tm: e.scalar_tensor_tensor(
                                out=T8[:, g, :], in0=identF, scalar=dvec[:, g:g + 1], in1=tm[:],
                                op0=ALU.mult, op1=ALU.add), [btm, bcst], [bT8[g]])
                        else:
                            TT("vector", tm[:], PS[pb][:, gg * 128:(gg + 1) * 128], maskB, ALU.mult,
                               [bPS[pb], bcst], [btm])
                            TT("vector", T8[:, g, :], tm[:], T8[:, g, :], ALU.add, [btm, bT8[g]], [bT8[g]])
            k.emit()

        with ExitStack() as stZ:
            Zs = sbt(nc, stZ, "Zs", [64, 2, 64, 128], BF16); bZs = [Buf() for _ in range(64)]
            hist = sbt(nc, stZ, "hist", [64, 2, 2, 32, 64], BF16); bhist = [Buf(), Buf()]
            ST = sbt(nc, stZ, "ST", [128, 2, 32, 64], BF16); bST = [Buf(), Buf()]
            with ExitStack() as st:
                JM = [sbt(nc, st, "JM%d" % i, [128, 7, 128], BF16) for i in range(2)]
                bJM = [Buf() for _ in range(2)]
                B64 = [sbt(nc, st, "B64_%d" % i, [128, 8, 128], BF16) for i in range(2)]
                bB64 = [Buf() for _ in range(2)]
                def r1_build(it):
                    g, d = it // 2, it % 2
                    gd = d * 32 + g
                    q = it % 2
                    TT("vector", JM[q][:].rearrange("p k (w c) -> p k w c", w=2),
                       II[:].rearrange("p (w c) -> p w c", w=2).unsqueeze(1).broadcast_to([128, 7, 2, 64]),
                       sc[:, 2:4, :, gd].rearrange("p w k -> p k w").unsqueeze(3).broadcast_to([128, 7, 2, 64]),
                       ALU.mult, [bII, bsc], [bJM[q]])

                def r1_gen(it):
                    g, d = it // 2, it % 2
                    gd = d * 32 + g
                    q = it % 2
                    for half in range(2):
                        pb = nextps()
                        for xx in range(4):
                            x = half * 4 + xx
                            k8 = 7 - x
                            rhs = identJ[:] if k8 == 0 else JM[q][:, k8 - 1, :]
                            MM(PS[pb][:, xx * 128:(xx + 1) * 128], BfS[:, gd, :], rhs, True, True,
                               [bBfS[d], bJM[q], bidJ], [bPS[pb]], sig=(xx == 3))
                        if half == 0:
                            ACT(B64[q][:, 0:4, :], PS[pb][:].rearrange("p (a b) -> p a b", a=4), AF.Copy,
                                [bPS[pb]], [bB64[q]])
                        else:
                            CP("vector", B64[q][:, 4:8, :], PS[pb][:].rearrange("p (a b) -> p a b", a=4),
                               [bPS[pb]], [bB64[q]])

                def r1_z(it):
                    g, d = it // 2, it % 2
                    gd = d * 32 + g
                    q = it % 2
                    pb = nextps()
                    N = 128 if d == 0 else 64
                    for ri in range(2):
                        for x in range(8):
                            if d == 0:
                                rhs = U[:, g, :].rearrange("p (hf j m) -> p hf j m", hf=2, j=8)[:, :, x, :]
                            else:
                                rhs = U[:, g, 512 + (7 - x) * 64:512 + (8 - x) * 64]
                            MM(PS[pb][0:64, ri * 128:ri * 128 + N], B64[q][:, x, ri * 64:(ri + 1) * 64], rhs,
                               x == 0, x == 7, [bB64[q]] + bUblk, [bPS[pb]], sig=(ri == 1 and x == 7))
                    m0 = 0 if d == 0 else 64
                    ACT(Zs[:, 0, gd, m0:m0 + N], PS[pb][0:64, 0:N], AF.Copy, [bPS[pb]], [bZs[gd]])
                    ACT(Zs[:, 1, gd, m0:m0 + N], PS[pb][0:64, 128:128 + N], AF.Copy, [bPS[pb]], [bZs[gd]], scale=-1.0)

                r1_build(0)
                for it in range(65):
                    if it + 1 < 64:
                        r1_build(it + 1)
                    if it < 64:
                        r1_gen(it)
                    if it >= 1:
                        r1_z(it - 1)
                k.emit()
            with ExitStack() as st:
                Sst = [sbt(nc, st, "S_%d" % i, [64, 2, 2, 32], F32) for i in range(2)]
                bS = [Buf() for _ in range(2)]
                T1 = sbt(nc, st, "T1", [64, 2, 2, 32], F32); bT1 = Buf()
                T2 = sbt(nc, st, "T2", [64, 2, 2, 32], F32); bT2 = Buf()
                for i in range(2):
                    k.op("vector", lambda e, i=i: e.memset(Sst[i][:], 0.0), [], [bS[i]])
                for s_ in range(128):
                    cur = s_ % 2
                    S, bSc = Sst[cur], bS[cur]
                    Sn, bSn = Sst[1 - cur], bS[1 - cur]
                    dsl = slice(0, 1) if s_ < 64 else slice(0, 2)
                    mb = 191 - s_
                    if s_ >= 64:
                        ACT(hist[:, 0, :, :, s_ - 64], S[:, 0, :, :], AF.Copy, [bSc], [bhist[0]])
                        ACT(hist[:, 1, :, :, mb - 64], S[:, 1, :, :], AF.Copy, [bSc], [bhist[1]])
                    TT("vector", T1[:, dsl], S[:, dsl], AA[:, dsl], ALU.mult, [bSc, bAA], [bT1])
                    TT("vector", T1[:, 0], T1[:, 0], Zs[:, :, 0:32, s_], ALU.add, [bT1] + bZs[0:32], [bT1])
                    if s_ >= 64:
                        TT("vector", T1[:, 1], T1[:, 1], Zs[:, :, 32:64, mb], ALU.add, [bT1] + bZs[32:64], [bT1])
                    TT("vector", T2[:, dsl, 0, :], S[:, dsl, 1, :], AB[:, dsl, 0, :], ALU.mult, [bSc, bAB], [bT2])
                    TT("vector", T2[:, dsl, 1, :], S[:, dsl, 0, :], AB[:, dsl, 1, :], ALU.mult, [bSc, bAB], [bT2])
                    TT("vector", Sn[:, dsl], T1[:, dsl], T2[:, dsl], ALU.add, [bT1, bT2], [bSn])
                for d in range(2):
                    k.dma("sync", lambda e, d=d: e.dma_start(out=ST[0:64, d, :, :], in_=hist[:, d, 0, :, :]),
                          reads=[bhist[d]], writes=[bST[d]])
                    k.dma("sync", lambda e, d=d: e.dma_start(out=ST[64:128, d, :, :], in_=hist[:, d, 1, :, :]),
                          reads=[bhist[d]], writes=[bST[d]])
                k.emit()
            with ExitStack() as st:
                MMb = [sbt(nc, st, "MMb%d" % i, [128, 7, 128], BF16) for i in range(2)]
                bMMb = [Buf() for _ in range(2)]
                C64 = [sbt(nc, st, "C64_%d" % i, [128, 7, 128], BF16) for i in range(6)]
                bC64 = [Buf() for _ in range(6)]
                Tz = [sbt(nc, st, "Tz%d" % i, [128, 7, 128], BF16) for i in range(4)]
                bTz = [Buf() for _ in range(4)]

                def r2_A(g):
                    for d in range(2):
                        gd = d * 32 + g
                        q = d
                        cs = (g % 3) * 2 + d
                        TT("vector", MMb[q][:].rearrange("p k (w c) -> p k w c", w=2),
                           II[:].rearrange("p (w c) -> p w c", w=2).unsqueeze(1).broadcast_to([128, 7, 2, 64]),
                           sc[:, 0:2, :, gd].rearrange("p w k -> p k w").unsqueeze(3).broadcast_to([128, 7, 2, 64]),
                           ALU.mult, [bII, bsc], [bMMb[q]])
                        for half in range(2):
                            pb = nextps()
                            n = 4 if half == 0 else 3
                            for xx in range(n):
                                x = 1 + half * 4 + xx
                                MM(PS[pb][:, xx * 128:(xx + 1) * 128], MMb[q][:, x - 1, :], CfS[:, gd, :], True, True,
                                   [bMMb[q], bCfS[d]], [bPS[pb]], sig=(xx == n - 1))
                            src_ = PS[pb][:, 0:n * 128].rearrange("p (a b) -> p a b", a=n)
                            dst_ = C64[cs][:, half * 4:half * 4 + n, :]
                            if half == 0:
                                ACT(dst_, src_, AF.Copy, [bPS[pb]], [bC64[cs]])
                            else:
                                CP("vector", dst_, src_, [bPS[pb]], [bC64[cs]])

                def r2_B(g):
                    for d in range(2):
                        gd = d * 32 + g
                        cs = (g % 3) * 2 + d
                        ts = (g % 2) * 2 + d
                        for half in range(2):
                            pb = nextps()
                            n = 4 if half == 0 else 3
                            for xx in range(n):
                                dl = 1 + half * 4 + xx
                                rhs = CfS[:, gd, :] if dl == 1 else C64[cs][:, dl - 2, :]
                                MM(PS[pb][:, xx * 128:(xx + 1) * 128], BfS[:, gd, :], rhs, True, True,
                                   [bBfS[d], bCfS[d], bC64[cs]], [bPS[pb]], sig=(xx == n - 1))
                            src_ = PS[pb][:, 0:n * 128].rearrange("p (a b) -> p a b", a=n)
                            dst_ = Tz[ts][:, half * 4:half * 4 + n, :]
                            if half == 0:
                                CP("vector", dst_, src_, [bPS[pb]], [bTz[ts]])
                            else:
                                ACT(dst_, src_, AF.Copy, [bPS[pb]], [bTz[ts]])

                def r2_C(g):
                    cs0, cs1 = (g % 3) * 2, (g % 3) * 2 + 1
                    ts0, ts1 = (g % 2) * 2, (g % 2) * 2 + 1
                    pb = nextps()
                    rdall = bUblk + [bT8[g], bTz[ts0], bTz[ts1], bC64[cs0], bC64[cs1],
                                     bCfS[0], bCfS[1], bST[0], bST[1]]
                    MM(PS[pb][:], T8[:, g, :], U[:, g, 512:1024], True, False, rdall, [bPS[pb]], sig=False)
                    for dl in range(1, 8):
                        MM(PS[pb][:, dl * 64:512], Tz[ts0][:, dl - 1, :], U[:, g, 512:512 + (8 - dl) * 64], False, False,
                           rdall, [bPS[pb]], sig=False)
                        MM(PS[pb][:, 0:(8 - dl) * 64], Tz[ts1][:, dl - 1, :], U[:, g, 512 + dl * 64:1024], False, False,
                           rdall, [bPS[pb]], sig=False)
                    for j2 in range(8):
                        lf = CfS[:, g, :] if j2 == 0 else C64[cs0][:, j2 - 1, :]
                        MM(PS[pb][:, j2 * 64:(j2 + 1) * 64], lf, ST[:, 0, g, :], False, False, rdall, [bPS[pb]], sig=False)
                        xb_ = 7 - j2
                        lb = CfS[:, 32 + g, :] if xb_ == 0 else C64[cs1][:, xb_ - 1, :]
                        MM(PS[pb][:, j2 * 64:(j2 + 1) * 64], lb, ST[:, 1, g, :], False, j2 == 7, rdall, [bPS[pb]],
                           sig=(j2 == 7))
                    yout = U[:, g, 0:512].rearrange("p (m j) -> p j m", j=8)
                    yin = PS[pb][:].rearrange("p (j m) -> p j m", j=8)
                    if g % 2 == 0:
                        ACT(yout, yin, AF.Copy, [bPS[pb]], [bUblk[0]])
                    else:
                        CP("vector", yout, yin, [bPS[pb]], [bUblk[0]])

                for t in range(34):
                    if t < 32:
                        r2_A(t)
                    if 1 <= t <= 32:
                        r2_B(t - 1)
                    if t >= 2:
                        r2_C(t - 2)
                if stage == 2:
                    t1 = k.dma("gpsimd", lambda e: e.dma_start(
                        out=dr["dbg_y"].rearrange("p (g c) -> p g c", g=32), in_=U[:, :, 0:512]), reads=bUblk)
                    k.wait_tok("gpsimd", t1)
                k.emit()


def pass_S(nc, k, dr, PS, bPS, ident, bident, U, bUblk, stage):
    def TT(eng, out, a, b, op, rd, wr):
        k.op(eng, lambda e: e.tensor_tensor(out=out, in0=a, in1=b, op=op), rd, wr)

    def ACT(out, in_, func, rd, wr, **kw):
        k.op("scalar", lambda e: e.activation(out=out, in_=in_, func=func, **kw), rd, wr)

    def CP(eng, out, in_, rd, wr):
        k.op(eng, lambda e: e.tensor_copy(out=out, in_=in_), rd, wr)

    def MM(out, lhsT, rhs, start, stop, rd, wr, sig):
        k.op("tensor", lambda e: e.matmul(out, lhsT=lhsT, rhs=rhs, start=start, stop=stop), rd, wr, sig=sig)

    pi = [0]

    def nextps():
        pb = pi[0] % 8
        pi[0] += 1
        return pb

    with ExitStack() as st:
        wglu = sbt(nc, st, "wglu", [128, 4, 512], BF16); bwglu = Buf()
        k.dma("gpsimd", lambda e: e.dma_start(out=wglu[:], in_=dr["w_glu"].rearrange("(kt f) c -> f kt c", f=128)),
              writes=[bwglu])
        bglu = sbt(nc, st, "bglu", [1, 512], BF16); bbglu = Buf()
        k.dma("gpsimd", lambda e: e.dma_start(out=bglu[:], in_=dr["b_glu"]), writes=[bbglu])
        ones = sbt(nc, st, "ones", [1, 128], BF16); bones = Buf()
        k.op("vector", lambda e: e.memset(ones[:], 1.0), [], [bones])
        ys = [sbt(nc, st, "ys%d" % i, [128, 8, 512], F32) for i in range(2)]; bys = [Buf(), Buf()]
        yg = [sbt(nc, st, "yg%d" % i, [128, 8, 512], BF16) for i in range(2)]; byg = [Buf(), Buf()]
        zsl = [sbt(nc, st, "zsl%d" % i, [128, 8, 512], BF16) for i in range(2)]; bzsl = [Buf(), Buf()]
        mixs = [sbt(nc, st, "mixs%d" % i, [128, 8, 512], BF16) for i in range(2)]
        bmixs = [Buf() for _ in range(2)]
        ygT = [sbt(nc, st, "ygT%d" % i, [128, 4, 128], BF16) for i in range(2)]
        bygT = [Buf() for _ in range(2)]
        sgl = [sbt(nc, st, "sgl%d" % i, [128, 512], F32) for i in range(2)]
        bsgl = [Buf() for _ in range(2)]
        t1 = [sbt(nc, st, "t1s%d" % i, [128, 512], F32) for i in range(2)]
        bt1 = [Buf() for _ in range(2)]
        mdv = dr["mix_d"].rearrange("(blk c j) f -> blk c j f", c=128, j=8)

        def s_A(blk):
            b2 = blk % 2
            k.dma("sync", lambda e: e.dma_start(out=zsl[b2][:].rearrange("p a b -> p (a b)"), in_=dr["zs_d"][blk]),
                  writes=[bzsl[b2]])
            for g4 in range(4):
                pb = nextps()
                pT = PS[pb][:].bitcast(BF16).rearrange("p (g j h) -> p g j h", g=8, j=8)
                for gg in range(8):
                    g = g4 * 8 + gg
                    k.op("tensor", lambda e, pT=pT, gg=gg, g=g: e.transpose(
                        out=pT[:, gg, :, :].rearrange("p j h -> p (j h)"), in_=U[:, g, blk * 128:(blk + 1) * 128],
                        identity=ident[:]), reads=bUblk + [bident], writes=[bPS[pb]], sig=(gg == 7))
                ACT(ys[b2][:, :, g4 * 128:(g4 + 1) * 128].rearrange("p j (g h) -> p j g h", h=16),
                    pT.rearrange("p g j h -> p j g h"), AF.Copy, [bPS[pb]], [bys[b2]])
            for half in range(2):
                ACT(yg[b2][:, half * 4:(half + 1) * 4, :], ys[b2][:, half * 4:(half + 1) * 4, :], AF.Gelu_apprx_tanh,
                    [bys[b2]], [byg[b2]])

        def s_T(blk, j):
            b2 = blk % 2
            q = j % 2
            pb = nextps()
            pT = PS[pb][:].bitcast(BF16).rearrange("p (a b) -> p a b", a=8)
            for kt in range(4):
                k.op("tensor", lambda e, kt=kt: e.transpose(
                    out=pT[:, kt, :], in_=yg[b2][:, j, kt * 128:(kt + 1) * 128], identity=ident[:]),
                    reads=[byg[b2], bident], writes=[bPS[pb]], sig=(kt == 3))
            CP("vector", ygT[q][:], pT[:, 0:4, :], [bPS[pb]], [bygT[q]])

        def s_M(blk, j):
            b2 = blk % 2
            mb = blk % 2
            q = j % 2
            pb = nextps()
            for kt in range(4):
                MM(PS[pb][:], ygT[q][:, kt, :], wglu[:, kt, :], kt == 0, False, [bygT[q], bwglu], [bPS[pb]], False)
            MM(PS[pb][:], ones[:], bglu[:], False, True, [bones, bbglu], [bPS[pb]], True)
            ACT(sgl[q][:], PS[pb][:], AF.Tanh, [bPS[pb]], [bsgl[q]], scale=0.5)
            k.op("vector", lambda e: e.scalar_tensor_tensor(
                out=t1[q][:], in0=yg[b2][:, j, :], scalar=0.25, in1=zsl[b2][:, j, :], op0=ALU.mult, op1=ALU.mult),
                [byg[b2], bzsl[b2]], [bt1[q]])
            k.op("vector", lambda e: e.scalar_tensor_tensor(
                out=mixs[mb][:, j, :], in0=sgl[q][:], scalar=1.0, in1=t1[q][:], op0=ALU.add, op1=ALU.mult),
                [bt1[q], bsgl[q]], [bmixs[mb]])
            if j == 7:
                k.dma("sync", lambda e: e.dma_start(out=mdv[blk][:, :, 0:512], in_=mixs[mb][:]), reads=[bmixs[mb]])

        s_A(0)
        for blk in range(4):
            if blk + 1 < 4:
                s_A(blk + 1)
            s_T(blk, 0)
            for j in range(8):
                if j + 1 < 8:
                    s_T(blk, j + 1)
                s_M(blk, j)
        if stage == 3:
            t_ = k.dma("sync", lambda e: e.dma_start(out=dr["dbg_ms"], in_=mixs[1][:].rearrange("p a b -> p (a b)")),
                       reads=[bmixs[1]])
            k.wait_tok("sync", t_)
        k.emit()


def prefetch_N(nc, k, dr, st):
    W = {}
    wsrc = dr["w_in"].rearrange("(kt f) c -> f kt c", f=128)
    W["wq"] = sbt(nc, st, "wq", [128, 8, 2048], BF16)
    W["bwq"] = {nm: [Buf() for _ in range(8)] for nm in ("v", "k", "zn", "q")}
    own = {nm: Buf() for nm in ("v", "k", "zn", "q")}
    for nm, c0 in (("v", 1024), ("k", 512), ("zn", 1536), ("q", 0)):
        for kt in range(8):
            k.dma("gpsimd", lambda e, kt=kt, c0=c0: e.dma_start(
                out=W["wq"][:, kt, c0:c0 + 512], in_=wsrc[:, kt, 1024 + c0:1024 + c0 + 512]),
                writes=[W["bwq"][nm][kt]], owner=own[nm])
    W["tab"] = sbt(nc, st, "tab", [128, 40, 128], BF16); W["btab"] = Buf()
    tabv = dr["na_tab"].rearrange("kd h i k q -> kd k (h i) q")
    tparts = [Buf() for _ in range(4)]
    for hi_, h in enumerate(range(0, 8, 2)):
        k.dma("gpsimd", lambda e, h=h: e.dma_start(
            out=W["tab"][:, h * 5:(h + 2) * 5, :], in_=tabv[0][:, h * 5:(h + 2) * 5, :]),
            writes=[tparts[hi_]], owner=W["btab"])
    W["tparts"] = tparts
    for nm, key, n_ in (("w_out", "wout", 8), ("w_gate", "wgate", 8), ("w_ple", "wple", 2)):
        W[key] = sbt(nc, st, key, [128, n_, 1024], BF16); W["b" + key] = [Buf() for _ in range(n_)]
        ownb = Buf()
        sv = dr[nm].rearrange("(kt f) c -> f kt c", f=128)
        for kt in range(n_):
            k.dma("gpsimd", lambda e, kt=kt, key=key, sv=sv: e.dma_start(out=W[key][:, kt, :], in_=sv[:, kt, :]),
                  writes=[W["b" + key][kt]], owner=ownb)
    return W


def pass_N(nc, k, dr, PS, bPS, ident, bident, stage, W):
    def TT(eng, out, a, b, op, rd, wr):
        k.op(eng, lambda e: e.tensor_tensor(out=out, in0=a, in1=b, op=op), rd, wr)

    def ACT(out, in_, func, rd, wr, **kw):
        k.op("scalar", lambda e: e.activation(out=out, in_=in_, func=func, **kw), rd, wr)

    def CP(eng, out, in_, rd, wr):
        k.op(eng, lambda e: e.tensor_copy(out=out, in_=in_), rd, wr)

    def MM(out, lhsT, rhs, start, stop, rd, wr, sig):
        k.op("tensor", lambda e: e.matmul(out, lhsT=lhsT, rhs=rhs, start=start, stop=stop), rd, wr, sig=sig)

    def TR(out, in_, rd, wr, sig):
        k.op("tensor", lambda e: e.transpose(out=out, in_=in_, identity=ident[:]), rd + [bident], wr, sig=sig)

    pi = [0]

    def nextps():
        pb = pi[0] % 6
        pi[0] += 1
        return pb

    out_toks = []
    with ExitStack() as st:
        wq, bwq, tab, btab = W["wq"], W["bwq"], W["tab"], W["btab"]
        wout, bwout, wgate, bwgate, wple, bwple = W["wout"], W["bwout"], W["wgate"], W["bwgate"], W["wple"], W["bwple"]
        tabv = dr["na_tab"].rearrange("kd h i k q -> kd k (h i) q")

        def load_tab(kind):
            for h in range(0, 8, 2):
                k.dma("gpsimd", lambda e, h=h, kind=kind: e.dma_start(
                    out=tab[:, h * 5:(h + 2) * 5, :], in_=tabv[kind][:, h * 5:(h + 2) * 5, :]), writes=[btab])
            ACT(tab[:], tab[:], AF.Exp, [btab], [btab])
        ACT(tab[:], tab[:], AF.Exp, W["tparts"], [btab])
        npre = sbt(nc, st, "npre2", [128, 8], F32); bnpre = Buf()
        k.dma("sync", lambda e: e.dma_start(out=npre[:], in_=dr["npre"]), writes=[bnpre])
        npost = sbt(nc, st, "npost", [128, 1024], F32); bnpost = Buf()
        k.dma("sync", lambda e: e.dma_start(out=npost[:], in_=dr["npost_b"]), writes=[bnpost])
        plen = sbt(nc, st, "plen", [128, 1024], F32); bplen = Buf()
        k.dma("sync", lambda e: e.dma_start(out=plen[:], in_=dr["plen_b"]), writes=[bplen])
        KT = sbt(nc, st, "KT", [128, 4, 1536], BF16); bKT = [Buf() for _ in range(3)]
        V = sbt(nc, st, "V", [128, 12, 8, 65], BF16); bV = [Buf() for _ in range(3)]
        k.op("vector", lambda e: e.memset(V[:], 2.0), [], bV)
        nhalf = sbt(nc, st, "nhalfN", [128, 1], F32); bnhalf = Buf()
        k.op("gpsimd", lambda e: e.memset(nhalf[:], -0.5), [], [bnhalf])
        tz = [sbt(nc, st, "tz%d" % i, [128, 512], BF16) for i in range(2)]; btz = [Buf() for _ in range(2)]

        def rsqrt_mean(dst, srcap, rd, wr):
            k.op("gpsimd", lambda e: e.tensor_scalar(out=dst, in0=srcap, scalar1=1.0 / D, scalar2=EPS,
                                                      op0=ALU.mult, op1=ALU.add), rd, wr)
            k.op("gpsimd", lambda e: e.tensor_tensor(out=dst, in0=dst, in1=nhalf[:], op=ALU.pow),
                 wr + [bnhalf], wr)
        QT = [sbt(nc, st, "QT%d" % i, [128, 4, 512], BF16) for i in range(2)]; bQT = [Buf() for _ in range(2)]
        zn = [sbt(nc, st, "zn%d" % i, [128, 4, 512], BF16) for i in range(2)]; bzn = [Buf() for _ in range(2)]
        hnTg = [sbt(nc, st, "hnTg%d" % i, [128, 8, 512], BF16) for i in range(2)]
        bhn = [[Buf() for _ in range(4)] for _ in range(2)]
        xin = [sbt(nc, st, "xinN%d" % i, [128, 1024], F32) for i in range(2)]; bxin = [Buf() for _ in range(2)]
        xs = [sbt(nc, st, "xsN%d" % i, [128, 1024], BF16) for i in range(2)]; bxs = [Buf() for _ in range(2)]
        sst = sbt(nc, st, "sstN", [128, 40], F32); bss = [Buf() for _ in range(40)]
        rst = sbt(nc, st, "rstN", [128, 40], F32); brs = [Buf() for _ in range(40)]
        PT = [sbt(nc, st, "PT%d" % i, [128, 5, 128], BF16) for i in range(4)]; bPT = [Buf() for _ in range(4)]
        rden = [sbt(nc, st, "rden%d" % i, [128, 8], F32) for i in range(2)]; brden = [Buf() for _ in range(2)]
        wt = [sbt(nc, st, "wt%d" % i, [128, 4, 64], BF16) for i in range(2)]; bwt = [Buf() for _ in range(2)]
        mixt = [sbt(nc, st, "mixt%d" % i, [128, 1024], BF16) for i in range(2)]
        bmixA = [Buf() for _ in range(2)]; bmixB = [[Buf(), Buf()] for _ in range(2)]
        mixT = sbt(nc, st, "mixT", [128, 8, 128], BF16); bmixT = Buf()
        xr = [sbt(nc, st, "xr%d" % i, [128, 1024], F32) for i in range(2)]; bxr = [Buf() for _ in range(2)]
        ptl = [sbt(nc, st, "ptl%d" % i, [128, 256], F32) for i in range(2)]; bptl = [Buf() for _ in range(2)]
        pb16 = sbt(nc, st, "pb16", [128, 256], BF16); bpb16 = Buf()
        pT = sbt(nc, st, "pT", [128, 2, 128], BF16); bpT = Buf()
        h1 = [sbt(nc, st, "h1_%d" % i, [128, 1024], F32) for i in range(2)]; bh1 = [Buf() for _ in range(2)]
        h1b = [sbt(nc, st, "h1b%d" % i, [128, 1024], BF16) for i in range(2)]; bh1b = [Buf() for _ in range(2)]
        h1T = sbt(nc, st, "h1T", [128, 8, 128], BF16); bh1T = Buf()
        sg = [sbt(nc, st, "sg%d" % i, [128, 1024], BF16) for i in range(2)]; bsg = [Buf() for _ in range(2)]
        et = [sbt(nc, st, "et%d" % i, [128, 1024], F32) for i in range(2)]; bet = [Buf() for _ in range(2)]
        s2 = [sbt(nc, st, "s2_%d" % i, [128, 8], F32) for i in range(2)]; bs2 = [Buf() for _ in range(2)]

        def tile_ok(tt):
            return -2 <= tt <= 31

        def grp(tt):
            G = tt // 4
            return G, tt - 4 * G, (G + 1) % 3

        def pre(tt):
            if not tile_ok(tt):
                return
            tok0 = HALF + 128 * tt
            si = (tt + 2) % 40
            xb = (tt + 2) % 2
            k.dma("sync", lambda e: e.dma_start(out=xin[xb][:], in_=dr["x_all"][tok0:tok0 + 128, :]), writes=[bxin[xb]])

        def pre_act(tt):
            if not tile_ok(tt):
                return
            si = (tt + 2) % 40
            xb = (tt + 2) % 2
            ACT(xs[xb][:], xin[xb][:], AF.Square, [bxin[xb]], [bxs[xb], bss[si]], accum_out=sst[:, si:si + 1])
            rsqrt_mean(rst[:, si:si + 1], sst[:, si:si + 1], [bss[si]], [brs[si]])
            ACT(xs[xb][:], xin[xb][:], AF.Copy, [bxin[xb], brs[si]], [bxs[xb]], scale=rst[:, si:si + 1])

        def trn(tt):
            if not tile_ok(tt):
                return
            G, pos, sg3 = grp(tt)
            xb = (tt + 2) % 2
            hb = G % 2
            pb = nextps()
            pTt = PS[pb][:].bitcast(BF16).rearrange("p (a b) -> p a b", a=8)
            for kt in range(8):
                TR(pTt[:, kt, :], xs[xb][:, kt * 128:(kt + 1) * 128], [bxs[xb]], [bPS[pb]], kt == 7)
            TT("vector", hnTg[hb][:, :, pos * 128:(pos + 1) * 128], pTt,
               npre[:].unsqueeze(2).broadcast_to([128, 8, 128]), ALU.mult, [bPS[pb], bnpre], [bhn[hb][pos]])

        def mmv(tt):
            if not tile_ok(tt):
                return
            G, pos, sg3 = grp(tt)
            hb = G % 2
            qg = G % 2
            hT = hnTg[hb]
            pb = nextps()
            for kt in range(8):
                MM(PS[pb][:], hT[:, kt, pos * 128:(pos + 1) * 128], wq[:, kt, 1024:1536], kt == 0, kt == 7,
                   [bhn[hb][pos]] + bwq["v"], [bPS[pb]], kt == 7)
            CP("vector", V[:, sg3 * 4 + pos, :, 0:64], PS[pb][:].rearrange("p (h d) -> p h d", d=64),
               [bPS[pb]], [bV[sg3]])
            if G >= 0:
                pb = nextps()
                for kt in range(8):
                    MM(PS[pb][:], hT[:, kt, pos * 128:(pos + 1) * 128], wq[:, kt, 1536:2048], kt == 0, kt == 7,
                       [bhn[hb][pos]] + bwq["zn"], [bPS[pb]], kt == 7)
                tzi = tt % 2
                ACT(tz[tzi][:], PS[pb][:], AF.Tanh, [bPS[pb]], [btz[tzi]], scale=0.5)
                k.op("vector", lambda e, pb=pb: e.scalar_tensor_tensor(
                    out=zn[qg][:, pos, :], in0=tz[tzi][:], scalar=1.0, in1=PS[pb][:], op0=ALU.add, op1=ALU.mult),
                    [bPS[pb], btz[tzi]], [bzn[qg]])
            if pos == 3:
                poss = [0, 1, 2, 3] if G >= 0 else [2, 3]
                c0 = poss[0] * 128
                N = len(poss) * 128
                rdh = [bhn[hb][p_] for p_ in poss]
                if G >= 0:
                    for ct in range(4):
                        pb = nextps()
                        for kt in range(8):
                            MM(PS[pb][:], wq[:, kt, ct * 128:(ct + 1) * 128], hT[:, kt, :], kt == 0, kt == 7,
                               rdh + bwq["q"], [bPS[pb]], kt == 7)
                        ACT(QT[qg][:, ct, :], PS[pb][:], AF.Copy, [bPS[pb]], [bQT[qg]], scale=0.125)
                for ct in range(4):
                    pb = nextps()
                    for kt in range(8):
                        MM(PS[pb][:, 0:N], wq[:, kt, 512 + ct * 128:512 + (ct + 1) * 128], hT[:, kt, c0:c0 + N],
                           kt == 0, kt == 7, rdh + bwq["k"], [bPS[pb]], kt == 7)
                    CP("vector", KT[:, ct, sg3 * 512 + c0:sg3 * 512 + c0 + N], PS[pb][:, 0:N], [bPS[pb]], [bKT[sg3]])

        def key_tiles(R):
            tiles = [R - 2 + i for i in range(5)] if R <= 29 else [28, 29, 30, 31]
            info = []
            for tl in tiles:
                Gt, pt_, sgt = grp(tl)
                info.append((sgt, pt_))
            return info

        def attn_start(R):
            if not (0 <= R <= 31):
                return
            q = R % 2
            if R == 30:
                load_tab(1)
            if R == 31:
                load_tab(2)
            k.dma("sync", lambda e: e.dma_start(out=mixt[q][:, 0:512], in_=dr["mix_d"][R * 128:(R + 1) * 128, 0:512]),
                  writes=[bmixA[q]])
            k.dma("sync", lambda e: e.dma_start(out=xr[q][:], in_=dr["x_all"][HALF + R * 128:HALF + (R + 1) * 128, :]),
                  writes=[bxr[q]])
            k.dma("sync", lambda e: e.dma_start(out=ptl[q][:], in_=dr["p_own"][R * 128:(R + 1) * 128, :]),
                  writes=[bptl[q]])

        def qk(R, h):
            if not (0 <= R <= 31):
                return
            G, pos, _ = grp(R)
            qg = G % 2
            kinfo = key_tiles(R)
            nk = len(kinfo)
            hp = h // 2
            lo = 64 * (h % 2)
            hq = h % 4
            pbA = nextps()
            pbB = nextps() if nk == 5 else None
            for idx in range(nk):
                sgt, pt_ = kinfo[idx]
                if idx < 4:
                    tgt = PS[pbA][:, idx * 128:(idx + 1) * 128]; wr_ = [bPS[pbA]]
                else:
                    tgt = PS[pbB][:, 0:128]; wr_ = [bPS[pbB]]
                last = (idx == min(nk, 4) - 1) or idx == 4
                MM(tgt, KT[lo:lo + 64, hp, sgt * 512 + pt_ * 128:sgt * 512 + (pt_ + 1) * 128],
                   QT[qg][lo:lo + 64, hp, pos * 128:(pos + 1) * 128], True, True,
                   [bKT[sgt], bQT[qg]], wr_, last)
            n4 = min(nk, 4)
            ACT(PT[hq][:, 0:n4, :], PS[pbA][:, 0:n4 * 128].rearrange("p (a b) -> p a b", a=n4), AF.Exp,
                [bPS[pbA]], [bPT[hq]])
            if nk == 5:
                ACT(PT[hq][:, 4, :], PS[pbB][:, 0:128], AF.Exp, [bPS[pbB]], [bPT[hq]])
            TT("vector", PT[hq][:, 0:nk, :], PT[hq][:, 0:nk, :], tab[:, h * 5:h * 5 + nk, :], ALU.mult,
               [bPT[hq], btab], [bPT[hq]])

        def pv(R, h):
            if not (0 <= R <= 31):
                return
            kinfo = key_tiles(R)
            nk = len(kinfo)
            hq = h % 4
            ob = 6 + h // 4
            for idx in range(nk):
                sgt, pt_ = kinfo[idx]
                MM(PS[ob][:, (h % 4) * 65:(h % 4) * 65 + 65], PT[hq][:, idx, :], V[:, sgt * 4 + pt_, h, :],
                   idx == 0, idx == nk - 1, [bPT[hq], bV[sgt]], [bPS[ob]], (idx == nk - 1))

        def norm(R, quad):
            if not (0 <= R <= 31):
                return
            G, pos, _ = grp(R)
            qg = G % 2
            q = R % 2
            ob = 6 + quad
            pv_ = PS[ob][:, 0:260].rearrange("p (h e) -> p h e", e=65)
            k.op("vector", lambda e: e.reciprocal(out=rden[q][:, quad * 4:(quad + 1) * 4], in_=pv_[:, :, 64]),
                 [bPS[ob]], [brden[q]])
            TT("vector", wt[quad][:], zn[qg][:, pos, quad * 256:(quad + 1) * 256].rearrange("p (h d) -> p h d", d=64),
               rden[q][:, quad * 4:(quad + 1) * 4].unsqueeze(2).broadcast_to([128, 4, 64]), ALU.mult,
               [bzn[qg], brden[q]], [bwt[quad]])
            TT("vector", mixt[q][:, 512 + quad * 256:512 + (quad + 1) * 256].rearrange("p (h d) -> p h d", d=64),
               pv_[:, :, 0:64], wt[quad][:], ALU.mult, [bPS[ob], bwt[quad]], [bmixB[q][quad]])

        def pcopy(R):
            if not (0 <= R <= 31):
                return
            CP("vector", pb16[:], ptl[R % 2][:], [bptl[R % 2]], [bpb16])

        def tail1a(R):
            if not (0 <= R <= 31):
                return
            q = R % 2
            pb = nextps()
            pTt = PS[pb][:].bitcast(BF16).rearrange("p (a b) -> p a b", a=8)
            for kt in range(8):
                TR(pTt[:, kt, :], mixt[q][:, kt * 128:(kt + 1) * 128], [bmixA[q]] + bmixB[q], [bPS[pb]], kt == 7)
            ACT(mixT[:], pTt, AF.Copy, [bPS[pb]], [bmixT])
            pb = nextps()
            pTt = PS[pb][:].bitcast(BF16).rearrange("p (a b) -> p a b", a=8)
            for kt in range(2):
                TR(pTt[:, kt, :], pb16[:, kt * 128:(kt + 1) * 128], [bpb16], [bPS[pb]], kt == 1)
            CP("vector", pT[:], pTt[:, 0:2, :], [bPS[pb]], [bpT])

        def tail1b(R):
            if not (0 <= R <= 31):
                return
            q = R % 2
            for half in range(2):
                hs = slice(half * 512, (half + 1) * 512)
                pb = nextps()
                for kt in range(8):
                    MM(PS[pb][:], mixT[:, kt, :], wout[:, kt, hs], kt == 0, kt == 7, [bmixT] + bwout, [bPS[pb]], kt == 7)
                ACT(h1[q][:, hs], PS[pb][:], AF.Copy, [bPS[pb]], [bh1[q]])
                ACT(h1b[q][:, hs], h1[q][:, hs], AF.Square, [bh1[q]], [bh1b[q], bs2[q]], accum_out=s2[q][:, half:half + 1])
            for half in range(2):
                hs = slice(half * 512, (half + 1) * 512)
                pb = nextps()
                for kt in range(2):
                    MM(PS[pb][:], pT[:, kt, :], wple[:, kt, hs], kt == 0, kt == 1, [bpT] + bwple, [bPS[pb]], kt == 1)
                ACT(et[q][:, hs], PS[pb][:], AF.Copy, [bPS[pb]], [bet[q]])
                ACT(h1b[q][:, hs], et[q][:, hs], AF.Square, [bet[q]], [bh1b[q], bs2[q]],
                    accum_out=s2[q][:, 4 + half:5 + half])
            TT("vector", s2[q][:, 2:3], s2[q][:, 0:1], s2[q][:, 1:2], ALU.add, [bs2[q]], [bs2[q]])
            rsqrt_mean(s2[q][:, 3:4], s2[q][:, 2:3], [bs2[q]], [bs2[q]])
            for half in range(2):
                hs = slice(half * 512, (half + 1) * 512)
                k.op("vector", lambda e, hs=hs: e.scalar_tensor_tensor(
                    out=h1[q][:, hs], in0=h1[q][:, hs], scalar=s2[q][:, 3:4], in1=npost[:, hs],
                    op0=ALU.mult, op1=ALU.mult), [bh1[q], bs2[q], bnpost], [bh1[q]])
            TT("vector", h1[q][:], h1[q][:], xr[q][:], ALU.add, [bh1[q], bxr[q]], [bh1[q]])
            ACT(h1b[q][:], h1[q][:], AF.Copy, [bh1[q]], [bh1b[q]])
            TT("vector", s2[q][:, 6:7], s2[q][:, 4:5], s2[q][:, 5:6], ALU.add, [bs2[q]], [bs2[q]])
            rsqrt_mean(s2[q][:, 7:8], s2[q][:, 6:7], [bs2[q]], [bs2[q]])
            for half in range(2):
                hs = slice(half * 512, (half + 1) * 512)
                k.op("vector", lambda e, hs=hs: e.scalar_tensor_tensor(
                    out=et[q][:, hs], in0=et[q][:, hs], scalar=s2[q][:, 7:8], in1=plen[:, hs],
                    op0=ALU.mult, op1=ALU.mult), [bet[q], bs2[q], bplen], [bet[q]])

        def tail2a(R):
            if not (0 <= R <= 31):
                return
            q = R % 2
            pb = nextps()
            pTt = PS[pb][:].bitcast(BF16).rearrange("p (a b) -> p a b", a=8)
            for kt in range(8):
                TR(pTt[:, kt, :], h1b[q][:, kt * 128:(kt + 1) * 128], [bh1b[q]], [bPS[pb]], kt == 7)
            CP("vector", h1T[:], pTt, [bPS[pb]], [bh1T])

        def tail2b(R):
            if not (0 <= R <= 31):
                return
            q = R % 2
            for half in range(2):
                hs = slice(half * 512, (half + 1) * 512)
                pb = nextps()
                for kt in range(8):
                    MM(PS[pb][:], h1T[:, kt, :], wgate[:, kt, hs], kt == 0, kt == 7, [bh1T] + bwgate, [bPS[pb]], kt == 7)
                ACT(sg[q][:, hs], PS[pb][:], AF.Tanh, [bPS[pb]], [bsg[q]], scale=0.5)
            k.op("vector", lambda e: e.scalar_tensor_tensor(
                out=et[q][:], in0=sg[q][:], scalar=1.0, in1=et[q][:], op0=ALU.add, op1=ALU.mult),
                [bet[q], bsg[q]], [bet[q]])
            k.op("vector", lambda e: e.scalar_tensor_tensor(
                out=et[q][:], in0=et[q][:], scalar=0.5, in1=h1[q][:], op0=ALU.mult, op1=ALU.add),
                [bet[q], bh1[q]], [bet[q]])
            tok = k.dma("gpsimd", lambda e: e.dma_start(out=dr["out"][R * 128:(R + 1) * 128, :], in_=et[q][:]),
                        reads=[bet[q]])
            out_toks.append(tok)

        def slot(i, h):
            qk(i, h)
            Hh = 8 * i + h - 3
            Ri, hh = Hh // 8, Hh % 8
            pv(Ri, hh)
            if hh == 3:
                norm(Ri, 0)
            if hh == 7:
                norm(Ri, 1)

        for i in range(-10, 35):
            pre(i + 8)
            attn_start(i)
            pcopy(i - 1)
            slot(i, 0); slot(i, 1)
            tail2a(i - 2)
            slot(i, 2); slot(i, 3)
            pre_act(i + 8)
            trn(i + 7)
            slot(i, 4)
            tail1a(i - 1)
            slot(i, 5)
            tail2b(i - 2)
            slot(i, 6)
            tail1b(i - 1)
            slot(i, 7)
            mmv(i + 6)
        for tok in out_toks[-4:]:
            k.wait_tok("gpsimd", tok)
        k.emit()


def build(stage=99):
    nc = bass.Bass("TRN2", target_bir_lowering=False)
    dr = {}

    def din(name, shape, dt=F32):
        dr[name] = nc.dram_tensor(name, list(shape), dt, kind="ExternalInput").ap()

    def dout(name, shape, dt=F32):
        dr[name] = nc.dram_tensor(name, list(shape), dt, kind="ExternalOutput").ap()

    def dint(name, shape, dt):
        dr[name] = nc.dram_tensor(name, list(shape), dt, kind="Internal").ap()

    din("x_all", [NTOK, D])
    din("w_in", [D, 3072])
    din("npre", [128, 8])
    din("ident", [128, 128])
    dint("zs_d", [4, 128, 8 * 512], BF16)
    din("ssm_par", [2, 128, 2144])
    din("ssm_c", [128, NK + 32 + 1 + 4 * 128])
    if stage == 2:
        dout("dbg_y", [128, 32 * 512])
    din("w_glu", [512, 512])
    din("b_glu", [1, 512])
    dint("mix_d", [HALF, D], BF16)
    if stage == 3:
        dout("dbg_ms", [128, 8 * 512], BF16)
    din("w_out", [D, D])
    din("w_gate", [D, D])
    din("w_ple", [256, D])
    din("npost_b", [128, D])
    din("plen_b", [128, D])
    din("na_tab", [3, 8, 5, 128, 128])
    din("p_own", [HALF, 256])
    if stage >= 4:
        dout("out", [HALF, D])
    if stage == 1:
        dout("dbg_u", [128, 32 * 1024])
        dout("dbg_zs", [128, 8 * 512])

    with ExitStack() as st0:
        k = K(nc, st0)
        PS = [st0.enter_context(nc.psum_tensor("ps%d" % i, [128, 512], F32)) for i in range(8)]
        bPS = [Buf("ps%d" % i) for i in range(8)]
        ident = sbt(nc, st0, "ident", [128, 128], BF16)
        bident = Buf("ident")
        k.dma("gpsimd", lambda e: e.dma_start(out=ident[:], in_=dr["ident"]), writes=[bident])
        stU = ExitStack()
        U = sbt(nc, stU, "U", [128, 32, 1024], BF16)
        bUblk = [Buf("U%d" % i) for i in range(8)]

        with ExitStack() as st:
          if stage != 5:
              npre = sbt(nc, st, "npre", [128, 8], F32); bnpre = Buf()
              k.dma("sync", lambda e: e.dma_start(out=npre[:], in_=dr["npre"]), writes=[bnpre])
              wu = sbt(nc, st, "wu", [128, 8, 1024], BF16); bwu = Buf(); bwul = [Buf() for _ in range(8)]
              bwu2 = Buf(); bwzl = [Buf() for _ in range(8)]
              wsrc = dr["w_in"].rearrange("(kt f) c -> f kt c", f=128)
              for kt in range(8):
                  k.dma("gpsimd", lambda e, kt=kt: e.dma_start(out=wu[:, kt, 0:512], in_=wsrc[:, kt, 0:512]),
                        writes=[bwul[kt]], owner=bwu)
              for kt in range(8):
                  k.dma("gpsimd", lambda e, kt=kt: e.dma_start(out=wu[:, kt, 512:1024], in_=wsrc[:, kt, 512:1024]),
                        writes=[bwzl[kt]], owner=bwu2)
              xin = [sbt(nc, st, "xin%d" % i, [128, 2, D], F32) for i in range(2)]
              bxin = [Buf() for _ in range(2)]
              junk = sbt(nc, st, "junk", [128, D], BF16); bjunk = Buf()
              xs = [sbt(nc, st, "xs%d" % i, [128, D], BF16) for i in range(2)]
              bxs = [Buf() for _ in range(2)]
              ss = sbt(nc, st, "ss", [128, 64], F32)
              bss = [Buf() for _ in range(64)]
              rs = sbt(nc, st, "rs", [128, 64], F32)
              brs = [Buf() for _ in range(64)]
              hnT = [sbt(nc, st, "hnT%d" % i, [128, 8, 8, 128], BF16) for i in range(2)]
              bhn = [[Buf() for _ in range(8)] for _ in range(2)]
              Tt = [sbt(nc, st, "Tt%d" % i, [128, 32, 8, 16], BF16) for i in range(2)]
              bTt = [[Buf() for _ in range(8)] for _ in range(2)]
              zst = [sbt(nc, st, "zst%d" % i, [128, 8, 512], BF16) for i in range(2)]
              bzs = [Buf(), Buf()]
              nhalf = sbt(nc, st, "nhalfA", [128, 1], F32); bnhalf = Buf()
              k.op("gpsimd", lambda e: e.memset(nhalf[:], -0.5), [], [bnhalf])
              tza = [sbt(nc, st, "tza%d" % i, [128, 512], BF16) for i in range(2)]
              btza = [Buf() for _ in range(2)]
              xv = dr["x_all"].rearrange("(blk c j) f -> blk c j f", c=128, j=8)
              pi = [0]

              def nps():
                  pb = pi[0] % 8
                  pi[0] += 1
                  return pb

              def a_pre(sl):
                  blk, j = sl // 8, sl % 8
                  xb = (sl // 2) % 2
                  if j % 2 == 0:
                      k.dma("sync", lambda e: e.dma_start(out=xin[xb][:], in_=xv[blk, :, j:j + 2, :]), writes=[bxin[xb]])
                  xt = xin[xb][:, j % 2, :]
                  sx = sl % 2
                  k.op("scalar", lambda e: e.activation(
                      out=xs[sx][:], in_=xt, func=AF.Square, accum_out=ss[:, sl:sl + 1]),
                      reads=[bxin[xb]], writes=[bxs[sx], bss[sl]])
                  k.op("gpsimd", lambda e: e.tensor_scalar(
                      out=rs[:, sl:sl + 1], in0=ss[:, sl:sl + 1], scalar1=1.0 / D, scalar2=EPS,
                      op0=ALU.mult, op1=ALU.add), reads=[bss[sl]], writes=[brs[sl]])
                  k.op("gpsimd", lambda e: e.tensor_tensor(
                      out=rs[:, sl:sl + 1], in0=rs[:, sl:sl + 1], in1=nhalf[:], op=ALU.pow),
                      reads=[brs[sl], bnhalf], writes=[brs[sl]])
                  k.op("scalar", lambda e: e.activation(
                      out=xs[sx][:], in_=xt, func=AF.Copy, scale=rs[:, sl:sl + 1]),
                      reads=[bxin[xb], brs[sl]], writes=[bxs[sx]])

              def a_trn(sl):
                  blk, j = sl // 8, sl % 8
                  hb = blk % 2
                  sx = sl % 2
                  pb = nps()
                  pT = PS[pb][:].bitcast(BF16).rearrange("p (a b) -> p a b", a=8)
                  for kt in range(8):
                      k.op("tensor", lambda e, kt=kt: e.transpose(
                          out=pT[:, kt, :], in_=xs[sx][:, kt * 128:(kt + 1) * 128], identity=ident[:]),
                          reads=[bxs[sx], bident], writes=[bPS[pb]], sig=(kt == 7))
                  k.op("vector", lambda e: e.tensor_tensor(
                      out=hnT[hb][:, :, j, :], in0=pT, in1=npre[:].unsqueeze(2).broadcast_to([128, 8, 128]),
                      op=ALU.mult), reads=[bPS[pb], bnpre], writes=[bhn[hb][j]])

              def a_mm(sl):
                  blk, j = sl // 8, sl % 8
                  hb = blk % 2
                  own = blk >= 4
                  pb = nps()
                  for kt in range(8):
                      k.op("tensor", lambda e, kt=kt: e.matmul(
                          PS[pb][:], lhsT=hnT[hb][:, kt, j, :], rhs=wu[:, kt, 0:512],
                          start=(kt == 0), stop=(kt == 7)),
                          reads=[bhn[hb][j]] + bwul, writes=[bPS[pb]], sig=(kt == 7))
                  k.op("vector", lambda e: e.tensor_copy(
                      out=Tt[hb][:, :, j, :], in_=PS[pb][:].rearrange("p (g h) -> p g h", h=16)),
                      reads=[bPS[pb]], writes=[bTt[hb][j]])
                  if own:
                      pb2 = nps()
                      for kt in range(8):
                          k.op("tensor", lambda e, kt=kt: e.matmul(
                              PS[pb2][:], lhsT=hnT[hb][:, kt, j, :], rhs=wu[:, kt, 512:1024],
                              start=(kt == 0), stop=(kt == 7)),
                              reads=[bhn[hb][j]] + bwzl, writes=[bPS[pb2]], sig=(kt == 7))
                      k.op("scalar", lambda e: e.activation(
                          out=tza[j % 2][:], in_=PS[pb2][:], func=AF.Tanh, scale=0.5),
                          reads=[bPS[pb2]], writes=[btza[j % 2]])
                      k.op("vector", lambda e: e.scalar_tensor_tensor(
                          out=zst[hb][:, j, :], in0=tza[j % 2][:], scalar=1.0, in1=PS[pb2][:],
                          op0=ALU.add, op1=ALU.mult), reads=[bPS[pb2], btza[j % 2]], writes=[bzs[hb]])

              def a_fin(blk):
                  hb = blk % 2
                  if blk >= 4:
                      k.dma("sync", lambda e: e.dma_start(
                          out=dr["zs_d"][blk - 4], in_=zst[hb][:].rearrange("p a b -> p (a b)")), reads=[bzs[hb]])
                  for g4 in range(4):
                      pb = nps()
                      pT = PS[pb][:].bitcast(BF16).rearrange("p (a b) -> p a b", a=8)
                      for gg in range(8):
                          g = g4 * 8 + gg
                          k.op("tensor", lambda e, gg=gg, g=g, pT=pT: e.transpose(
                              out=pT[:, gg, :], in_=Tt[hb][:, g, :, :].rearrange("p a b -> p (a b)"), identity=ident[:]),
                              reads=bTt[hb] + [bident], writes=[bPS[pb]], sig=(gg == 7))
                      hf, b4 = blk // 4, blk % 4
                      uo = U[:, g4 * 8:(g4 + 1) * 8, hf * 512:(hf + 1) * 512].rearrange(
                          "p g (j m) -> p g j m", j=8)[:, :, :, b4 * 16:(b4 + 1) * 16]
                      k.op("vector", lambda e, uo=uo, pT=pT: e.tensor_copy(
                          out=uo, in_=pT.rearrange("p g (m j) -> p g j m", j=8)),
                          reads=[bPS[pb]], writes=[bUblk[blk]])

              a_pre(0)
              for s_ in range(72):
                  if s_ + 1 < 64:
                      a_pre(s_ + 1)
                  if s_ < 64:
                      a_trn(s_)
                  if s_ >= 8:
                      a_mm(s_ - 8)
                      if (s_ - 8) % 8 == 7:
                          a_fin((s_ - 8) // 8)
              if stage == 1:
                  t1 = k.dma("gpsimd", lambda e: e.dma_start(
                      out=dr["dbg_u"], in_=U[:].rearrange("p a b -> p (a b)")), reads=bUblk)
                  t2 = k.dma("gpsimd", lambda e: e.dma_start(
                      out=dr["dbg_zs"], in_=zst[1][:].rearrange("p a b -> p (a b)")), reads=[bzs[1]])
                  k.wait_tok("gpsimd", t1)
                  k.wait_tok("gpsimd", t2)
              k.emit()
        if stage >= 2 and stage != 5:
            ssm_phase(nc, k, st0, dr, PS, bPS, ident, bident, U, bUblk, stage)
        if stage >= 3 and stage != 5:
            pass_S(nc, k, dr, PS, bPS, ident, bident, U, bUblk, stage)
        stU.close()
        if stage >= 4:
            stW = ExitStack()
            W = prefetch_N(nc, k, dr, stW)
            pass_N(nc, k, dr, PS, bPS, ident, bident, stage, W)
            stW.close()
    return nc


def core_inputs(inp, b, s):
    x = inp["x"][b]
    if s == 1:
        x_all = x
    else:
        x_all = x[::-1]
    d = {}
    d["x_all"] = np.ascontiguousarray(x_all, dtype=np.float32)
    d["w_in"] = np.ascontiguousarray(inp["w_in"][0], dtype=np.float32)
    d["npre"] = np.ascontiguousarray(inp["norm_pre"][0].reshape(8, 128).T, dtype=np.float32)
    d["ident"] = np.eye(128, dtype=np.float32)
    par = np.zeros((2, 128, 2144), np.float32)
    for dd in range(2):
        sd = dd if s == 1 else 1 - dd
        a_re = inp["ssm_a_re"][0, sd]; a_im = inp["ssm_a_im"][0, sd]
        ldt = inp["ssm_log_dt"][0, sd]
        b_re = inp["ssm_b_re"][0, sd]; b_im = inp["ssm_b_im"][0, sd]
        c_re = inp["ssm_c_re"][0, sd]; c_im = inp["ssm_c_im"][0, sd]
        blk = np.concatenate([
            a_re.T, a_im.T, np.broadcast_to(ldt[None, :], (64, 32)),
            b_re.transpose(1, 0, 2).reshape(64, 512), b_im.transpose(1, 0, 2).reshape(64, 512),
            c_re.transpose(2, 0, 1).reshape(64, 512), c_im.transpose(2, 0, 1).reshape(64, 512)], axis=1)
        par[dd, 0:64] = blk
        par[dd, 64:128] = blk
    d["ssm_par"] = par
    cst = np.zeros((128, NK + 33 + 512), np.float32)
    cst[:, 0:NK] = np.asarray(KVALS, np.float32)[None, :]
    cst[:, NK:NK + 32] = np.tile(inp["ssm_d"][0].T, (8, 1))
    cst[0:64, NK + 32] = 1.0
    cst[64:128, NK + 32] = -1.0
    c0 = NK + 33
    r = np.arange(128)
    cst[:, c0:c0 + 128] = (r[:, None] % 64 == r[None, :] % 64)
    cst[:, c0 + 128:c0 + 256] = (r[None, :] // 16 >= r[:, None] // 16)
    cst[:, c0 + 256:c0 + 384] = (r[:, None] // 16 >= r[None, :] // 16)
    cst[:, c0 + 384:c0 + 512] = np.eye(128)
    d["ssm_c"] = cst
    d["w_glu"] = np.ascontiguousarray(inp["w_glu"][0], dtype=np.float32)
    d["b_glu"] = np.ascontiguousarray(inp["b_glu"][0][None, :], dtype=np.float32)
    d["w_out"] = np.ascontiguousarray(inp["w_out"][0], dtype=np.float32)
    d["w_gate"] = np.ascontiguousarray(inp["w_ple_gate"][0], dtype=np.float32)
    d["w_ple"] = np.ascontiguousarray(inp["w_ple"][0], dtype=np.float32)
    d["npost_b"] = np.ascontiguousarray(np.broadcast_to(inp["norm_post"][0][None, :], (128, D)), dtype=np.float32)
    d["plen_b"] = np.ascontiguousarray(np.broadcast_to(inp["ple_norm"][0][None, :], (128, D)), dtype=np.float32)
    p = inp["p"][0, b]
    p_own = p[HALF:] if s == 1 else p[:HALF][::-1]
    d["p_own"] = np.ascontiguousarray(p_own, dtype=np.float32)
    d["na_tab"] = bias_tables(inp["na_rpb"][0], s)
    return d


def bias_tables(rpb, s):
    NEG = np.float32(-30000.0)
    tab = np.full((3, 8, 5, 128, 128), NEG, np.float32)
    kb = (np.arange(128) // 64)[:, None]
    kc = (np.arange(128) % 64)[:, None]
    qa = (np.arange(128) // 64)[None, :]
    qc = (np.arange(128) % 64)[None, :]
    for kind, R in enumerate([10, 30, 31]):
        tiles = [R - 2 + i for i in range(5)] if R <= 29 else [28, 29, 30, 31]
        for idx, tl in enumerate(tiles):
            kl = 2 * tl + kb
            ql = 2 * R + qa
            if s == 1:
                rk, rq, ck, cq = 64 + kl, 64 + ql, kc, qc
            else:
                rk, rq, ck, cq = 63 - kl, 63 - ql, 63 - kc, 63 - qc
            rs = np.clip(rq - 4, 0, 120)
            cs = np.clip(cq - 8, 0, 48)
            valid = (rk >= rs) & (rk < rs + 8) & (ck >= cs) & (ck < cs + 16)
            dr_ = np.clip(rk - rq + 7, 0, 14)
            dc_ = np.clip(ck - cq + 15, 0, 30)
            vals = rpb[:, dr_, dc_]
            tab[kind, :, idx] = np.where(valid[None], vals, NEG)
    return tab


_NC_CACHE = {}


def kernel(**inputs):
    inp = {k_: np.asarray(v) for k_, v in inputs.items()}
    if "nc" not in _NC_CACHE:
        _NC_CACHE["nc"] = build(stage=4)
    nc = _NC_CACHE["nc"]
    in_maps = []
    for c in range(8):
        in_maps.append(core_inputs(inp, c // 2, c % 2))
    res = run_bass_kernel_spmd(nc, in_maps, core_ids=list(range(8)))
    out = np.zeros((4, NTOK, D), np.float32)
    for c in range(8):
        b, s = c // 2, c % 2
        o = res.results[c]["out"]
        if s == 1:
            out[b, HALF:] = o
        else:
            out[b, :HALF] = o[::-1]
    return out
```
